# Optimizing a Trainium2 kernel written in Bass

```python
import math
import jax, jax.numpy as jnp
from jax import lax
import numpy as np

D_MODEL = 1024
BATCH = 4
SEQ = 4096
DEPTH = 2
DEC_BATCH = 16
DEC_SEQ = 16
PAST_LEN = 4096

CHUNK = 64
Q_BLOCK = 128
D_MIX = D_MODEL
HEAD_DIM = 64
SB_HEADS = D_MIX // (4 * HEAD_DIM)
SB_WIDTH = SB_HEADS * HEAD_DIM
FOX_HEADS = D_MIX // (4 * HEAD_DIM)
FOX_WIDTH = FOX_HEADS * HEAD_DIM
SSD_INNER = D_MIX - SB_WIDTH - FOX_WIDTH
SSD_HEAD_DIM = 64
SSD_HEADS = SSD_INNER // SSD_HEAD_DIM
SSD_GROUPS = 2
SSD_HEADS_PER_GROUP = SSD_HEADS // SSD_GROUPS
SSD_STATE = 128
SSD_CONV = 4
SSD_CHUNK = CHUNK
CONV_DIM = SSD_INNER + 2 * SSD_GROUPS * SSD_STATE
IN_SIZES = (SB_WIDTH, SB_WIDTH, SB_WIDTH, FOX_WIDTH, FOX_WIDTH, FOX_WIDTH, FOX_HEADS, SSD_INNER, CONV_DIM, SSD_HEADS)
N_IN = 3 * SB_WIDTH + 3 * FOX_WIDTH + FOX_HEADS + SSD_INNER + CONV_DIM + SSD_HEADS
MEM_LEN = 256
X_HEADS = 4
X_HEAD_DIM = D_MODEL // X_HEADS
D_FF = ((8 * D_MODEL + 3 * 256 - 1) // (3 * 256)) * 256
EPS = 1e-6

kernel_name = 'hybrid_sb_ssd_fox_stream_step'


def rmsnorm(x, g):
    xf = x.astype(jnp.float32)
    y = xf * lax.rsqrt(jnp.mean(xf * xf, axis=-1, keepdims=True) + EPS)
    return (y * g.astype(jnp.float32)).astype(x.dtype)


def split_last(a, sizes):
    idx = np.cumsum(sizes)[:-1].tolist()
    return jnp.split(a, idx, axis=-1)


def query_sweep(fn, q_args, n):
    nb = n // Q_BLOCK
    blocks = tuple(jnp.moveaxis(a.reshape(a.shape[0], nb, Q_BLOCK, *a.shape[2:]), 1, 0) for a in q_args)
    pos = jnp.arange(n, dtype=jnp.int32).reshape(nb, Q_BLOCK)
    out = lax.map(lambda xs: fn(*xs), blocks + (pos,))
    out = jnp.moveaxis(out, 0, 1)
    return out.reshape(out.shape[0], n, *out.shape[3:])


def sb_attend(q, k, v, q_pos, k_pos):
    z = jnp.einsum('bqhd,bkhd->bhqk', q, k).astype(jnp.float32) * (HEAD_DIM ** -0.5)
    valid = k_pos[None, :] < q_pos[:, None]
    log_keep = jnp.where(valid, jax.nn.log_sigmoid(-z), 0.0)
    after = lax.cumsum(log_keep, axis=3, reverse=True) - log_keep
    w = jnp.where(valid, jnp.exp(jax.nn.log_sigmoid(z) + after), 0.0)
    return jnp.einsum('bhqk,bkhd->bqhd', w, v.astype(jnp.float32)).astype(q.dtype)


def fox_attend(q, k, v, fq, fk, q_pos, k_pos):
    s = jnp.einsum('bqhd,bkhd->bhqk', q, k).astype(jnp.float32) * (HEAD_DIM ** -0.5)
    s = s + (jnp.transpose(fq, (0, 2, 1))[..., :, None] - jnp.transpose(fk, (0, 2, 1))[..., None, :])
    valid = k_pos[None, :] <= q_pos[:, None]
    p = jax.nn.softmax(jnp.where(valid, s, -jnp.inf), axis=-1)
    return jnp.einsum('bhqk,bkhd->bqhd', p, v.astype(jnp.float32)).astype(q.dtype)


def causal_dwconv(xpad, w, b):
    y = lax.conv_general_dilated(xpad, w[:, None, :].astype(xpad.dtype), (1,), 'VALID',
                                 dimension_numbers=('NWC', 'WIO', 'NWC'),
                                 feature_group_count=xpad.shape[-1])
    return y + b.astype(xpad.dtype)


def ssd_scan(x, a, B, C, h0, chunk):
    b, n, nh, hp = x.shape
    ns = B.shape[-1]
    nc = n // chunk
    x = x.reshape(b, nc, chunk, nh, hp)
    a = a.reshape(b, nc, chunk, nh)
    B = B.reshape(b, nc, chunk, nh, ns)
    C = C.reshape(b, nc, chunk, nh, ns)
    a_cum = jnp.cumsum(a, axis=2)
    causal = jnp.tril(jnp.ones((chunk, chunk), dtype=bool))
    seg = a_cum[:, :, :, None, :] - a_cum[:, :, None, :, :]
    decay = jnp.exp(jnp.where(causal[None, None, :, :, None], seg, -jnp.inf))
    scores = jnp.einsum('bcthn,bcshn->bctsh', C, B) * decay
    y_diag = jnp.einsum('bctsh,bcshp->bcthp', scores, x)
    to_end = jnp.exp(a_cum[:, :, -1:, :] - a_cum)
    chunk_states = jnp.einsum('bcshn,bcsh,bcshp->bchpn', B, to_end, x)
    chunk_decay = jnp.exp(a_cum[:, :, -1, :])

    def step(hc, inp):
        dec, st = inp
        return dec[:, :, None, None] * hc + st, hc

    h_final, h_in = lax.scan(step, h0, (jnp.moveaxis(chunk_decay, 1, 0), jnp.moveaxis(chunk_states, 1, 0)))
    h_in = jnp.moveaxis(h_in, 0, 1)
    y_off = jnp.einsum('bcthn,bchpn->bcthp', C, h_in) * jnp.exp(a_cum)[..., None]
    return (y_diag + y_off).reshape(b, n, nh, hp), h_final


def mixers(u, p, past):
    b, n, _ = u.shape
    proj = jnp.einsum('bnd,de->bne', u, p['w_in'])
    sq, sk, sv, fq, fk, fv, fg, z, xbc, dt = split_last(proj, IN_SIZES)
    heads = lambda t: t.reshape(b, n, -1, HEAD_DIM)
    sq, sk, sv, fq, fk, fv = map(heads, (sq, sk, sv, fq, fk, fv))
    logf = jax.nn.log_sigmoid((fg + p['fox_b_f']).astype(jnp.float32))
    if past is None:
        kpos = jnp.arange(n, dtype=jnp.int32)
        sb_o = query_sweep(lambda qb, qp: sb_attend(qb, sk, sv, qp, kpos), (sq,), n)
        fcum = jnp.cumsum(logf, axis=1)
        fox_o = query_sweep(lambda qb, fb, qp: fox_attend(qb, fk, fv, fb, fcum, qp, kpos), (fq, fcum), n)
        xbc_pad = jnp.concatenate([jnp.zeros((b, SSD_CONV - 1, CONV_DIM), xbc.dtype), xbc], axis=1)
        h0 = jnp.zeros((b, SSD_HEADS, SSD_HEAD_DIM, SSD_STATE), jnp.float32)
        ssd_chunk = SSD_CHUNK
    else:
        past_len = past['sb_k'].shape[1]
        kpos = jnp.arange(past_len + n, dtype=jnp.int32)
        qpos = past_len + jnp.arange(n, dtype=jnp.int32)
        sb_k = jnp.concatenate([past['sb_k'].astype(sk.dtype), sk], axis=1)
        sb_v = jnp.concatenate([past['sb_v'].astype(sv.dtype), sv], axis=1)
        sb_o = sb_attend(sq, sb_k, sb_v, qpos, kpos)
        fcum = jnp.cumsum(jnp.concatenate([past['fox_logf'].astype(jnp.float32), logf], axis=1), axis=1)
        fox_k = jnp.concatenate([past['fox_k'].astype(fk.dtype), fk], axis=1)
        fox_v = jnp.concatenate([past['fox_v'].astype(fv.dtype), fv], axis=1)
        fox_o = fox_attend(fq, fox_k, fox_v, fcum[:, past_len:], fcum, qpos, kpos)
        xbc_pad = jnp.concatenate([past['conv'].astype(xbc.dtype), xbc], axis=1)
        h0 = past['ssm'].astype(jnp.float32)
        ssd_chunk = n
    conv_state = xbc_pad[:, -(SSD_CONV - 1):]
    xbc_c = jax.nn.silu(causal_dwconv(xbc_pad, p['conv_w'], p['conv_b']))
    xs, Bm, Cm = split_last(xbc_c, (SSD_INNER, SSD_GROUPS * SSD_STATE, SSD_GROUPS * SSD_STATE))
    xs = xs.reshape(b, n, SSD_HEADS, SSD_HEAD_DIM).astype(jnp.float32)
    Bm = jnp.repeat(Bm.reshape(b, n, SSD_GROUPS, SSD_STATE), SSD_HEADS_PER_GROUP, axis=2).astype(jnp.float32)
    Cm = jnp.repeat(Cm.reshape(b, n, SSD_GROUPS, SSD_STATE), SSD_HEADS_PER_GROUP, axis=2).astype(jnp.float32)
    dt = jax.nn.softplus((dt + p['dt_bias']).astype(jnp.float32))
    A = -jnp.exp(p['a_log'].astype(jnp.float32))
    y, h_final = ssd_scan(xs * dt[..., None], dt * A, Bm, Cm, h0, ssd_chunk)
    y = (y + p['d_skip'].astype(jnp.float32)[:, None] * xs).reshape(b, n, SSD_INNER).astype(u.dtype)
    ssd_o = rmsnorm(y * jax.nn.silu(z), p['ssd_norm_g'])
    mix = jnp.concatenate([sb_o.reshape(b, n, SB_WIDTH), fox_o.reshape(b, n, FOX_WIDTH), ssd_o], axis=-1)
    out = jnp.einsum('bne,ed->bnd', mix, p['w_out'])
    return out, (sk, sv, fk, fv, logf, h_final, conv_state)


def mem_kv(mem, g, wk, wv):
    b, m, _ = mem.shape
    mn = rmsnorm(mem, g)
    k = jnp.einsum('bmd,de->bme', mn, wk).reshape(b, m, X_HEADS, X_HEAD_DIM)
    v = jnp.einsum('bmd,de->bme', mn, wv).reshape(b, m, X_HEADS, X_HEAD_DIM)
    return k, v


def cross_attend(u, mk, mv, wq, wo):
    b, n, _ = u.shape
    q = jnp.einsum('bnd,de->bne', u, wq).reshape(b, n, X_HEADS, X_HEAD_DIM)
    s = jnp.einsum('bnhd,bmhd->bhnm', q, mk.astype(q.dtype)).astype(jnp.float32) * (X_HEAD_DIM ** -0.5)
    pr = jax.nn.softmax(s, axis=-1)
    o = jnp.einsum('bhnm,bmhd->bnhd', pr, mv.astype(jnp.float32)).astype(u.dtype)
    return jnp.einsum('bne,ed->bnd', o.reshape(b, n, D_MODEL), wo)


def swiglu(u, wg, wu, wd):
    hdn = jax.nn.silu(jnp.einsum('bnd,df->bnf', u, wg)) * jnp.einsum('bnd,df->bnf', u, wu)
    return jnp.einsum('bnf,fd->bnd', hdn, wd)


def layer(h, p, past, mk, mv):
    mix, st = mixers(rmsnorm(h, p['norm_mix_g']), p, past)
    h = h + mix.astype(h.dtype)
    h = h + cross_attend(rmsnorm(h, p['norm_x_g']), mk, mv, p['wq_x'], p['wo_x']).astype(h.dtype)
    h = h + swiglu(rmsnorm(h, p['norm_ffn_g']), p['w_gate'], p['w_up'], p['w_down']).astype(h.dtype)
    return h, st


def setup_inputs(seed: int = 0) -> dict:
    key = jax.random.key(seed)
    counter = [0]

    def nk():
        counter[0] += 1
        return jax.random.fold_in(key, counter[0])

    def nrm(shape, scale):
        return jax.random.normal(nk(), shape, jnp.float32) * scale

    def gain(shape):
        return 1.0 + nrm(shape, 0.02)

    x_prompt = nrm((BATCH, SEQ, D_MODEL), 1.0)
    x_sample = nrm((DEC_BATCH, DEC_SEQ, D_MODEL), 1.0)
    cache_sb_k = nrm((DEPTH, DEC_BATCH, PAST_LEN, SB_HEADS, HEAD_DIM), 1.0)
    cache_sb_v = nrm((DEPTH, DEC_BATCH, PAST_LEN, SB_HEADS, HEAD_DIM), 1.0)
    cache_fox_k = nrm((DEPTH, DEC_BATCH, PAST_LEN, FOX_HEADS, HEAD_DIM), 1.0)
    cache_fox_v = nrm((DEPTH, DEC_BATCH, PAST_LEN, FOX_HEADS, HEAD_DIM), 1.0)
    cache_fox_logf = jax.nn.log_sigmoid(3.0 + nrm((DEPTH, DEC_BATCH, PAST_LEN, FOX_HEADS), 1.0))
    state_ssm = nrm((DEPTH, DEC_BATCH, SSD_HEADS, SSD_HEAD_DIM, SSD_STATE), 0.1)
    state_conv = nrm((DEPTH, DEC_BATCH, SSD_CONV - 1, CONV_DIM), 1.0)
    cache_mem_k = nrm((DEPTH, DEC_BATCH, MEM_LEN, X_HEADS, X_HEAD_DIM), 1.0)
    cache_mem_v = nrm((DEPTH, DEC_BATCH, MEM_LEN, X_HEADS, X_HEAD_DIM), 1.0)
    mem_prompt = nrm((BATCH, MEM_LEN, D_MODEL), 1.0)
    norm_mix_g = gain((DEPTH, D_MODEL))
    w_in = nrm((DEPTH, D_MODEL, N_IN), D_MODEL ** -0.5)
    fox_b_f = jax.random.uniform(nk(), (DEPTH, FOX_HEADS), jnp.float32, 1.0, 6.0)
    conv_w = nrm((DEPTH, SSD_CONV, CONV_DIM), SSD_CONV ** -0.5)
    conv_b = nrm((DEPTH, CONV_DIM), 0.02)
    dt0 = jnp.exp(jax.random.uniform(nk(), (DEPTH, SSD_HEADS), jnp.float32, math.log(1e-3), math.log(1e-1)))
    dt_bias = dt0 + jnp.log(-jnp.expm1(-dt0))
    a_log = jnp.log(jax.random.uniform(nk(), (DEPTH, SSD_HEADS), jnp.float32, 1.0, 16.0))
    d_skip = gain((DEPTH, SSD_HEADS))
    ssd_norm_g = gain((DEPTH, SSD_INNER))
    w_out = nrm((DEPTH, D_MIX, D_MODEL), D_MIX ** -0.5)
    norm_x_g = gain((DEPTH, D_MODEL))
    mem_norm_g = gain((DEPTH, D_MODEL))
    wq_x = nrm((DEPTH, D_MODEL, D_MODEL), D_MODEL ** -0.5)
    wk_x = nrm((DEPTH, D_MODEL, D_MODEL), D_MODEL ** -0.5)
    wv_x = nrm((DEPTH, D_MODEL, D_MODEL), D_MODEL ** -0.5)
    wo_x = nrm((DEPTH, D_MODEL, D_MODEL), D_MODEL ** -0.5)
    norm_ffn_g = gain((DEPTH, D_MODEL))
    w_gate = nrm((DEPTH, D_MODEL, D_FF), D_MODEL ** -0.5)
    w_up = nrm((DEPTH, D_MODEL, D_FF), D_MODEL ** -0.5)
    w_down = nrm((DEPTH, D_FF, D_MODEL), D_FF ** -0.5)
    final_norm_g = gain((D_MODEL,))
    return {'x_prompt': x_prompt, 'x_sample': x_sample,
            'cache_sb_k': cache_sb_k, 'cache_sb_v': cache_sb_v,
            'cache_fox_k': cache_fox_k, 'cache_fox_v': cache_fox_v, 'cache_fox_logf': cache_fox_logf,
            'state_ssm': state_ssm, 'state_conv': state_conv,
            'cache_mem_k': cache_mem_k, 'cache_mem_v': cache_mem_v,
            'mem_prompt': mem_prompt,
            'norm_mix_g': norm_mix_g, 'w_in': w_in, 'fox_b_f': fox_b_f, 'conv_w': conv_w, 'conv_b': conv_b,
            'dt_bias': dt_bias, 'a_log': a_log, 'd_skip': d_skip, 'ssd_norm_g': ssd_norm_g, 'w_out': w_out,
            'norm_x_g': norm_x_g, 'mem_norm_g': mem_norm_g, 'wq_x': wq_x, 'wk_x': wk_x, 'wv_x': wv_x, 'wo_x': wo_x,
            'norm_ffn_g': norm_ffn_g, 'w_gate': w_gate, 'w_up': w_up, 'w_down': w_down,
            'final_norm_g': final_norm_g}


def reference(x_prompt, x_sample, cache_sb_k, cache_sb_v, cache_fox_k, cache_fox_v, cache_fox_logf,
              state_ssm, state_conv, cache_mem_k, cache_mem_v, mem_prompt,
              norm_mix_g, w_in, fox_b_f, conv_w, conv_b, dt_bias, a_log, d_skip, ssd_norm_g, w_out,
              norm_x_g, mem_norm_g, wq_x, wk_x, wv_x, wo_x, norm_ffn_g, w_gate, w_up, w_down, final_norm_g):
    hp, hs = x_prompt, x_sample
    p_st, s_st, p_mk, p_mv = [], [], [], []
    for l in range(DEPTH):
        p = dict(norm_mix_g=norm_mix_g[l], w_in=w_in[l], fox_b_f=fox_b_f[l], conv_w=conv_w[l], conv_b=conv_b[l],
                 dt_bias=dt_bias[l], a_log=a_log[l], d_skip=d_skip[l], ssd_norm_g=ssd_norm_g[l], w_out=w_out[l],
                 norm_x_g=norm_x_g[l], wq_x=wq_x[l], wo_x=wo_x[l], norm_ffn_g=norm_ffn_g[l],
                 w_gate=w_gate[l], w_up=w_up[l], w_down=w_down[l])
        mk, mv = mem_kv(mem_prompt, mem_norm_g[l], wk_x[l], wv_x[l])
        hp, st = layer(hp, p, None, mk, mv)
        p_st.append(st)
        p_mk.append(mk)
        p_mv.append(mv)
        past = dict(sb_k=cache_sb_k[l], sb_v=cache_sb_v[l], fox_k=cache_fox_k[l], fox_v=cache_fox_v[l],
                    fox_logf=cache_fox_logf[l], ssm=state_ssm[l], conv=state_conv[l])
        hs, st = layer(hs, p, past, cache_mem_k[l], cache_mem_v[l])
        s_st.append(st)
    y_prompt = rmsnorm(hp, final_norm_g)
    y_sample = rmsnorm(hs, final_norm_g)
    stk = lambda sts, i: jnp.stack([s[i] for s in sts], axis=0)
    p_sb_k, p_sb_v, p_fox_k, p_fox_v = stk(p_st, 0), stk(p_st, 1), stk(p_st, 2), stk(p_st, 3)
    p_fox_logf, p_ssm, p_conv = stk(p_st, 4), stk(p_st, 5), stk(p_st, 6)
    p_mem_k, p_mem_v = jnp.stack(p_mk, axis=0), jnp.stack(p_mv, axis=0)
    s_sb_k, s_sb_v, s_fox_k, s_fox_v = stk(s_st, 0), stk(s_st, 1), stk(s_st, 2), stk(s_st, 3)
    s_fox_logf, s_ssm, s_conv = stk(s_st, 4), stk(s_st, 5), stk(s_st, 6)
    return (y_prompt, y_sample,
            p_sb_k, p_sb_v, p_fox_k, p_fox_v, p_fox_logf, p_ssm, p_conv, p_mem_k, p_mem_v,
            s_sb_k, s_sb_v, s_fox_k, s_fox_v, s_fox_logf, s_ssm, s_conv)
```

```python
import numpy as np
from contextlib import ExitStack
import concourse.bass as bass
import concourse.mybir as mybir
from concourse.bass_utils import run_bass_kernel_spmd

F32 = mybir.dt.float32
BF16 = mybir.dt.bfloat16
AF = mybir.ActivationFunctionType
ALU = mybir.AluOpType

NCORES = 8
DM = 1024
KC = 8
TP = 2048
TS = 32
T = TP + TS
NIN = 3084
DFF = 2816
PAST = 4096
MEM = 256
EPS = 1e-6
NEG = -30000.0
SEM_CH = 16000
DMA_R = 8


class Buf:
    __slots__ = ("name", "w", "r", "excl")

    def __init__(self, name="", excl=False):
        self.name = name
        self.w = None
        self.r = []
        self.excl = excl


class Op:
    __slots__ = ("eng", "fn", "waits", "inc", "pos", "dma_n", "q")

    def __init__(self, eng, fn):
        self.eng = eng
        self.fn = fn
        self.waits = []
        self.inc = False
        self.pos = -1
        self.dma_n = -1


ENGS = ("pe", "act", "dve", "pool", "sp")
DMAQ = ("sp", "pool", "act", "cc")
QINC = {"sp": 16, "pool": 16, "act": 16, "cc": 1}


class Prog:
    def __init__(self, nc):
        self.nc = nc
        self.ops = {e: [] for e in ENGS}
        self.seen = {e: {} for e in ENGS}
        self.dmas = {q: [] for q in DMAQ}
        self.pending = {e: [] for e in ENGS}

    def _need(self, op, tgt, raw):
        if tgt is None:
            return
        if tgt[0] == "c":
            _, te, pos = tgt
            if te == op.eng and te == "pe":
                return
            key = ("c", te)
            if self.seen[op.eng].get(key, -1) >= pos:
                return
            self.seen[op.eng][key] = pos
            self.ops[te][pos].inc = True
            op.waits.append(tgt)
        else:
            _, q, n = tgt
            key = ("d", q, n % DMA_R)
            if self.seen[op.eng].get(key, -1) >= n:
                return
            self.seen[op.eng][key] = n
            op.waits.append(tgt)

    def add(self, eng, fn, reads=(), writes=(), dma=False, q=None):
        op = Op(eng, fn)
        op.pos = len(self.ops[eng])
        for t in self.pending[eng]:
            self._need(op, t, True)
        self.pending[eng] = []
        if dma:
            q = q or eng
            op.q = q
            n = len(self.dmas[q])
            op.dma_n = n
            if n >= DMA_R:
                self._need(op, ("d", q, n - DMA_R), True)
            me = ("d", q, n)
            self.dmas[q].append(op)
        else:
            me = ("c", eng, op.pos)
        for b in reads:
            self._need(op, b.w, True)
            if b.excl:
                for t in b.r:
                    self._need(op, t, False)
        for b in writes:
            self._need(op, b.w, False)
            for t in b.r:
                self._need(op, t, False)
        for b in reads:
            b.r.append(me)
        for b in writes:
            b.w = me
            b.r = []
        self.ops[eng].append(op)
        return op

    def barrier(self):
        tg = []
        for e in ENGS:
            for op in reversed(self.ops[e]):
                if op.dma_n < 0:
                    tg.append(("c", e, op.pos))
                    break
        for q in DMAQ:
            n = len(self.dmas[q])
            for k in range(max(0, n - DMA_R), n):
                tg.append(("d", q, k))
        for e in ENGS:
            self.pending[e] = self.pending[e] + tg

    def finish(self):
        self.barrier()
        for e in ENGS:
            self.add(e, lambda eng: eng.nop())

    def emit(self):
        nc = self.nc
        with ExitStack() as st:
            csem = {}
            cidx = {}
            for e in ENGS:
                k = 0
                for op in self.ops[e]:
                    if op.inc:
                        cidx[(e, op.pos)] = k
                        k += 1
                nch = (k + SEM_CH - 1) // SEM_CH
                csem[e] = [st.enter_context(nc.semaphore(f"c_{e}_{i}")) for i in range(max(1, nch))]
            dsem = {q: [st.enter_context(nc.semaphore(f"d_{q}_{i}")) for i in range(DMA_R)] for q in DMAQ}
            block = st.enter_context(nc.Block())

            def replay(e, eng):
                for op in self.ops[e]:
                    for t in op.waits:
                        if t[0] == "c":
                            k = cidx[(t[1], t[2])]
                            eng.wait_ge(csem[t[1]][k // SEM_CH], k % SEM_CH + 1)
                        else:
                            _, q, n = t
                            eng.wait_ge(dsem[q][n % DMA_R], QINC[q] * (n // DMA_R + 1))
                    ins = op.fn(eng)
                    if op.dma_n >= 0:
                        ins.then_inc(dsem[op.q][op.dma_n % DMA_R], QINC[op.q])
                    elif op.inc:
                        k = cidx[(e, op.pos)]
                        ins.then_inc(csem[e][k // SEM_CH], 1)

            @block.tensor
            def _(eng):
                replay("pe", eng)

            @block.scalar
            def _(eng):
                replay("act", eng)

            @block.vector
            def _(eng):
                replay("dve", eng)

            @block.gpsimd
            def _(eng):
                replay("pool", eng)

            @block.sync
            def _(eng):
                replay("sp", eng)


def bcast(ap, pos, n):
    dims = [list(d) for d in ap.ap]
    dims.insert(pos, [0, n])
    return bass.AP(ap.tensor, ap.offset, dims)


class Arena:
    def __init__(self, nc, base=16512, limit=229376):
        self.nc = nc
        self.off = base
        self.limit = limit
        self.n = 0

    def alloc(self, shape, dtype, name="t"):
        esz = 4 if dtype == F32 else 2
        per = esz
        for s in shape[1:]:
            per *= s
        off = (self.off + 31) // 32 * 32
        assert off + per <= self.limit, f"SBUF overflow {name} {off}+{per}"
        self.n += 1
        h = self.nc.alloc_sbuf_tensor_at(f"{name}{self.n}", list(shape), dtype, offset=off)
        self.off = off + per
        return h.ap()

    def mark(self):
        return self.off

    def release(self, m):
        self.off = m


class K:
    def __init__(self, stage=99, ncores=NCORES):
        self.stage = stage
        self.ncores = ncores
        nc = bass.Bass("TRN2", target_bir_lowering=False)
        self.nc = nc
        self.P = Prog(nc)
        self.A = Arena(nc)
        self.ins = {}
        self.outs = {}
        self.build()

    def din(self, name, shape):
        t = self.nc.dram_tensor(name, list(shape), F32, kind="ExternalInput").ap()
        self.ins[name] = t
        return t

    def I(self, name):
        if name not in self.ins:
            self.din(name, self.in_shapes[name])
        return self.ins[name]

    def dout(self, name, shape):
        t = self.nc.dram_tensor(name, list(shape), F32, kind="ExternalOutput").ap()
        self.outs[name] = t
        return t

    def pe(self, fn, r=(), w=()):
        return self.P.add("pe", fn, r, w)

    def act(self, fn, r=(), w=()):
        return self.P.add("act", fn, r, w)

    def dve(self, fn, r=(), w=()):
        return self.P.add("dve", fn, r, w)

    def pool(self, fn, r=(), w=()):
        return self.P.add("pool", fn, r, w)

    def dma(self, q, out, in_, r=(), w=(), nc_ok=False):
        if nc_ok:
            def fn(e):
                with self.nc.allow_non_contiguous_dma(reason="small strided transfer"):
                    return e.dma_start(out=out, in_=in_)
        else:
            def fn(e):
                return e.dma_start(out=out, in_=in_)
        return self.P.add(q, fn, r, w, dma=True)

    def psum_bank(self):
        i = self.ps_i % 6
        self.ps_i += 1
        return self.ps[i], self.psb[i]

    def acc_bank(self):
        i = 6 + self.pa_i % 2
        self.pa_i += 1
        return self.ps[i], self.psb[i]

    def declare_io(self):
        self.in_shapes = {
            "xp": [TP, DM],
            "xs": [TS, DM],
            "c_sb_k": [2, 2, PAST, 256],
            "c_sb_v": [2, 2, PAST, 256],
            "c_fox_k": [2, 2, PAST, 256],
            "c_fox_v": [2, 2, PAST, 256],
            "c_fox_logf": [2, 2, PAST, 4],
            "st_ssm": [2, 2, 512, 128],
            "st_conv": [2, 2, 3, 1024],
            "c_mem_k": [2, 2, MEM, 1024],
            "c_mem_v": [2, 2, MEM, 1024],
            "memp": [MEM, DM],
            "flags": [128, 8],
            "norm_mix_g": [2, DM],
            "w_in": [2, DM, NIN],
            "fox_b_f": [2, 4],
            "conv_w": [2, 4, 1024],
            "conv_b": [2, 1024],
            "dt_bias": [2, 8],
            "a_log": [2, 8],
            "d_skip": [2, 8],
            "ssd_norm_g": [2, 512],
            "w_out": [2, DM, DM],
            "norm_x_g": [2, DM],
            "mem_norm_g": [2, DM],
            "wq_x": [2, DM, DM],
            "wk_x": [2, DM, DM],
            "wv_x": [2, DM, DM],
            "wo_x": [2, DM, DM],
            "norm_ffn_g": [2, DM],
            "w_gate": [2, DM, DFF],
            "w_up": [2, DM, DFF],
            "w_down": [2, DFF, DM],
            "final_norm_g": [1, DM],
        }
        o = self.dout
        self.yp = o("yp", [TP, DM])
        self.ys = o("ys", [TS, DM])
        self.o_kv = {}
        for nm in ("sb_k", "sb_v", "fox_k", "fox_v"):
            self.o_kv["p_" + nm] = o("p_" + nm, [2, TP, 256])
            self.o_kv["s_" + nm] = o("s_" + nm, [2, TS, 256])
        self.p_logf = o("p_fox_logf", [2, TP, 4])
        self.s_logf = o("s_fox_logf", [2, TS, 4])
        self.p_ssm = o("p_ssm", [2, 512, 128])
        self.s_ssm = o("s_ssm", [2, 2, 512, 128])
        self.p_conv = o("p_conv", [2, 3, 1024])
        self.s_conv = o("s_conv", [2, 2, 3, 1024])
        self.p_mem_k = o("p_mem_k", [2, MEM, 1024])
        self.p_mem_v = o("p_mem_v", [2, MEM, 1024])

    def consts(self):
        A = self.A
        nc = self.nc
        self.ones_bf = A.alloc([128, 128], BF16, "ones_bf")
        self.ones_f = A.alloc([128, 128], F32, "ones_f")
        self.ident_bf = A.alloc([128, 128], BF16, "ident_bf")
        self.ident_f = A.alloc([128, 128], F32, "ident_f")
        self.B_const = Buf("const")
        cb = self.B_const
        self.pool(lambda e: e.memset(self.ones_bf, 1.0), w=[cb])
        self.pool(lambda e: e.memset(self.ones_f, 1.0), w=[cb])
        self.pool(lambda e: e.affine_select(self.ident_f, self.ones_f, [[-1, 128]], ALU.is_equal, 0.0,
                                            base=0, channel_multiplier=1), r=[cb], w=[cb])
        self.pool(lambda e: e.affine_select(self.ident_bf, self.ones_bf, [[-1, 128]], ALU.is_equal, 0.0,
                                            base=0, channel_multiplier=1), r=[cb], w=[cb])
        self.U_f = A.alloc([128, 128], F32, "U_f")
        self.U_bf = A.alloc([128, 128], BF16, "U_bf")
        self.SL_f = A.alloc([128, 128], F32, "SL_f")
        self.pool(lambda e: e.affine_select(self.U_f, self.ones_f, [[1, 128]], ALU.is_ge, 0.0,
                                            base=0, channel_multiplier=-1), r=[cb], w=[cb])
        self.pool(lambda e: e.affine_select(self.U_bf, self.ones_bf, [[1, 128]], ALU.is_ge, 0.0,
                                            base=0, channel_multiplier=-1), r=[cb], w=[cb])
        self.pool(lambda e: e.affine_select(self.SL_f, self.ones_f, [[-1, 128]], ALU.is_gt, 0.0,
                                            base=0, channel_multiplier=1), r=[cb], w=[cb])
        self.negs_bf = A.alloc([128, 128], BF16, "negs")
        self.MSK_s = A.alloc([128, 128], BF16, "MSK_s")
        self.MSK_i = A.alloc([128, 128], BF16, "MSK_i")
        self.NLI_bf = A.alloc([128, 128], BF16, "NLI")
        self.mones_bf = A.alloc([128, 128], BF16, "mones")
        self.pool(lambda e: e.memset(self.negs_bf, NEG), w=[cb])
        self.pool(lambda e: e.memset(self.mones_bf, -1.0), w=[cb])
        self.pool(lambda e: e.affine_select(self.MSK_s, self.negs_bf, [[-1, 128]], ALU.is_ge, 0.0, base=0, channel_multiplier=1), r=[cb], w=[cb])
        self.pool(lambda e: e.affine_select(self.MSK_i, self.negs_bf, [[-1, 128]], ALU.is_gt, 0.0, base=0, channel_multiplier=1), r=[cb], w=[cb])
        self.pool(lambda e: e.affine_select(self.NLI_bf, self.mones_bf, [[-1, 128]], ALU.is_ge, 0.0, base=0, channel_multiplier=1), r=[cb], w=[cb])
        self.flags_t = A.alloc([128, 8], F32, "flags")
        self.dma("sp", self.flags_t, self.I("flags"), w=[cb])
        def colvec(src_row, n, name):
            t = A.alloc([128, n], F32, name)
            self.dma("sp", t, src_row.rearrange("(c p) -> p c", p=128), w=[cb], nc_ok=True)
            return t
        self.g_mix = [colvec(self.I("norm_mix_g")[l], KC, "g_mix") for l in range(2)]
        self.g_x = [colvec(self.I("norm_x_g")[l], KC, "g_x") for l in range(2)]
        self.g_ffn = [colvec(self.I("norm_ffn_g")[l], KC, "g_ffn") for l in range(2)]
        self.g_mem = [colvec(self.I("mem_norm_g")[l], KC, "g_mem") for l in range(2)]
        self.g_fin = colvec(self.I("final_norm_g")[0], KC, "g_fin")

    def load_transposed(self, src, ntok, dst, dst_buf, col0):
        nblk = (ntok + 127) // 128
        for tb in range(nblk):
            n = min(128, ntok - tb * 128)
            st, sb = self.stage_f[self.stg_i % 2], self.stage_fb[self.stg_i % 2]
            self.stg_i += 1
            self.dma("sp", st[:n, :], src[tb * 128: tb * 128 + n, :], w=[sb])
            for half in range(2):
                ps, pb = self.psum_bank()
                for j in range(4):
                    kc = half * 4 + j
                    self.pe(lambda e, ps=ps, j=j, st=st, kc=kc, n=n: e.transpose(
                        ps[:, j * 128: j * 128 + n], st[:n, kc * 128:(kc + 1) * 128], self.ident_f[:n, :n]),
                        r=[sb, self.B_const], w=[pb])
                c0 = col0 + tb * 128
                o = dst[:, half * 4: half * 4 + 4, c0: c0 + n]
                i = ps.rearrange("p (j t) -> p j t", j=4)[:, :, :n]
                if half == 0:
                    self.act(lambda e, o=o, i=i: e.copy(out=o, in_=i), r=[pb], w=[dst_buf])
                else:
                    self.dve(lambda e, o=o, i=i: e.tensor_copy(o, i), r=[pb], w=[dst_buf])

    def rmsnorm_fm(self, src, src_buf, dst, dst_buf, g, c0, n, dcol=None):
        d0 = c0 if dcol is None else dcol
        sq, sqb = self.sqt, self.sqt_b
        ps, pb = self.psum_bank()
        for kc in range(KC):
            s2, s2b = sq[kc % 2], sqb[kc % 2]
            self.act(lambda e, kc=kc, s2=s2: e.activation(out=s2[:, :n], in_=src[:, kc, c0:c0 + n], func=AF.Square),
                     r=[src_buf], w=[s2b])
            self.pe(lambda e, kc=kc, s2=s2: e.matmul(ps[:, :n], self.ones_bf, s2[:, :n], start=(kc == 0), stop=(kc == KC - 1)),
                    r=[s2b, self.B_const], w=[pb])
        rs, rsb = self.rstd, self.rstd_b
        self.act(lambda e: e.activation(out=rs[:, :n], in_=ps[:, :n], func=AF.Sqrt, bias=self.eps_t[:, 0:1], scale=1.0 / DM),
                 r=[pb, self.B_const], w=[rsb])
        self.dve(lambda e: e.reciprocal(rs[:, :n], rs[:, :n]), r=[rsb], w=[rsb])
        for kc in range(KC):
            self.dve(lambda e, kc=kc: e.scalar_tensor_tensor(dst[:, kc, d0:d0 + n], src[:, kc, c0:c0 + n], g[:, kc:kc + 1],
                                                              rs[:, :n], ALU.mult, ALU.mult),
                     r=[src_buf, rsb, self.B_const], w=[dst_buf])

    def load_w_panel(self, wsrc, c0, n, kc_n=KC):
        i = self.wp_i % len(self.wpan)
        self.wp_i += 1
        wt, wb = self.wpan[i], self.wpan_b[i]
        for kc in range(kc_n):
            self.dma("pool", wt[:, kc, :n], wsrc[kc * 128:(kc + 1) * 128, c0:c0 + n], w=[wb])
        return wt, wb

    def ttiles(self):
        return [(i * 512, 512) for i in range(4)] + [(TP, TS)]

    def tblocks(self):
        return [(i * 128, 128) for i in range(16)] + [(TP, 16), (TP + 16, 16)]

    def dram(self, name, shape, dtype=F32):
        t = self.nc.dram_tensor(name, list(shape), dtype).ap()
        return t, Buf(name)

    def allgather(self, xin, xin_b, xout, xout_b):
        groups = [[2 * i, 2 * i + 1] for i in range(self.ncores // 2)]
        self.P.add("pool", lambda e: e.collective_compute("AllGather", ALU.bypass, replica_groups=groups,
                                                          ins=[xin], outs=[xout]),
                   [xin_b], [xout_b], dma=True, q="cc")

    def dump_h(self, name):
        t = self.dout(name, [128, KC * T])
        self.dma("sp", t, self.hT.rearrange("p c t -> p (c t)"), r=[self.hT_b])

    def alloc_stage(self):
        self.stage_f = [self.A.alloc([128, DM], F32, "stg") for _ in range(2)]
        self.stage_fb = [Buf("stg") for _ in range(2)]

    def alloc_wpan(self, n=2):
        self.wpan = [self.A.alloc([128, KC, 512], BF16, "wpan") for _ in range(n)]
        self.wpan_b = [Buf("wpan") for _ in range(n)]
        self.wp_i = 0

    def bank_bf(self, ps):
        return ps.bitcast(BF16)

    def build(self):
        nc = self.nc
        A = self.A
        self.declare_io()
        self.ps_i = 0
        self.pa_i = 0
        self.stg_i = 0
        self.wp_i = 0
        self.ps = [nc.alloc_psum_tensor(f"ps{i}", [128, 512], F32).ap() for i in range(8)]
        self.psb = [Buf(f"ps{i}", excl=True) for i in range(8)]
        self.consts()
        self.eps_t = A.alloc([128, 1], F32, "eps")
        self.pool(lambda e: e.memset(self.eps_t, EPS), w=[self.B_const])
        self.hT = A.alloc([128, KC, T], F32, "hT")
        self.hT_b = Buf("hT")
        self.sqt = [A.alloc([128, 512], BF16, "sq") for _ in range(2)]
        self.sqt_b = [Buf("sq") for _ in range(2)]
        self.rstd = A.alloc([128, 512], F32, "rstd")
        self.rstd_b = Buf("rstd")
        mload = A.mark()
        self.alloc_stage()
        self.load_transposed(self.I("xp"), TP, self.hT, self.hT_b, 0)
        self.load_transposed(self.I("xs"), TS, self.hT, self.hT_b, TP)
        self.P.barrier()
        A.release(mload)

        for l in range(2):
            if self.stage <= 0:
                break
            self.layer(l)
            if self.stage < 10:
                self.dump_h("dbg_h")
                break
        if self.stage >= 10:
            self.final_phase()
        self.P.finish()
        self.P.emit()

    def layer(self, l):
        A = self.A
        m0 = A.mark()
        uT = self.uT = A.alloc([128, KC, T], BF16, "uT")
        uT_b = Buf("uT")
        self.uT_b = uT_b
        for (c0, n) in self.ttiles():
            self.rmsnorm_fm(self.hT, self.hT_b, uT, uT_b, self.g_mix[l], c0, n)
        if self.stage <= 0.5:
            return
        if self.stage >= 2:
            self.ssd_phase(l)
        if self.stage >= 3 or self.stage == 1:
            self.attn_phase(l, "sb")
        if self.stage >= 4 or self.stage == 1:
            self.attn_phase(l, "fox")
        self.P.barrier()
        A.release(m0)
        if self.stage >= 5:
            self.xattn_phase(l)
        if self.stage >= 6:
            self.ffn_phase(l)

    def kv_tm(self, l, kind):
        A = self.A
        uT, uT_b = self.uT, self.uT_b
        col = 1024 if kind == "fox" else 256
        vt = A.alloc([128, 18, 256], BF16, "v_" + kind)
        vb = Buf("v_" + kind)
        m1 = A.mark()
        self.alloc_wpan(1)
        self.alloc_stage()
        wt, wb = self.load_w_panel(self.I("w_in")[l], col, 512)
        for bi, (c0, n) in enumerate(self.tblocks()):
            ps, pb = self.psum_bank()
            for kc in range(KC):
                self.pe(lambda e, ps=ps, kc=kc, c0=c0, n=n: e.matmul(
                    ps[:n, :], uT[:, kc, c0:c0 + n], wt[:, kc, :], start=(kc == 0), stop=(kc == KC - 1)),
                    r=[uT_b, wb], w=[pb])
            st, sb = self.stage_f[self.stg_i % 2], self.stage_fb[self.stg_i % 2]
            self.stg_i += 1
            self.act(lambda e, st=st, ps=ps, n=n: e.copy(out=st[:n, 0:512], in_=ps[:n, :]), r=[pb], w=[sb])
            self.dve(lambda e, bi=bi, ps=ps, n=n: e.tensor_copy(vt[:n, bi, :], ps[:n, 256:512]), r=[pb], w=[vb])
            pre = "p_" if c0 < TP else "s_"
            r0 = c0 if c0 < TP else c0 - TP
            self.dma("sp", self.o_kv[pre + kind + "_k"][l, r0:r0 + n, :], st[:n, 0:256], r=[sb])
            self.dma("sp", self.o_kv[pre + kind + "_v"][l, r0:r0 + n, :], st[:n, 256:512], r=[sb])
        self.P.barrier()
        A.release(m1)
        return vt, vb

    def ssd_phase(self, l):
        A = self.A
        uT, uT_b = self.uT, self.uT_b
        win = self.I("w_in")[l]
        m0 = A.mark()
        cbuf = self.B_const
        XW = 3 + TP + 2 * 19
        seqs = [(0, TP, 3, 128), (TP, 16, 3 + TP + 3, 16), (TP + 16, 16, 3 + TP + 19 + 3, 16)]
        xcT = A.alloc([128, 8, T], BF16, "xcT")
        xcT_b = Buf("xcT")
        tailp = A.alloc([128, 8, 3], F32, "tailp")
        tails = A.alloc([128, 8, 2, 3], F32, "tails")
        tail_b = Buf("tail")
        prevt = A.alloc([128, 8, 3], F32, "prevt")
        prevt_b = Buf("prevt")
        cw = A.alloc([128, 8, 4], F32, "cw")
        cbias = A.alloc([128, 8], F32, "cbias")
        par_b = Buf("ssdpar")
        for kc in range(8):
            self.dma("sp", cw[:, kc, :], self.I("conv_w")[l][:, kc * 128:(kc + 1) * 128].rearrange("k p -> p k"), w=[par_b], nc_ok=True)
        self.dma("sp", cbias, self.I("conv_b")[l].rearrange("(c p) -> p c", p=128), w=[par_b], nc_ok=True)
        dtb = A.alloc([128, 8], F32, "dtb")
        alog = A.alloc([128, 8], F32, "alog")
        Abc = A.alloc([128, 8], F32, "Abc")
        dsk = A.alloc([128, 8], F32, "dsk")
        gssd = A.alloc([128, 512], F32, "gssd")
        self.dma("sp", dtb, bcast(self.I("dt_bias")[l], 0, 128), w=[par_b])
        self.dma("sp", alog, bcast(self.I("a_log")[l], 0, 128), w=[par_b])
        self.dma("sp", dsk, bcast(self.I("d_skip")[l], 0, 128), w=[par_b])
        self.dma("sp", gssd, bcast(self.I("ssd_norm_g")[l], 0, 128), w=[par_b])
        self.act(lambda e: e.activation(out=Abc, in_=alog, func=AF.Exp), r=[par_b], w=[par_b])
        self.dve(lambda e: e.tensor_scalar(Abc, Abc, -1.0, None, ALU.mult), r=[par_b], w=[par_b])
        wz = A.alloc([128, 8, 512], BF16, "wz")
        wz_b = Buf("wz")
        for kc in range(KC):
            self.dma("pool", wz[:, kc, :], win[kc * 128:(kc + 1) * 128, 1540:2052], w=[wz_b])
        wos = A.alloc([128, 4, 1024], BF16, "wos")
        wos_b = Buf("wos")
        for kc in range(4):
            self.dma("pool", wos[:, kc, :], self.I("w_out")[l][512 + kc * 128: 512 + (kc + 1) * 128, :], w=[wos_b])
        wdt = A.alloc([128, 8, 8], BF16, "wdt")
        wdt_b = Buf("wdt")
        self.dma("pool", wdt, win[:, 3076:3084].rearrange("(c p) n -> p c n", p=128), w=[wdt_b], nc_ok=True)

        m1 = A.mark()
        self.alloc_wpan()
        xtmp = [A.alloc([128, XW], BF16, "xtmp") for _ in range(2)]
        xtmp_b = [Buf("xtmp") for _ in range(2)]
        acc = [A.alloc([128, 512], F32, "acc") for _ in range(2)]
        acc_b = [Buf("acc") for _ in range(2)]
        acc3 = A.alloc([128, 8, 3], F32, "acc3")
        acc3_b = Buf("acc3")
        pf = A.alloc([128, 8, 3], F32, "pf")
        ctmp = A.alloc([128, 8], F32, "ctmp")
        pads = A.alloc([128, 8, 3, 3], BF16, "pads")
        pads_b = Buf("pads")
        pans = [self.load_w_panel(win, 2052 + pn * 512, 512) for pn in range(2)]
        for oc in range(8):
            wt, wb = pans[oc // 4]
            j = oc % 4
            ps, pb = self.psum_bank()
            for si, c0 in enumerate((TP - 32, TP)):
                for kc in range(KC):
                    self.pe(lambda e, ps=ps, kc=kc, c0=c0, si=si, wt=wt, j=j: e.matmul(
                        ps[:, si * 32:(si + 1) * 32], wt[:, kc, j * 128:(j + 1) * 128], uT[:, kc, c0:c0 + 32],
                        start=(kc == 0), stop=(kc == KC - 1)), r=[uT_b, wb], w=[pb])
            self.dve(lambda e, ps=ps, oc=oc: e.tensor_copy(tailp[:, oc, :], ps[:, 29:32]), r=[pb], w=[tail_b])
            for jj in range(2):
                self.dve(lambda e, ps=ps, oc=oc, jj=jj: e.tensor_copy(tails[:, oc, jj, :], ps[:, 32 + jj * 16 + 13:32 + jj * 16 + 16]),
                         r=[pb], w=[tail_b])
        x1in, x1in_b = self.dram(f"x1in{l}", [128, 24])
        x1out, x1out_b = self.dram(f"x1out{l}", [256, 24])
        self.dve(lambda e: e.memset(pads[:, :, 0, :], 0.0), w=[pads_b])
        stc = A.alloc([3, 2, DM], F32, "stc")
        stc_b = Buf("stc")
        for jj in range(2):
            self.dma("sp", stc[:, jj, :], self.I("st_conv")[l, jj], w=[stc_b])
        ps_c, pb_c = self.psum_bank()
        for jj in range(2):
            for kc in range(8):
                self.pe(lambda e, jj=jj, kc=kc: e.transpose(ps_c[:, (jj * 8 + kc) * 3:(jj * 8 + kc) * 3 + 3], stc[:, jj, kc * 128:(kc + 1) * 128], self.ident_f[:3, :3]),
                        r=[stc_b, cbuf], w=[pb_c])
        for jj in range(2):
            self.act(lambda e, jj=jj: e.copy(out=pads[:, :, 1 + jj, :], in_=ps_c[:, jj * 24:(jj + 1) * 24].rearrange("p (c t) -> p c t", t=3)),
                     r=[pb_c], w=[pads_b])
        for kc in range(8):
            for jj in range(2):
                self.dma("sp", self.s_conv[l, jj][:, kc * 128:(kc + 1) * 128].rearrange("t p -> p t"), tails[:, kc, jj, :], r=[tail_b], nc_ok=True)
            self.dma("sp", self.p_conv[l][:, kc * 128:(kc + 1) * 128].rearrange("t p -> p t"), tailp[:, kc, :], r=[tail_b], nc_ok=True)
        self.dma("sp", x1in, tailp.rearrange("p a b -> p (a b)"), r=[tail_b], w=[x1in_b])
        self.allgather(x1in, x1in_b, x1out, x1out_b)
        self.dma("sp", prevt.rearrange("p a b -> p (a b)"), x1out[0:128, :], r=[x1out_b], w=[prevt_b])
        ai = 0
        for oc in range(8):
            wt, wb = pans[oc // 4]
            j = oc % 4
            xt, xtb = xtmp[oc % 2], xtmp_b[oc % 2]
            for si, (tok0, n, xc, L) in enumerate(seqs):
                self.act(lambda e, xt=xt, oc=oc, si=si, xc=xc: e.copy(out=xt[:, xc - 3:xc], in_=pads[:, oc, si, :]), r=[pads_b], w=[xtb])
            for (c0, n) in self.ttiles():
                ps, pb = self.psum_bank()
                for kc in range(KC):
                    self.pe(lambda e, ps=ps, kc=kc, c0=c0, n=n, wt=wt, j=j: e.matmul(
                        ps[:, :n], wt[:, kc, j * 128:(j + 1) * 128], uT[:, kc, c0:c0 + n],
                        start=(kc == 0), stop=(kc == KC - 1)), r=[uT_b, wb], w=[pb])
                if c0 < TP:
                    self.act(lambda e, ps=ps, xt=xt, c0=c0, n=n: e.copy(out=xt[:, 3 + c0:3 + c0 + n], in_=ps[:, :n]), r=[pb], w=[xtb])
                else:
                    for jj in range(2):
                        xc = seqs[1 + jj][2]
                        self.act(lambda e, ps=ps, xt=xt, xc=xc, jj=jj: e.copy(out=xt[:, xc:xc + 16], in_=ps[:, jj * 16:(jj + 1) * 16]), r=[pb], w=[xtb])
            segs = [(c0, 512, 3 + c0) for c0 in range(0, TP, 512)] + [(TP, 16, seqs[1][2]), (TP + 16, 16, seqs[2][2])]
            for (tok0, n, xc) in segs:
                ac, acb = acc[ai % 2], acc_b[ai % 2]
                ai += 1
                self.dve(lambda e, ac=ac, xt=xt, oc=oc, xc=xc, n=n: e.tensor_scalar(ac[:, :n], xt[:, xc - 3:xc - 3 + n], cw[:, oc, 0:1], None, ALU.mult),
                         r=[xtb, par_b], w=[acb])
                for k in range(1, 4):
                    self.dve(lambda e, ac=ac, xt=xt, oc=oc, xc=xc, n=n, k=k: e.scalar_tensor_tensor(
                        ac[:, :n], xt[:, xc - 3 + k:xc - 3 + k + n], cw[:, oc, k:k + 1], ac[:, :n], ALU.mult, ALU.add),
                        r=[xtb, par_b, acb], w=[acb])
                self.act(lambda e, ac=ac, oc=oc, tok0=tok0, n=n: e.activation(out=xcT[:, oc, tok0:tok0 + n], in_=ac[:, :n], func=AF.Silu,
                                                                              bias=cbias[:, oc:oc + 1]),
                         r=[acb, par_b], w=[xcT_b])
                if tok0 == 0:
                    self.dve(lambda e, ac=ac, oc=oc: e.tensor_copy(acc3[:, oc, :], ac[:, 0:3]), r=[acb], w=[acc3_b])
        self.dve(lambda e: e.tensor_scalar(pf, prevt, self.flags_t[:, 0:1], None, ALU.mult), r=[prevt_b, cbuf], w=[acc3_b])
        for t in range(3):
            for k in range(3 - t):
                self.dve(lambda e, t=t, k=k: e.tensor_tensor(ctmp, cw[:, :, k], pf[:, :, t + k], ALU.mult), r=[par_b, acc3_b], w=[acc3_b])
                self.dve(lambda e, t=t: e.tensor_tensor(acc3[:, :, t], acc3[:, :, t], ctmp, ALU.add), r=[acc3_b], w=[acc3_b])
        self.dve(lambda e: e.tensor_tensor(acc3, acc3, bcast(cbias, 2, 3), ALU.add), r=[acc3_b, par_b], w=[acc3_b])
        self.act(lambda e: e.activation(out=xcT[:, :, 0:3], in_=acc3, func=AF.Silu), r=[acc3_b], w=[xcT_b])
        self.P.barrier()
        A.release(m1)
        NB = 18
        blocks = self.tblocks()
        ps_dt, pb_dt = self.psum_bank()
        for bi, (c0, n) in enumerate(blocks):
            for kc in range(KC):
                self.pe(lambda e, bi=bi, kc=kc, c0=c0, n=n: e.matmul(ps_dt[:n, bi * 8:(bi + 1) * 8], uT[:, kc, c0:c0 + n], wdt[:, kc, :],
                                                                     start=(kc == 0), stop=(kc == KC - 1)),
                        r=[uT_b, wdt_b], w=[pb_dt])
        dt_all = A.alloc([128, NB, 8], F32, "dt_all")
        a_all = A.alloc([128, NB, 8], F32, "a_all")
        dta_b = Buf("dta")
        self.dve(lambda e: e.tensor_tensor(dt_all, ps_dt[:, :NB * 8].rearrange("p (b h) -> p b h", h=8), bcast(dtb, 1, NB), ALU.add),
                 r=[pb_dt, par_b], w=[dta_b])
        self.act(lambda e: e.activation(out=dt_all, in_=dt_all, func=AF.Exp), r=[dta_b], w=[dta_b])
        self.act(lambda e: e.activation(out=dt_all, in_=dt_all, func=AF.Ln, bias=1.0), r=[dta_b], w=[dta_b])
        self.dve(lambda e: e.tensor_tensor(a_all, dt_all, bcast(Abc, 1, NB), ALU.mult), r=[dta_b, par_b], w=[dta_b])
        ps_ac, pb_ac = self.psum_bank()
        ps_at, pb_at = self.psum_bank()
        a2 = a_all.rearrange("p b h -> p (b h)")
        self.pe(lambda e: e.matmul(ps_ac[:, 0:128], self.U_f, a2[:, 0:128], start=True, stop=True), r=[dta_b, cbuf], w=[pb_ac])
        self.pe(lambda e: e.matmul(ps_ac[:16, 128:144], self.U_f[:16, :16], a2[:16, 128:144], start=True, stop=True), r=[dta_b, cbuf], w=[pb_ac])
        self.pe(lambda e: e.matmul(ps_at[:, 0:128], self.ones_f, a2[:, 0:128], start=True, stop=True), r=[dta_b, cbuf], w=[pb_at])
        self.pe(lambda e: e.matmul(ps_at[:, 128:144], self.ones_f[:16, :], a2[:16, 128:144], start=True, stop=True), r=[dta_b, cbuf], w=[pb_at])
        acum = A.alloc([128, 144], F32, "acum")
        ea = A.alloc([128, 144], F32, "ea")
        dec = A.alloc([128, 144], F32, "dec")
        wend = A.alloc([128, 144], F32, "wend")
        cs_b = Buf("cs")
        self.act(lambda e: e.copy(out=acum, in_=ps_ac[:, :144]), r=[pb_ac], w=[cs_b])
        self.act(lambda e: e.activation(out=ea, in_=ps_ac[:, :144], func=AF.Exp), r=[pb_ac], w=[cs_b])
        self.act(lambda e: e.activation(out=dec, in_=ps_at[:, :144], func=AF.Exp), r=[pb_at], w=[cs_b])
        self.dve(lambda e: e.tensor_tensor(wend, ps_at[:, :144], acum, ALU.subtract), r=[pb_at, cs_b], w=[cs_b])
        self.act(lambda e: e.activation(out=wend, in_=wend, func=AF.Exp), r=[cs_b], w=[cs_b])

        NR = 2
        xs_tm = [A.alloc([128, 512], BF16, "xs_tm") for _ in range(NR)]
        btm = [A.alloc([128, 256], BF16, "btm") for _ in range(NR)]
        xdt = [A.alloc([128, 512], BF16, "xdt") for _ in range(NR)]
        xdtw = [A.alloc([128, 512], BF16, "xdtw") for _ in range(NR)]
        prep_b = [Buf("prep") for _ in range(NR)]
        self.prep_i = 0

        def prep(bi, c0, L):
            i = self.prep_i % NR
            self.prep_i += 1
            pb_ = prep_b[i]
            ps, pb = self.psum_bank()
            pv = self.bank_bf(ps)
            for f in range(6):
                self.pe(lambda e, f=f: e.transpose(pv[:L, f * 128:(f + 1) * 128], xcT[:, f, c0:c0 + L], self.ident_bf),
                        r=[xcT_b, cbuf], w=[pb])
            self.act(lambda e: e.copy(out=xs_tm[i][:L, :], in_=pv[:L, 0:512]), r=[pb], w=[pb_])
            self.act(lambda e: e.copy(out=btm[i][:L, :], in_=pv[:L, 512:768]), r=[pb], w=[pb_])
            self.dve(lambda e: e.tensor_tensor(xdt[i][:L, :].rearrange("p (h d) -> p h d", h=8),
                                               xs_tm[i][:L, :].rearrange("p (h d) -> p h d", h=8),
                                               bcast(dt_all[:L, bi, :], 2, 64), ALU.mult), r=[pb_, dta_b], w=[pb_])
            self.dve(lambda e: e.tensor_tensor(xdtw[i][:L, :].rearrange("p (h d) -> p h d", h=8),
                                               xdt[i][:L, :].rearrange("p (h d) -> p h d", h=8),
                                               bcast(wend[:L, bi * 8:(bi + 1) * 8], 2, 64), ALU.mult), r=[pb_, cs_b], w=[pb_])
            return i, pb_

        hs = A.alloc([128, 512], F32, "hs")
        hs_bf = A.alloc([128, 512], BF16, "hs_bf")
        hs_b = Buf("hs")
        hsb_b = Buf("hs_bf")

        def state_update(bi, L, i, pb_):
            ps, pb = self.psum_bank()
            for g in range(2):
                self.pe(lambda e, g=g: e.matmul(ps[:, g * 256:(g + 1) * 256], btm[i][:L, g * 128:(g + 1) * 128], xdtw[i][:L, g * 256:(g + 1) * 256],
                                                start=True, stop=True), r=[pb_], w=[pb])
            h3 = hs.rearrange("p (h d) -> p h d", h=8)
            self.dve(lambda e: e.tensor_tensor(h3, h3, bcast(dec[:, bi * 8:(bi + 1) * 8], 2, 64), ALU.mult), r=[hs_b, cs_b], w=[hs_b])
            self.dve(lambda e: e.tensor_tensor(hs, hs, ps, ALU.add), r=[hs_b, pb], w=[hs_b])

        self.dve(lambda e: e.memset(hs, 0.0), w=[hs_b])
        for ci in range(16):
            i, pb_ = prep(ci, ci * 128, 128)
            state_update(ci, 128, i, pb_)
        x2in, x2in_b = self.dram(f"x2in{l}", [128, 512])
        x2out, x2out_b = self.dram(f"x2out{l}", [256, 512])
        self.dma("sp", x2in, hs, r=[hs_b], w=[x2in_b])
        self.allgather(x2in, x2in_b, x2out, x2out_b)

        t1_l = [A.alloc([128, 512], F32, "t1") for _ in range(2)]
        t4_l = [A.alloc([128, 512], F32, "t4") for _ in range(2)]
        yb_l = [Buf("y") for _ in range(2)]
        gm_l = [A.alloc([128, 2, 128], F32, "gm") for _ in range(2)]
        RA_l = [A.alloc([128, 8, 128], F32, "RA") for _ in range(2)]
        Lm_l = [A.alloc([128, 8, 128], BF16, "Lm") for _ in range(2)]
        Mm_l = [A.alloc([128, 8, 128], BF16, "Mm") for _ in range(2)]
        mm_bl = [Buf("mm") for _ in range(2)]
        zs_l = [A.alloc([128, 512], F32, "zs") for _ in range(2)]
        zs_bl = [Buf("zs") for _ in range(2)]
        ss_l = [A.alloc([128, 1], F32, "ss") for _ in range(2)]
        ob_l = [A.alloc([128, 512], BF16, "ob") for _ in range(2)]
        oT_l = [A.alloc([128, 4, 128], BF16, "oT") for _ in range(2)]
        o_bl = [Buf("o") for _ in range(2)]
        sttm = A.alloc([128, 4, 128], F32, "sttm")
        sttm_b = Buf("sttm")

        def load_state_tm(src):
            self.dma("sp", sttm, src.rearrange("(c p) n -> p c n", p=128), w=[sttm_b])
            ps, pb = self.psum_bank()
            for c in range(4):
                self.pe(lambda e, c=c: e.transpose(ps[:, c * 128:(c + 1) * 128], sttm[:, c, :], self.ident_f), r=[sttm_b, cbuf], w=[pb])
            self.dve(lambda e: e.tensor_copy(hs, ps), r=[pb], w=[hs_b])

        def store_state_tm(dst):
            ps, pb = self.psum_bank()
            for c in range(4):
                self.pe(lambda e, c=c: e.transpose(ps[:, c * 128:(c + 1) * 128], hs[:, c * 128:(c + 1) * 128], self.ident_f), r=[hs_b, cbuf], w=[pb])
            self.act(lambda e: e.copy(out=sttm, in_=ps.rearrange("p (c n) -> p c n", c=4)), r=[pb], w=[sttm_b])
            self.dma("sp", dst.rearrange("(c p) n -> p c n", p=128), sttm, r=[sttm_b])

        def chunk_full(bi, c0, L):
            par = bi % 2
            t1, t4, yb = t1_l[par], t4_l[par], yb_l[par]
            gm, RA, Lm, Mm, mm_b = gm_l[par], RA_l[par], Lm_l[par], Mm_l[par], mm_bl[par]
            zs, zs_b, ss, ob, oT, o_b = zs_l[par], zs_bl[par], ss_l[par], ob_l[par], oT_l[par], o_bl[par]
            i, pb_ = prep(bi, c0, L)
            ps_g, pb_g = self.psum_bank()
            for g in range(2):
                self.pe(lambda e, g=g: e.matmul(ps_g[:L, g * 128:g * 128 + L], xcT[:, 4 + g, c0:c0 + L], xcT[:, 6 + g, c0:c0 + L], start=True, stop=True),
                        r=[xcT_b], w=[pb_g])
            self.dve(lambda e: e.tensor_tensor(gm[:L, :, :L], ps_g[:L, 0:256].rearrange("p (g t) -> p g t", g=2)[:, :, :L],
                                               bcast(self.U_f[:L, :L], 1, 2), ALU.mult), r=[pb_g, cbuf], w=[mm_b])
            self.dve(lambda e: e.tensor_tensor(RA[:L, :, :L], bcast(self.U_f[:L, :L], 1, 8), bcast(a_all[:L, bi, :], 2, L), ALU.mult),
                     r=[dta_b, cbuf], w=[mm_b])
            for hh in range(2):
                ps_s, pb_s = self.psum_bank()
                self.pe(lambda e, hh=hh, ps_s=ps_s: e.matmul(ps_s[:L, :4 * L].rearrange("p (h t) -> p h t", h=4), self.SL_f[:L, :L], RA[:L, hh * 4:hh * 4 + 4, :L],
                                                             start=True, stop=True), r=[mm_b, cbuf], w=[pb_s])
                self.act(lambda e, hh=hh, ps_s=ps_s: e.activation(out=Lm[:L, hh * 4:hh * 4 + 4, :L], in_=ps_s[:L, :4 * L].rearrange("p (h t) -> p h t", h=4), func=AF.Exp),
                         r=[pb_s], w=[mm_b])
                self.pool(lambda e, hh=hh: e.tensor_tensor(Mm[:L, hh * 4:hh * 4 + 4, :L], Lm[:L, hh * 4:hh * 4 + 4, :L], bcast(gm[:L, hh, :L], 1, 4), ALU.mult),
                          r=[mm_b], w=[mm_b])
            ps_yd, pb_yd = self.psum_bank()
            for h in range(8):
                self.pe(lambda e, h=h: e.matmul(ps_yd[:L, h * 64:(h + 1) * 64], Mm[:L, h, :L], xdt[i][:L, h * 64:(h + 1) * 64], start=True, stop=True),
                        r=[mm_b, pb_], w=[pb_yd])
            ps_yo, pb_yo = self.psum_bank()
            for g in range(2):
                self.pe(lambda e, g=g: e.matmul(ps_yo[:L, g * 256:(g + 1) * 256], xcT[:, 6 + g, c0:c0 + L], hs_bf[:, g * 256:(g + 1) * 256], start=True, stop=True),
                        r=[xcT_b, hsb_b], w=[pb_yo])
            self.dve(lambda e: e.tensor_tensor(t1[:L, :].rearrange("p (h d) -> p h d", h=8), ps_yo[:L, :].rearrange("p (h d) -> p h d", h=8),
                                               bcast(ea[:L, bi * 8:(bi + 1) * 8], 2, 64), ALU.mult), r=[pb_yo, cs_b], w=[yb])
            self.dve(lambda e: e.tensor_tensor(t1[:L, :], t1[:L, :], ps_yd[:L, :], ALU.add), r=[pb_yd, yb], w=[yb])
            self.pool(lambda e: e.tensor_tensor(t4[:L, :].rearrange("p (h d) -> p h d", h=8), xs_tm[i][:L, :].rearrange("p (h d) -> p h d", h=8),
                                                bcast(dsk[:L, :], 2, 64), ALU.mult), r=[pb_, par_b], w=[zs_b])
            self.dve(lambda e: e.tensor_tensor(t1[:L, :], t1[:L, :], t4[:L, :], ALU.add), r=[yb, zs_b], w=[yb])
            state_update(bi, L, i, pb_)
            self.act(lambda e: e.copy(out=hs_bf, in_=hs), r=[hs_b], w=[hsb_b])
            ps_z, pb_z = self.psum_bank()
            for kc in range(KC):
                self.pe(lambda e, kc=kc: e.matmul(ps_z[:L, :], uT[:, kc, c0:c0 + L], wz[:, kc, :], start=(kc == 0), stop=(kc == KC - 1)),
                        r=[uT_b, wz_b], w=[pb_z])
            self.act(lambda e: e.activation(out=zs[:L, :], in_=ps_z[:L, :], func=AF.Silu), r=[pb_z], w=[zs_b])
            self.dve(lambda e: e.tensor_tensor(t1[:L, :], t1[:L, :], zs[:L, :], ALU.mult), r=[yb, zs_b], w=[yb])
            self.act(lambda e: e.activation(out=zs[:L, :], in_=t1[:L, :], func=AF.Square, accum_out=ss[:L, 0:1]), r=[yb], w=[zs_b])
            self.act(lambda e: e.activation(out=ss[:L, :], in_=ss[:L, :], func=AF.Sqrt, bias=self.eps_t[:L, 0:1], scale=1.0 / 512), r=[zs_b, cbuf], w=[zs_b])
            self.dve(lambda e: e.reciprocal(ss[:L, :], ss[:L, :]), r=[zs_b], w=[zs_b])
            self.dve(lambda e: e.scalar_tensor_tensor(ob[:L, :], t1[:L, :], ss[:L, 0:1], gssd[:L, :], ALU.mult, ALU.mult), r=[yb, zs_b, par_b], w=[o_b])
            ps_t, pb_t = self.psum_bank()
            pv = self.bank_bf(ps_t)
            for f in range(4):
                self.pe(lambda e, f=f: e.transpose(pv[:, f * 128:f * 128 + L], ob[:L, f * 128:(f + 1) * 128], self.ident_bf[:L, :L]), r=[o_b, cbuf], w=[pb_t])
            self.act(lambda e: e.copy(out=oT[:, :, :L], in_=pv[:, 0:512].rearrange("p (f t) -> p f t", f=4)[:, :, :L]), r=[pb_t], w=[o_b])
            for half in range(2):
                ps_o, pb_o = self.psum_bank()
                for oo in range(4):
                    o_c = half * 4 + oo
                    for kc in range(4):
                        self.pe(lambda e, oo=oo, o_c=o_c, kc=kc, ps_o=ps_o: e.matmul(ps_o[:, oo * 128:oo * 128 + L], wos[:, kc, o_c * 128:(o_c + 1) * 128], oT[:, kc, :L],
                                                                                     start=(kc == 0), stop=(kc == 3)), r=[wos_b, o_b], w=[pb_o])
                hsl = self.hT[:, half * 4:half * 4 + 4, c0:c0 + L]
                self.dve(lambda e, hsl=hsl, ps_o=ps_o: e.tensor_tensor(hsl, hsl, ps_o.rearrange("p (o t) -> p o t", o=4)[:, :, :L], ALU.add),
                         r=[pb_o, self.hT_b], w=[self.hT_b])

        self.dma("sp", hs, x2out[0:128, :], r=[x2out_b], w=[hs_b])
        self.dve(lambda e: e.tensor_scalar(hs, hs, self.flags_t[:, 0:1], None, ALU.mult), r=[hs_b, cbuf], w=[hs_b])
        self.act(lambda e: e.copy(out=hs_bf, in_=hs), r=[hs_b], w=[hsb_b])
        for ci in range(16):
            chunk_full(ci, ci * 128, 128)
        store_state_tm(self.p_ssm[l])
        for jj in range(2):
            load_state_tm(self.I("st_ssm")[l, jj])
            self.act(lambda e: e.copy(out=hs_bf, in_=hs), r=[hs_b], w=[hsb_b])
            chunk_full(16 + jj, TP + 16 * jj, 16)
            store_state_tm(self.s_ssm[l, jj])
        self.P.barrier()
        A.release(m0)


    def attn_phase(self, l, kind):
        A = self.A
        uT, uT_b = self.uT, self.uT_b
        cbuf = self.B_const
        win = self.I("w_in")[l]
        m0 = A.mark()
        fox = kind == "fox"
        KA = 71 if fox else 65
        VW = 128
        colq = 768 if fox else 0
        wo_base = 256 if fox else 0
        vt, vb = self.kv_tm(l, kind)
        if self.stage == 1:
            self.P.barrier()
            A.release(m0)
            return
        Fk = Fq = None
        if fox:
            Fk, Fq, Fk_b, Fp, Fp_b = self.fox_gates(l)
        ck = self.I("c_fox_k" if fox else "c_sb_k")
        cv = self.I("c_fox_v" if fox else "c_sb_v")
        blocks = self.tblocks()
        wt = A.alloc([128, KC, 256], BF16, "wqk")
        wb = Buf("wqk")
        wqh = A.alloc([128, KC, 128], BF16, "wqh")
        wqh_b = Buf("wqh")
        for kc in range(KC):
            self.dma("pool", wt[:, kc, :], win[kc * 128:(kc + 1) * 128, colq + 256:colq + 512], w=[wb])
        wo = A.alloc([128, 1, DM], BF16, "wo_att")
        wo_b = Buf("wo_att")
        kT = [A.alloc([128, T], BF16, "kT") for _ in range(4)]
        kT_b = [Buf("kT") for _ in range(4)]
        NB2 = 1
        qT = [A.alloc([128, T], BF16, "qT") for _ in range(2)]
        qT_b = [Buf("qT") for _ in range(2)]
        oT = [A.alloc([128, T], BF16, "oT") for _ in range(NB2)]
        oT_b = [Buf("oT") for _ in range(NB2)]
        kTp = [A.alloc([128, TP], BF16, "kTp") for _ in range(NB2)]
        kTp_b = [Buf("kTp") for _ in range(NB2)]
        vtp = [A.alloc([128, 16, VW], BF16, "vtp") for _ in range(NB2)]
        vtp_b = [Buf("vtp") for _ in range(NB2)]
        vown = [A.alloc([128, 18, VW], BF16, "vown") for _ in range(NB2)]
        vown_b = [Buf("vown") for _ in range(NB2)]
        kTc = [A.alloc([128, PAST], BF16, "kTc") for _ in range(NB2)]
        kTc_b = [Buf("kTc") for _ in range(NB2)]
        vtc = [A.alloc([128, 32, VW], BF16, "vtc") for _ in range(NB2)]
        vtc_b = [Buf("vtc") for _ in range(NB2)]
        kst = A.alloc([128, 32, 64], BF16, "kst")
        kst_b = Buf("kst")
        e32 = [A.alloc([128, 512], BF16, "e32") for _ in range(4)]
        e32_b = [Buf("e32") for _ in range(4)]
        tbf = [A.alloc([128, 512], BF16, "tbf") for _ in range(2)]
        tbf_b = [Buf("tbf") for _ in range(2)]
        spb = [A.alloc([128, 512], BF16, "spb") for _ in range(3)]
        spb_b = [Buf("spb") for _ in range(3)]
        arg = [A.alloc([128, 512], F32, "arg") for _ in range(2)]
        arg_b = [Buf("arg") for _ in range(2)]
        wbf = [A.alloc([128, 512], BF16, "wbf") for _ in range(3)]
        wbf_b = [Buf("wbf") for _ in range(3)]
        self.ag_i = 0
        self.wb_i = 0
        carry = A.alloc([128, 512], F32, "carry")
        carry_b = Buf("carry")
        rec = A.alloc([128, 512], F32, "rec")
        rec_b = Buf("rec")
        self.aw_i = 0

        for kt_, ktb_ in [(kT[i], kT_b[i]) for i in range(4)] + [(qT[i], qT_b[i]) for i in range(2)] + [(kTp[0], kTp_b[0]), (kTc[0], kTc_b[0])]:
            self.dve(lambda e, kt_=kt_: e.memset(kt_[64:128, :], 0.0), w=[ktb_])
        for vt_, vtb_ in ((vtp[0], vtp_b[0]), (vown[0], vown_b[0]), (vtc[0], vtc_b[0])):
            self.pool(lambda e, vt_=vt_: e.memset(vt_[:, :, 64:128], 1.0), w=[vtb_])
        def proj_pair(dst0, dst0_b, dst1, dst1_b, wsl, wsl_b, scale):
            for (c0, n) in self.ttiles():
                ps, pb = self.psum_bank()
                for kc in range(KC):
                    self.pe(lambda e, ps=ps, kc=kc, c0=c0, n=n: e.matmul(ps[:, :n], wsl[:, kc, :], uT[:, kc, c0:c0 + n],
                                                                         start=(kc == 0), stop=(kc == KC - 1)), r=[uT_b, wsl_b], w=[pb])
                self.act(lambda e, ps=ps, c0=c0, n=n: e.activation(out=dst0[0:64, c0:c0 + n], in_=ps[0:64, :n], func=AF.Copy, scale=scale), r=[pb], w=[dst0_b])
                self.act(lambda e, ps=ps, c0=c0, n=n: e.activation(out=dst1[0:64, c0:c0 + n], in_=ps[64:128, :n], func=AF.Copy, scale=scale), r=[pb], w=[dst1_b])

        for hp in range(2):
            proj_pair(kT[2 * hp], kT_b[2 * hp], kT[2 * hp + 1], kT_b[2 * hp + 1], wt[:, :, hp * 128:(hp + 1) * 128], wb, 1.0)
        for h in range(4):
            if fox:
                self.dve(lambda e, h=h: e.memset(kT[h][64:71, :], 1.0), w=[kT_b[h]])
                self.dve(lambda e, h=h: e.memset(kT[h][64:65, :], 0.0), w=[kT_b[h]])
                self.dma("sp", kT[h][65:68, :], Fk[h], r=[Fk_b], w=[kT_b[h]])
            else:
                self.dve(lambda e, h=h: e.memset(kT[h][64:65, :], 0.0), w=[kT_b[h]])
        for kc in range(KC):
            self.dma("pool", wqh[:, kc, :], win[kc * 128:(kc + 1) * 128, colq:colq + 128], w=[wqh_b], nc_ok=True)
        self.dma("pool", wo[:, 0, :], self.I("w_out")[l][wo_base: wo_base + 128, :], w=[wo_b])
        proj_pair(qT[0], qT_b[0], qT[1], qT_b[1], wqh, wqh_b, 0.125)
        xk_in, xk_in_b = self.dram(f"xk_in_{kind}{l}", [4 * KA, TP], BF16)
        xk_out, xk_out_b = self.dram(f"xk_out_{kind}{l}", [8 * KA, TP], BF16)
        xv_in, xv_in_b = self.dram(f"xv_in_{kind}{l}", [128, 16 * 256], BF16)
        xv_out, xv_out_b = self.dram(f"xv_out_{kind}{l}", [256, 16 * 256], BF16)
        onesrow = A.alloc([1, TP], BF16, "onesrow")
        onesrow_b = Buf("onesrow")
        self.dve(lambda e: e.memset(onesrow, 1.0), w=[onesrow_b])
        for h in range(4):
            self.dma("sp", xk_in[h * KA:h * KA + 64, :], kT[h][0:64, 0:TP], r=[kT_b[h]], w=[xk_in_b])
            self.dma("sp", xk_in[h * KA + 64:h * KA + 65, :], onesrow, r=[onesrow_b], w=[xk_in_b])
            if fox:
                self.dma("sp", xk_in[h * KA + 65:(h + 1) * KA, :], kT[h][65:KA, 0:TP], r=[kT_b[h]], w=[xk_in_b])
        self.dma("sp", xv_in, vt[:, 0:16, :].rearrange("p b c -> p (b c)"), r=[vb], w=[xv_in_b])
        self.allgather(xk_in, xk_in_b, xk_out, xk_out_b)
        self.allgather(xv_in, xv_in_b, xv_out, xv_out_b)

        def run_tile(blist, q_ap, q_b, nq, ps_o, pb_o):
            nb = len(blist)
            ctx = [dict() for _ in range(nb)]

            def sb_A(b):
                kT_ap, v_ap, kbufs, L, lo, diag = blist[b]
                n = nq - lo
                c = ctx[b]
                ps_z, pb_z = self.psum_bank()
                self.pe(lambda e: e.matmul(ps_z[:L, lo:nq], kT_ap, q_ap[:, lo:nq], start=True, stop=not diag), r=kbufs + [q_b], w=[pb_z])
                if diag:
                    self.pe(lambda e: e.matmul(ps_z[:L, lo:lo + L], self.ident_bf[:L, :L], self.MSK_s[:L, :L], start=False, stop=True), r=[cbuf], w=[pb_z])
                i = b % 4
                c["sp"], c["sp_b"] = spb[i % 3] if False else spb[b % 3], spb_b[b % 3]
                c["e"], c["e_b"] = e32[i], e32_b[i]
                self.act(lambda e: e.activation(out=e32[i][:L, :n], in_=ps_z[:L, lo:nq], func=AF.Exp), r=[pb_z], w=[e32_b[i]])
                self.act(lambda e: e.activation(out=spb[b % 3][:L, :n], in_=e32[i][:L, :n], func=AF.Ln, bias=1.0), r=[e32_b[i]], w=[spb_b[b % 3]])

            def sb_B1(b):
                kT_ap, v_ap, kbufs, L, lo, diag = blist[b]
                n = nq - lo
                c = ctx[b]
                sp_, sp_b = c["sp"], c["sp_b"]
                ps_r, pb_r = self.psum_bank()
                self.pe(lambda e: e.matmul(ps_r[:L, lo:nq], self.NLI_bf[:L, :L], sp_[:L, :n], start=True, stop=True), r=[sp_b, cbuf], w=[pb_r])
                ps_t, pb_t = self.psum_bank()
                self.pe(lambda e: e.matmul(ps_t[:, lo:nq], self.ones_bf[:L, :], sp_[:L, :n], start=True, stop=True), r=[sp_b, cbuf], w=[pb_t])
                j = b % 2
                c["arg"], c["arg_b"] = arg[j], arg_b[j]
                self.dve(lambda e: e.tensor_tensor(arg[j][:L, :n], ps_r[:L, lo:nq], carry[:L, lo:nq], ALU.subtract), r=[pb_r, carry_b], w=[arg_b[j]])
                self.dve(lambda e: e.tensor_tensor(carry[:, lo:nq], carry[:, lo:nq], ps_t[:, lo:nq], ALU.add), r=[pb_t, carry_b], w=[carry_b])

            def sb_B2(b):
                kT_ap, v_ap, kbufs, L, lo, diag = blist[b]
                n = nq - lo
                c = ctx[b]
                j = b % 2
                ar, ar_b = c["arg"], c["arg_b"]
                ee, ee_b = c["e"], c["e_b"]
                self.act(lambda e: e.activation(out=tbf[j][:L, :n], in_=ar[:L, :n], func=AF.Exp), r=[ar_b], w=[tbf_b[j]])
                jw = b % 3
                self.dve(lambda e: e.tensor_tensor(wbf[jw][:L, :n], ee[:L, :n], tbf[j][:L, :n], ALU.mult), r=[ee_b, tbf_b[j]], w=[wbf_b[jw]])

            def sb_B3(b):
                kT_ap, v_ap, kbufs, L, lo, diag = blist[b]
                n = nq - lo
                j = b % 3
                self.pe(lambda e: e.matmul(ps_o[:, lo:nq], v_ap, wbf[j][:L, :n], start=(b == 0), stop=(b == nb - 1), skip_group_check=True),
                        r=kbufs + [wbf_b[j]], w=[pb_o])

            def fox_A(b):
                kT_ap, v_ap, kbufs, L, lo, diag = blist[b]
                n = nq - lo
                c = ctx[b]
                ps_z, pb_z = self.psum_bank()
                self.pe(lambda e: e.matmul(ps_z[:L, lo:nq], kT_ap, q_ap[:, lo:nq], start=True, stop=not diag), r=kbufs + [q_b], w=[pb_z])
                if diag:
                    self.pe(lambda e: e.matmul(ps_z[:L, lo:lo + L], self.ident_bf[:L, :L], self.MSK_i[:L, :L], start=False, stop=True), r=[cbuf], w=[pb_z])
                i = b % 3
                c["w"], c["w_b"] = spb[i], spb_b[i]
                self.act(lambda e: e.activation(out=spb[i][:L, :n], in_=ps_z[:L, lo:nq], func=AF.Exp), r=[pb_z], w=[spb_b[i]])

            def fox_B(b):
                kT_ap, v_ap, kbufs, L, lo, diag = blist[b]
                n = nq - lo
                c = ctx[b]
                w_, w_b = c["w"], c["w_b"]
                self.pe(lambda e: e.matmul(ps_o[:, lo:nq], v_ap, w_[:L, :n], start=(b == 0), stop=(b == nb - 1), skip_group_check=True),
                        r=kbufs + [w_b], w=[pb_o])

            if fox:
                stages = [(fox_A, 0), (fox_B, 1)]
            else:
                stages = [(sb_B2, 2), (sb_A, 0), (sb_B1, 1), (sb_B3, 4)]
            S = max(k for _, k in stages)
            for s_ in range(nb + S):
                for st, k in stages:
                    bb = s_ - k
                    if 0 <= bb < nb:
                        st(bb)

        def finish_tile(ps_o, pb_o, dst, dst_b, c0, nq, po=0):
            if not fox:
                self.act(lambda e: e.copy(out=dst[po:po + 64, c0:c0 + nq], in_=ps_o[:64, :nq]), r=[pb_o], w=[dst_b])
                return
            self.dve(lambda e: e.reciprocal(rec[64:128, :nq], ps_o[64:128, :nq]), r=[pb_o], w=[rec_b])
            self.act(lambda e: e.copy(out=rec[0:64, :nq], in_=rec[64:128, :nq]), r=[rec_b], w=[rec_b])
            self.dve(lambda e: e.tensor_tensor(dst[po:po + 64, c0:c0 + nq], ps_o[:64, :nq], rec[0:64, :nq], ALU.mult), r=[pb_o, rec_b], w=[dst_b])

        for h in range(4):
            hi = 0
            q, q_b = qT[h % 2], qT_b[h % 2]
            o, o_b = oT[hi], oT_b[hi]
            if h == 2:
                for kc in range(KC):
                    self.dma("pool", wqh[:, kc, :], win[kc * 128:(kc + 1) * 128, colq + h * 64:colq + (h + 2) * 64], w=[wqh_b], nc_ok=True)
                proj_pair(qT[0], qT_b[0], qT[1], qT_b[1], wqh, wqh_b, 0.125)
            if h == 2:
                self.dma("pool", wo[:, 0, :], self.I("w_out")[l][wo_base + 128: wo_base + 256, :], w=[wo_b])
            if fox:
                self.dve(lambda e, q=q: e.memset(q[64:71, :], 1.0), w=[q_b])
                self.dma("sp", q[68:71, :], Fq[h], r=[Fk_b], w=[q_b])
            else:
                self.dve(lambda e, q=q: e.memset(q[64:65, :], 1.0), w=[q_b])
            self.dve(lambda e, q=q: e.tensor_scalar(q[64:65, :], q[64:65, :], self.flags_t[64:65, 1:2], None, ALU.mult), r=[q_b, cbuf], w=[q_b])
            vo, vo_b = vown[hi], vown_b[hi]
            self.dve(lambda e, vo=vo, h=h: e.tensor_copy(vo[:, 0:16, 0:64], vt[:, 0:16, h * 64:(h + 1) * 64]), r=[vb], w=[vo_b])
            self.dve(lambda e, vo=vo, h=h: e.tensor_copy(vo[:16, 16:18, 0:64], vt[:16, 16:18, h * 64:(h + 1) * 64]), r=[vb], w=[vo_b])
            kp, kp_b = kTp[hi], kTp_b[hi]
            self.dma("sp", kp[0:KA, :], xk_out[h * KA:(h + 1) * KA, :], r=[xk_out_b], w=[kp_b])
            if fox:
                self.dma("sp", kp[65:68, :], Fp[h], r=[Fp_b], w=[kp_b])
            vp, vp_b = vtp[hi], vtp_b[hi]
            self.dma("sp", vp[:, :, 0:64], xv_out[0:128, :].rearrange("p (b c) -> p b c", c=256)[:, :, h * 64:(h + 1) * 64], r=[xv_out_b], w=[vp_b], nc_ok=True)
            kc_, kc_b = kTc[0], kTc_b[0]
            vc, vc_b = vtc[0], vtc_b[0]

            def load_cache_k(jj):
                self.dma("pool", kst, ck[l, jj].rearrange("(b p) c -> p b c", p=128)[:, :, h * 64:(h + 1) * 64], w=[kst_b], nc_ok=True)

            def load_cache_v(jj):
                self.dma("pool", vc[:, :, 0:64], cv[l, jj].rearrange("(b p) c -> p b c", p=128)[:, :, h * 64:(h + 1) * 64], w=[vc_b], nc_ok=True)

            def build_kTc(jj):
                for g8 in range(4):
                    ps, pb = self.psum_bank()
                    pv = self.bank_bf(ps)
                    for b8 in range(8):
                        self.pe(lambda e, pv=pv, b8=b8, g8=g8: e.transpose(pv[:64, b8 * 128:(b8 + 1) * 128], kst[:, g8 * 8 + b8, :], self.ident_bf), r=[kst_b, cbuf], w=[pb])
                    self.act(lambda e, pv=pv, g8=g8: e.copy(out=kc_[0:64, g8 * 1024:(g8 + 1) * 1024], in_=pv[:64, 0:1024]), r=[pb], w=[kc_b])
                if fox:
                    self.dve(lambda e: e.memset(kc_[64:71, :], 1.0), w=[kc_b])
                    self.dve(lambda e: e.memset(kc_[64:65, :], 0.0), w=[kc_b])
                    self.dma("sp", kc_[65:68, :], self.gck[jj][0][h], r=[self.gck[jj][1]], w=[kc_b])
                else:
                    self.dve(lambda e: e.memset(kc_[64:65, :], 0.0), w=[kc_b])

            def sample_attend(jj):
                c0 = TP + 16 * jj
                ps_o, pb_o = self.acc_bank()
                if not fox:
                    self.dve(lambda e: e.memset(carry, 0.0), w=[carry_b])
                blist = [(kT[h][:, c0:c0 + 16], vo[:16, 16 + jj, :], [kT_b[h], vo_b], 16, 0, True)]
                for kb in range(31, -1, -1):
                    blist.append((kc_[:, kb * 128:(kb + 1) * 128], vc[:, kb, :], [kc_b, vc_b], 128, 0, False))
                run_tile(blist, q[:, c0:c0 + 16], q_b, 16, ps_o, pb_o)
                finish_tile(ps_o, pb_o, o, o_b, c0, 16, po=(h % 2) * 64)

            load_cache_k(0)
            load_cache_v(0)
            build_kTc(0)
            load_cache_k(1)
            for ti in range(4):
                c0 = ti * 512
                ps_o, pb_o = self.acc_bank()
                if not fox:
                    self.dve(lambda e: e.memset(carry, 0.0), w=[carry_b])
                blist = []
                for j in range(3, -1, -1):
                    kb = ti * 4 + j
                    blist.append((kT[h][:, kb * 128:(kb + 1) * 128], vo[:, kb, :], [kT_b[h], vo_b], 128, j * 128, True))
                for kb in range(ti * 4 - 1, -1, -1):
                    blist.append((kT[h][:, kb * 128:(kb + 1) * 128], vo[:, kb, :], [kT_b[h], vo_b], 128, 0, False))
                for kb in range(15, -1, -1):
                    blist.append((kp[:, kb * 128:(kb + 1) * 128], vp[:, kb, :], [kp_b, vp_b], 128, 0, False))
                run_tile(blist, q[:, c0:c0 + 512], q_b, 512, ps_o, pb_o)
                finish_tile(ps_o, pb_o, o, o_b, c0, 512, po=(h % 2) * 64)
            sample_attend(0)
            build_kTc(1)
            load_cache_v(1)
            sample_attend(1)
            for oc in range(8):
                if h % 2 == 0:
                    break
                for (c0, n) in self.ttiles():
                    ps, pb = self.psum_bank()
                    self.pe(lambda e, ps=ps, oc=oc, c0=c0, n=n, o=o, h=h: e.matmul(ps[:, :n], wo[:, 0, oc * 128:(oc + 1) * 128], o[:, c0:c0 + n], start=True, stop=True),
                            r=[wo_b, o_b], w=[pb])
                    hsl = self.hT[:, oc, c0:c0 + n]
                    self.dve(lambda e, ps=ps, hsl=hsl, n=n: e.tensor_tensor(hsl, hsl, ps[:, :n], ALU.add), r=[pb, self.hT_b], w=[self.hT_b])
        self.P.barrier()
        A.release(m0)

    def fox_gates(self, l):
        A = self.A
        uT, uT_b = self.uT, self.uT_b
        cbuf = self.B_const
        win = self.I("w_in")[l]
        m0 = A.mark()
        wg = A.alloc([128, KC, 4], BF16, "wg")
        wg_b = Buf("wg")
        self.dma("pool", wg, win[:, 1536:1540].rearrange("(c p) n -> p c n", p=128), w=[wg_b], nc_ok=True)
        bcol = A.alloc([4, 1], F32, "bcol")
        brow = A.alloc([128, 4], F32, "brow")
        b_b = Buf("foxb")
        self.dma("sp", bcol, self.I("fox_b_f")[l].rearrange("(h o) -> h o", o=1), w=[b_b], nc_ok=True)
        self.dma("sp", brow, bcast(self.I("fox_b_f")[l], 0, 128), w=[b_b])
        self.dve(lambda e: e.tensor_scalar(bcol, bcol, -1.0, None, ALU.mult), r=[b_b], w=[b_b])
        blocks = self.tblocks()
        ps_l, pb_l = self.psum_bank()
        for bi, (c0, n) in enumerate(blocks):
            for kc in range(KC):
                self.pe(lambda e, bi=bi, kc=kc, c0=c0, n=n: e.matmul(ps_l[:n, bi * 4:(bi + 1) * 4], uT[:, kc, c0:c0 + n], wg[:, kc, :],
                                                                     start=(kc == 0), stop=(kc == KC - 1)), r=[uT_b, wg_b], w=[pb_l])
        lf = A.alloc([128, 18, 4], F32, "lf")
        lf_b = Buf("lf")
        self.dve(lambda e: e.tensor_tensor(lf, ps_l[:, :72].rearrange("p (b h) -> p b h", h=4), bcast(brow, 1, 18), ALU.add), r=[pb_l, b_b], w=[lf_b])
        self.act(lambda e: e.activation(out=lf, in_=lf, func=AF.Exp, scale=-1.0), r=[lf_b], w=[lf_b])
        self.act(lambda e: e.activation(out=lf, in_=lf, func=AF.Ln, bias=1.0), r=[lf_b], w=[lf_b])
        self.dve(lambda e: e.tensor_scalar(lf, lf, -1.0, None, ALU.mult), r=[lf_b], w=[lf_b])
        for bi, (c0, n) in enumerate(blocks):
            dst = self.p_logf[l, c0:c0 + n, :] if c0 < TP else self.s_logf[l, c0 - TP:c0 - TP + n, :]
            self.dma("sp", dst, lf[:n, bi, :], r=[lf_b], nc_ok=True)
        S = A.alloc([4, T], F32, "S")
        G = A.alloc([4, T], F32, "G")
        S_b = Buf("S")
        G_b = Buf("G")
        for (c0, n) in self.ttiles():
            ps, pb = self.psum_bank()
            for kc in range(KC):
                self.pe(lambda e, ps=ps, kc=kc, c0=c0, n=n: e.matmul(ps[:4, :n], wg[:, kc, :], uT[:, kc, c0:c0 + n], start=(kc == 0), stop=(kc == KC - 1)),
                        r=[uT_b, wg_b], w=[pb])
            self.act(lambda e, ps=ps, c0=c0, n=n: e.activation(out=S[:, c0:c0 + n], in_=ps[:4, :n], func=AF.Exp, scale=-1.0, bias=bcol[:, 0:1]), r=[pb, b_b], w=[S_b])
        self.act(lambda e: e.activation(out=S, in_=S, func=AF.Ln, bias=1.0), r=[S_b], w=[S_b])
        ones4 = A.alloc([4, TP], F32, "ones4")
        self.dve(lambda e: e.memset(ones4, 1.0), w=[S_b])
        NH = 6
        his = [A.alloc([4, TP], BF16, "hi") for _ in range(NH)]
        his_b = [Buf("hi") for _ in range(NH)]
        r1s = [A.alloc([4, TP], F32, "r1") for _ in range(2)]
        r1s_b = [Buf("r1") for _ in range(2)]
        self.hi_i = 0

        def pieces(x, x_b, n, dk, dq, dk_b, dq_b):
            cur, cur_b = x, x_b
            for j in range(3):
                hi, hi_b = his[self.hi_i % NH], his_b[self.hi_i % NH]
                self.hi_i += 1
                self.dve(lambda e, cur=cur, hi=hi: e.tensor_copy(hi[:, :n], cur[:, :n]), r=[cur_b], w=[hi_b])
                if dk is not None:
                    self.dma("sp", dk[:, j, :], hi[:, :n], r=[hi_b], w=[dk_b])
                if j < 2:
                    r1, r1_b = r1s[j], r1s_b[j]
                    self.dve(lambda e, cur=cur, hi=hi, r1=r1: e.tensor_tensor(r1[:, :n], cur[:, :n], hi[:, :n], ALU.subtract), r=[cur_b, hi_b], w=[r1_b])
                if dq is not None:
                    hq, hq_b = his[self.hi_i % NH], his_b[self.hi_i % NH]
                    self.hi_i += 1
                    self.dve(lambda e, hi=hi, hq=hq: e.tensor_scalar(hq[:, :n], hi[:, :n], -1.0, None, ALU.mult), r=[hi_b], w=[hq_b])
                    self.dma("sp", dq[:, j, :], hq[:, :n], r=[hq_b], w=[dq_b])
                if j < 2:
                    cur, cur_b = r1, r1_b

        self.gck = []
        gtot = A.alloc([4, 2], F32, "gtot")
        gtot_b = Buf("gtot")
        lsts = [A.alloc([128, 8, 4], F32, "lst") for _ in range(2)]
        lsts_b = [Buf("lst") for _ in range(2)]
        Scs = [A.alloc([4, 1024], F32, "Sc") for _ in range(2)]
        Gcs = [A.alloc([4, 1024], F32, "Gc") for _ in range(2)]
        Scs_b = [Buf("Sc") for _ in range(2)]
        Gcs_b = [Buf("Gc") for _ in range(2)]
        cc_i = 0
        for jj in range(2):
            dk, dk_b = self.dram(f"gck{l}_{jj}", [4, 3, PAST], BF16)
            self.gck.append((dk, dk_b))
            for c in range(4):
                lst, lst_b = lsts[cc_i % 2], lsts_b[cc_i % 2]
                Sc, Sc_b = Scs[cc_i % 2], Scs_b[cc_i % 2]
                Gprev = Gcs[(cc_i + 1) % 2]
                Gprev_b = Gcs_b[(cc_i + 1) % 2]
                Gc, Gc_b = Gcs[cc_i % 2], Gcs_b[cc_i % 2]
                cc_i += 1
                self.dma("sp", lst, self.I("c_fox_logf")[l, jj, c * 1024:(c + 1) * 1024, :].rearrange("(b p) h -> p b h", p=128), w=[lst_b], nc_ok=True)
                for half in range(2):
                    ps, pb = self.psum_bank()
                    for b4 in range(4):
                        self.pe(lambda e, ps=ps, b4=b4, half=half, lst=lst: e.transpose(ps[:4, b4 * 128:(b4 + 1) * 128], lst[:, half * 4 + b4, :], self.ident_f),
                                r=[lst_b, cbuf], w=[pb])
                    self.act(lambda e, ps=ps, half=half, Sc=Sc: e.activation(out=Sc[:, half * 512:(half + 1) * 512], in_=ps[:4, :512], func=AF.Copy, scale=-1.0), r=[pb], w=[Sc_b])
                init = 0.0
                if c > 0:
                    self.dve(lambda e, jj=jj, Gprev=Gprev: e.tensor_copy(gtot[:, jj:jj + 1], Gprev[:, 1023:1024]), r=[Gprev_b], w=[gtot_b])
                    init = gtot[:, jj:jj + 1]
                self.dve(lambda e, init=init, Gc=Gc, Sc=Sc: e.tensor_tensor_scan(Gc, ones4[:, :1024], Sc, init, ALU.mult, ALU.add), r=[Sc_b, S_b, gtot_b], w=[Gc_b])
                pieces(Gc, Gc_b, 1024, dk[:, :, c * 1024:(c + 1) * 1024], None, dk_b, None)
            self.dve(lambda e, jj=jj, Gc=Gc: e.tensor_copy(gtot[:, jj:jj + 1], Gc[:, 1023:1024]), r=[Gc_b], w=[gtot_b])
        self.dve(lambda e: e.tensor_tensor_scan(G[:, 0:TP], ones4, S[:, 0:TP], 0.0, ALU.mult, ALU.add), r=[S_b], w=[G_b])
        for jj in range(2):
            c0 = TP + 16 * jj
            self.dve(lambda e, jj=jj, c0=c0: e.tensor_tensor_scan(G[:, c0:c0 + 16], ones4[:, :16], S[:, c0:c0 + 16], gtot[:, jj:jj + 1], ALU.mult, ALU.add),
                     r=[S_b, gtot_b], w=[G_b])
        x5in, x5in_b = self.dram(f"x5in{l}", [4, TP])
        x5out, x5out_b = self.dram(f"x5out{l}", [8, TP])
        self.dma("sp", x5in, G[:, 0:TP], r=[G_b], w=[x5in_b])
        self.allgather(x5in, x5in_b, x5out, x5out_b)
        Fk, Fk_b = self.dram(f"fk{l}", [4, 3, T], BF16)
        Fq, Fq_b = self.dram(f"fq{l}", [4, 3, T], BF16)
        pieces(G, G_b, TP, Fk[:, :, 0:TP], Fq[:, :, 0:TP], Fk_b, Fk_b)
        Gs = A.alloc([4, TS], F32, "Gs")
        Gs_b = Buf("Gs")
        self.dve(lambda e: e.tensor_copy(Gs, G[:, TP:T]), r=[G_b], w=[Gs_b])
        pieces(Gs, Gs_b, TS, Fk[:, :, TP:T], Fq[:, :, TP:T], Fk_b, Fk_b)
        Gp = A.alloc([4, TP], F32, "Gp")
        Gp_b = Buf("Gp")
        self.dma("sp", Gp, x5out[0:4, :], r=[x5out_b], w=[Gp_b])
        gpt = A.alloc([4, 1], F32, "gpt")
        self.dve(lambda e: e.tensor_copy(gpt, Gp[:, TP - 1:TP]), r=[Gp_b], w=[gtot_b])
        self.dve(lambda e: e.tensor_scalar(Gp, Gp, gpt[:, 0:1], None, ALU.subtract), r=[Gp_b, gtot_b], w=[Gp_b])
        Fp, Fp_b = self.dram(f"fp{l}", [4, 3, TP], BF16)
        pieces(Gp, Gp_b, TP, Fp, None, Fp_b, None)
        self.P.barrier()
        A.release(m0)
        return Fk, Fq, Fk_b, Fp, Fp_b

    def linear_fm(self, xT, x_b, wsrc, ncols, handler, col0=0):
        npan = (ncols + 511) // 512
        for pn in range(npan):
            w = min(512, ncols - pn * 512)
            wt, wb = self.load_w_panel(wsrc, col0 + pn * 512, w)
            for j in range(w // 128):
                oc = pn * 4 + j
                for (c0, n) in self.ttiles():
                    ps, pb = self.psum_bank()
                    for kc in range(KC):
                        self.pe(lambda e, ps=ps, kc=kc, c0=c0, n=n, wt=wt, j=j: e.matmul(
                            ps[:, :n], wt[:, kc, j * 128:(j + 1) * 128], xT[:, kc, c0:c0 + n],
                            start=(kc == 0), stop=(kc == KC - 1)), r=[x_b, wb], w=[pb])
                    handler(oc, ps, pb, c0, n)

    def add_to_h(self, oc, ps, pb, c0, n):
        hsl = self.hT[:, oc, c0:c0 + n]
        self.dve(lambda e: e.tensor_tensor(hsl, hsl, ps[:, :n], ALU.add), r=[pb, self.hT_b], w=[self.hT_b])

    def xattn_phase(self, l):
        A = self.A
        cbuf = self.B_const
        m0 = A.mark()
        xn = A.alloc([128, KC, T], BF16, "xn")
        xn_b = Buf("xn")
        for (c0, n) in self.ttiles():
            self.rmsnorm_fm(self.hT, self.hT_b, xn, xn_b, self.g_x[l], c0, n)
        qx = A.alloc([128, KC, T], BF16, "qx")
        qx_b = Buf("qx")
        self.alloc_wpan(2)
        self.alloc_stage()
        kmT = [A.alloc([128, KC, MEM], BF16, "kmT") for _ in range(3)]
        vm = [A.alloc([128, 2, DM], BF16, "vm") for _ in range(3)]
        km_b = [Buf("kmT") for _ in range(3)]
        vm_b = [Buf("vm") for _ in range(3)]
        m1 = A.mark()
        kst = A.alloc([128, 2, DM], BF16, "kmst")
        kst_b = Buf("kmst")
        memT = A.alloc([128, KC, MEM], F32, "memT")
        memT_b = Buf("memT")
        memn = A.alloc([128, KC, MEM], BF16, "memn")
        memn_b = Buf("memn")
        self.load_transposed(self.I("memp"), MEM, memT, memT_b, 0)
        self.rmsnorm_fm(memT, memT_b, memn, memn_b, self.g_mem[l], 0, MEM)
        for pn in range(2):
            wt, wb = self.load_w_panel(self.I("wk_x")[l], pn * 512, 512)
            for j in range(4):
                oc = pn * 4 + j
                ps, pb = self.psum_bank()
                for kc in range(KC):
                    self.pe(lambda e, ps=ps, kc=kc, wt=wt, j=j: e.matmul(ps[:, :MEM], wt[:, kc, j * 128:(j + 1) * 128], memn[:, kc, :],
                                                                         start=(kc == 0), stop=(kc == KC - 1)), r=[memn_b, wb], w=[pb])
                self.act(lambda e, ps=ps, oc=oc: e.copy(out=kmT[0][:, oc, :], in_=ps[:, :MEM]), r=[pb], w=[km_b[0]])
            for mb in range(2):
                ps, pb = self.psum_bank()
                for kc in range(KC):
                    self.pe(lambda e, ps=ps, kc=kc, wt=wt, mb=mb: e.matmul(ps[:, :], memn[:, kc, mb * 128:(mb + 1) * 128], wt[:, kc, :],
                                                                           start=(kc == 0), stop=(kc == KC - 1)), r=[memn_b, wb], w=[pb])
                st, sb = self.stage_f[self.stg_i % 2], self.stage_fb[self.stg_i % 2]
                self.stg_i += 1
                self.act(lambda e, st=st, ps=ps: e.copy(out=st[:, 0:512], in_=ps[:, :]), r=[pb], w=[sb])
                self.dma("sp", self.p_mem_k[l, mb * 128:(mb + 1) * 128, pn * 512:(pn + 1) * 512], st[:, 0:512], r=[sb])
        for pn in range(2):
            wt, wb = self.load_w_panel(self.I("wv_x")[l], pn * 512, 512)
            for mb in range(2):
                ps, pb = self.psum_bank()
                for kc in range(KC):
                    self.pe(lambda e, ps=ps, kc=kc, wt=wt, mb=mb: e.matmul(ps[:, :], memn[:, kc, mb * 128:(mb + 1) * 128], wt[:, kc, :],
                                                                           start=(kc == 0), stop=(kc == KC - 1)), r=[memn_b, wb], w=[pb])
                st, sb = self.stage_f[self.stg_i % 2], self.stage_fb[self.stg_i % 2]
                self.stg_i += 1
                self.act(lambda e, st=st, ps=ps: e.copy(out=st[:, 0:512], in_=ps[:, :]), r=[pb], w=[sb])
                self.dve(lambda e, ps=ps, mb=mb, pn=pn: e.tensor_copy(vm[0][:, mb, pn * 512:(pn + 1) * 512], ps[:, :]), r=[pb], w=[vm_b[0]])
                self.dma("sp", self.p_mem_v[l, mb * 128:(mb + 1) * 128, pn * 512:(pn + 1) * 512], st[:, 0:512], r=[sb])
        for jj in range(2):
            self.dma("pool", kst, self.I("c_mem_k")[l, jj].rearrange("(b p) c -> p b c", p=128), w=[kst_b])
            self.dma("pool", vm[1 + jj], self.I("c_mem_v")[l, jj].rearrange("(b p) c -> p b c", p=128), w=[vm_b[1 + jj]])
            for mb in range(2):
                ps, pb = self.psum_bank()
                pv = self.bank_bf(ps)
                for oc in range(8):
                    self.pe(lambda e, pv=pv, oc=oc, mb=mb: e.transpose(pv[:, oc * 128:(oc + 1) * 128], kst[:, mb, oc * 128:(oc + 1) * 128], self.ident_bf),
                            r=[kst_b, cbuf], w=[pb])
                self.act(lambda e, pv=pv, mb=mb, jj=jj: e.copy(out=kmT[1 + jj][:, :, mb * 128:(mb + 1) * 128], in_=pv[:, 0:1024].rearrange("p (c m) -> p c m", c=8)),
                         r=[pb], w=[km_b[1 + jj]])
        self.P.barrier()
        A.release(m1)
        def q_evac(oc, ps, pb, c0, n):
            self.act(lambda e: e.activation(out=qx[:, oc, c0:c0 + n], in_=ps[:, :n], func=AF.Copy, scale=1.0 / 16.0), r=[pb], w=[qx_b])
        self.linear_fm(xn, xn_b, self.I("wq_x")[l], DM, q_evac)
        pbf = [A.alloc([128, 2, 512], BF16, "pbf") for _ in range(2)]
        p_b = [Buf("pbf") for _ in range(2)]
        rec = A.alloc([128, 512], F32, "xrec")
        rec_b = Buf("xrec")
        groups = [(ti * 512, 512, 0) for ti in range(4)] + [(TP, 16, 1), (TP + 16, 16, 2)]
        gi = 0
        for (c0, nq, src) in groups:
            for h in range(4):
                pt, ptb = pbf[gi % 2], p_b[gi % 2]
                gi += 1
                for mb in range(2):
                    ps, pb = self.psum_bank()
                    for dc in range(2):
                        self.pe(lambda e, ps=ps, dc=dc, mb=mb, h=h, c0=c0, nq=nq, src=src: e.matmul(
                            ps[:, :nq], kmT[src][:, h * 2 + dc, mb * 128:(mb + 1) * 128], qx[:, h * 2 + dc, c0:c0 + nq],
                            start=(dc == 0), stop=(dc == 1)), r=[km_b[src], qx_b], w=[pb])
                    self.act(lambda e, ps=ps, pt=pt, mb=mb, nq=nq: e.activation(out=pt[:, mb, :nq], in_=ps[:, :nq], func=AF.Exp), r=[pb], w=[ptb])
                ps_d, pb_d = self.psum_bank()
                for mb in range(2):
                    self.pe(lambda e, ps_d=ps_d, pt=pt, mb=mb, nq=nq: e.matmul(ps_d[:, :nq], self.ones_bf, pt[:, mb, :nq], start=(mb == 0), stop=(mb == 1)),
                            r=[ptb, cbuf], w=[pb_d])
                self.dve(lambda e, ps_d=ps_d, nq=nq: e.reciprocal(rec[:, :nq], ps_d[:, :nq]), r=[pb_d], w=[rec_b])
                for dc in range(2):
                    ps_o, pb_o = self.psum_bank()
                    for mb in range(2):
                        self.pe(lambda e, ps_o=ps_o, pt=pt, mb=mb, dc=dc, h=h, nq=nq, src=src: e.matmul(
                            ps_o[:, :nq], vm[src][:, mb, h * 256 + dc * 128:h * 256 + (dc + 1) * 128], pt[:, mb, :nq],
                            start=(mb == 0), stop=(mb == 1)), r=[ptb, vm_b[src]], w=[pb_o])
                    self.dve(lambda e, ps_o=ps_o, dc=dc, h=h, c0=c0, nq=nq: e.tensor_tensor(xn[:, h * 2 + dc, c0:c0 + nq], ps_o[:, :nq], rec[:, :nq], ALU.mult),
                             r=[pb_o, rec_b], w=[xn_b])
        self.linear_fm(xn, xn_b, self.I("wo_x")[l], DM, self.add_to_h)
        self.P.barrier()
        A.release(m0)

    def ffn_phase(self, l):
        A = self.A
        m0 = A.mark()
        xn = A.alloc([128, KC, T], BF16, "fx")
        xn_b = Buf("fx")
        for (c0, n) in self.ttiles():
            self.rmsnorm_fm(self.hT, self.hT_b, xn, xn_b, self.g_ffn[l], c0, n)
        NFH = 12
        hid = A.alloc([128, NFH, T], BF16, "hid")
        hid_b = Buf("hid")
        self.alloc_wpan(4)
        sg = [A.alloc([128, 512], F32, "sg") for _ in range(2)]
        sg_b = [Buf("sg") for _ in range(2)]
        wd = [A.alloc([128, NFH, 128], BF16, "wd") for _ in range(2)]
        wd_b = [Buf("wd") for _ in range(2)]
        si = 0
        wi = 0
        for (f0, nf) in ((0, 12), (12, 10)):
            for pn in range((nf + 3) // 4):
                fw = min(4, nf - pn * 4)
                wg_t, wg_b = self.load_w_panel(self.I("w_gate")[l], (f0 + pn * 4) * 128, fw * 128)
                wu_t, wu_b = self.load_w_panel(self.I("w_up")[l], (f0 + pn * 4) * 128, fw * 128)
                for j in range(fw):
                    fi = pn * 4 + j
                    for (c0, n) in self.ttiles():
                        ps_g, pb_g = self.psum_bank()
                        ps_u, pb_u = self.psum_bank()
                        for kc in range(KC):
                            self.pe(lambda e, ps_g=ps_g, kc=kc, c0=c0, n=n, wg_t=wg_t, j=j: e.matmul(
                                ps_g[:, :n], wg_t[:, kc, j * 128:(j + 1) * 128], xn[:, kc, c0:c0 + n], start=(kc == 0), stop=(kc == KC - 1)),
                                r=[xn_b, wg_b], w=[pb_g])
                        for kc in range(KC):
                            self.pe(lambda e, ps_u=ps_u, kc=kc, c0=c0, n=n, wu_t=wu_t, j=j: e.matmul(
                                ps_u[:, :n], wu_t[:, kc, j * 128:(j + 1) * 128], xn[:, kc, c0:c0 + n], start=(kc == 0), stop=(kc == KC - 1)),
                                r=[xn_b, wu_b], w=[pb_u])
                        s_, s_b = sg[si % 2], sg_b[si % 2]
                        si += 1
                        self.act(lambda e, s_=s_, ps_g=ps_g, n=n: e.activation(out=s_[:, :n], in_=ps_g[:, :n], func=AF.Silu), r=[pb_g], w=[s_b])
                        self.dve(lambda e, s_=s_, ps_u=ps_u, fi=fi, c0=c0, n=n: e.tensor_tensor(hid[:, fi, c0:c0 + n], s_[:, :n], ps_u[:, :n], ALU.mult),
                                 r=[s_b, pb_u], w=[hid_b])
            for oc in range(8):
                w_, w_b = wd[wi % 2], wd_b[wi % 2]
                wi += 1
                self.dma("pool", w_[:, :nf, :], self.I("w_down")[l][f0 * 128:(f0 + nf) * 128, oc * 128:(oc + 1) * 128].rearrange("(f p) c -> p f c", p=128),
                         w=[w_b], nc_ok=True)
                for (c0, n) in self.ttiles():
                    ps, pb = self.psum_bank()
                    for fi in range(nf):
                        self.pe(lambda e, ps=ps, fi=fi, c0=c0, n=n, w_=w_, nf=nf: e.matmul(ps[:, :n], w_[:, fi, :], hid[:, fi, c0:c0 + n],
                                                                                      start=(fi == 0), stop=(fi == nf - 1)), r=[hid_b, w_b], w=[pb])
                    self.add_to_h(oc, ps, pb, c0, n)
        self.P.barrier()
        A.release(m0)

    def final_phase(self):
        A = self.A
        cbuf = self.B_const
        m0 = A.mark()
        self.alloc_stage()
        yT = [A.alloc([128, KC, 128], F32, "yT") for _ in range(2)]
        yT_b = [Buf("yT") for _ in range(2)]
        for bi, (c0, n) in enumerate(self.tblocks()):
            y, y_b = yT[bi % 2], yT_b[bi % 2]
            self.rmsnorm_fm(self.hT, self.hT_b, y, y_b, self.g_fin, c0, n, dcol=0)
            st, sb = self.stage_f[self.stg_i % 2], self.stage_fb[self.stg_i % 2]
            self.stg_i += 1
            for half in range(2):
                ps, pb = self.psum_bank()
                for j in range(4):
                    kc = half * 4 + j
                    self.pe(lambda e, ps=ps, j=j, kc=kc, n=n, y=y: e.transpose(ps[:n, j * 128:(j + 1) * 128], y[:, kc, :n], self.ident_f), r=[y_b, cbuf], w=[pb])
                if half == 0:
                    self.act(lambda e, ps=ps, st=st, n=n: e.copy(out=st[:n, 0:512], in_=ps[:n, :]), r=[pb], w=[sb])
                else:
                    self.dve(lambda e, ps=ps, st=st, n=n: e.tensor_copy(st[:n, 512:1024], ps[:n, :]), r=[pb], w=[sb])
            dst = self.yp[c0:c0 + n, :] if c0 < TP else self.ys[c0 - TP:c0 - TP + n, :]
            self.dma("sp", dst, st[:n, :], r=[sb])
        self.P.barrier()
        A.release(m0)


def prep_inputs(inp):
    f = lambda a: np.ascontiguousarray(a, dtype=np.float32)
    maps = []
    for c in range(NCORES):
        b, r = c // 2, c % 2
        sb = slice(2 * c, 2 * c + 2)
        m = {
            "xp": f(inp["x_prompt"][b, r * TP:(r + 1) * TP]),
            "xs": f(inp["x_sample"][sb].reshape(TS, DM)),
            "c_sb_k": f(inp["cache_sb_k"][:, sb].reshape(2, 2, PAST, 256)),
            "c_sb_v": f(inp["cache_sb_v"][:, sb].reshape(2, 2, PAST, 256)),
            "c_fox_k": f(inp["cache_fox_k"][:, sb].reshape(2, 2, PAST, 256)),
            "c_fox_v": f(inp["cache_fox_v"][:, sb].reshape(2, 2, PAST, 256)),
            "c_fox_logf": f(inp["cache_fox_logf"][:, sb]),
            "st_ssm": f(inp["state_ssm"][:, sb].reshape(2, 2, 512, 128)),
            "st_conv": f(inp["state_conv"][:, sb]),
            "c_mem_k": f(inp["cache_mem_k"][:, sb].reshape(2, 2, MEM, 1024)),
            "c_mem_v": f(inp["cache_mem_v"][:, sb].reshape(2, 2, MEM, 1024)),
            "memp": f(inp["mem_prompt"][b]),
        }
        fl = np.zeros((128, 8), np.float32)
        fl[:, 0] = float(r)
        fl[:, 1] = 0.0 if r == 1 else NEG
        m["flags"] = fl
        for k in ("norm_mix_g", "w_in", "fox_b_f", "conv_w", "conv_b", "dt_bias", "a_log", "d_skip", "ssd_norm_g",
                  "w_out", "norm_x_g", "mem_norm_g", "wq_x", "wk_x", "wv_x", "wo_x", "norm_ffn_g", "w_gate",
                  "w_up", "w_down"):
            m[k] = f(inp[k])
        m["final_norm_g"] = f(inp["final_norm_g"]).reshape(1, DM)
        maps.append(m)
    return maps


def assemble(res):
    R = res
    g = lambda c, k: np.asarray(R[c][k], dtype=np.float32)
    y_prompt = np.stack([np.concatenate([g(2 * b, "yp"), g(2 * b + 1, "yp")], 0) for b in range(4)], 0)
    y_sample = np.concatenate([g(c, "ys").reshape(2, 16, DM) for c in range(NCORES)], 0)
    outs = [y_prompt, y_sample]
    for nm in ("sb_k", "sb_v", "fox_k", "fox_v"):
        a = np.stack([np.concatenate([g(2 * b, "p_" + nm), g(2 * b + 1, "p_" + nm)], 1) for b in range(4)], 1)
        outs.append(a.reshape(2, 4, 4096, 4, 64))
    a = np.stack([np.concatenate([g(2 * b, "p_fox_logf"), g(2 * b + 1, "p_fox_logf")], 1) for b in range(4)], 1)
    outs.append(a.reshape(2, 4, 4096, 4))
    outs.append(np.stack([g(2 * b + 1, "p_ssm") for b in range(4)], 1).reshape(2, 4, 8, 64, 128))
    outs.append(np.stack([g(2 * b + 1, "p_conv") for b in range(4)], 1).reshape(2, 4, 3, 1024))
    outs.append(np.stack([g(2 * b, "p_mem_k") for b in range(4)], 1).reshape(2, 4, MEM, 4, 256))
    outs.append(np.stack([g(2 * b, "p_mem_v") for b in range(4)], 1).reshape(2, 4, MEM, 4, 256))
    for nm in ("sb_k", "sb_v", "fox_k", "fox_v"):
        a = np.concatenate([g(c, "s_" + nm).reshape(2, 2, 16, 256) for c in range(NCORES)], 1)
        outs.append(a.reshape(2, 16, 16, 4, 64))
    a = np.concatenate([g(c, "s_fox_logf").reshape(2, 2, 16, 4) for c in range(NCORES)], 1)
    outs.append(a)
    outs.append(np.concatenate([g(c, "s_ssm") for c in range(NCORES)], 1).reshape(2, 16, 8, 64, 128))
    outs.append(np.concatenate([g(c, "s_conv") for c in range(NCORES)], 1).reshape(2, 16, 3, 1024))
    return tuple(outs)


_CACHE = {}


def kernel(**inputs):
    stage = inputs.pop("_stage", 99)
    if stage not in _CACHE:
        _CACHE[stage] = K(stage)
    k = _CACHE[stage]
    maps = [{n: m[n] for n in k.ins} for m in prep_inputs(inputs)]
    res = run_bass_kernel_spmd(k.nc, maps, core_ids=list(range(NCORES)))
    return assemble(res.results)
```

```python
import numpy as np
from contextlib import ExitStack
import concourse.bass as bass
import concourse.mybir as mybir
from concourse.bass_utils import run_bass_kernel_spmd

F32 = mybir.dt.float32
BF16 = mybir.dt.bfloat16
AF = mybir.ActivationFunctionType
ALU = mybir.AluOpType

NCORES = 8
DM = 1024
KC = 8
TP = 2048
TS = 32
T = TP + TS
NIN = 3084
DFF = 2816
PAST = 4096
MEM = 256
EPS = 1e-6
NEG = -30000.0
SEM_CH = 16000
DMA_R = 8


class Buf:
    __slots__ = ("name", "w", "r", "excl")

    def __init__(self, name="", excl=False):
        self.name = name
        self.w = None
        self.r = []
        self.excl = excl


class Op:
    __slots__ = ("eng", "fn", "waits", "inc", "pos", "dma_n", "q")

    def __init__(self, eng, fn):
        self.eng = eng
        self.fn = fn
        self.waits = []
        self.inc = False
        self.pos = -1
        self.dma_n = -1


ENGS = ("pe", "act", "dve", "pool", "sp")
DMAQ = ("sp", "pool", "act", "cc")
QINC = {"sp": 16, "pool": 16, "act": 16, "cc": 1}


class Prog:
    def __init__(self, nc):
        self.nc = nc
        self.ops = {e: [] for e in ENGS}
        self.seen = {e: {} for e in ENGS}
        self.dmas = {q: [] for q in DMAQ}
        self.pending = {e: [] for e in ENGS}

    def _need(self, op, tgt, raw):
        if tgt is None:
            return
        if tgt[0] == "c":
            _, te, pos = tgt
            if te == op.eng and te == "pe":
                return
            key = ("c", te)
            if self.seen[op.eng].get(key, -1) >= pos:
                return
            self.seen[op.eng][key] = pos
            self.ops[te][pos].inc = True
            op.waits.append(tgt)
        else:
            _, q, n = tgt
            key = ("d", q, n % DMA_R)
            if self.seen[op.eng].get(key, -1) >= n:
                return
            self.seen[op.eng][key] = n
            op.waits.append(tgt)

    def add(self, eng, fn, reads=(), writes=(), dma=False, q=None):
        op = Op(eng, fn)
        op.pos = len(self.ops[eng])
        for t in self.pending[eng]:
            self._need(op, t, True)
        self.pending[eng] = []
        if dma:
            q = q or eng
            op.q = q
            n = len(self.dmas[q])
            op.dma_n = n
            if n >= DMA_R:
                self._need(op, ("d", q, n - DMA_R), True)
            me = ("d", q, n)
            self.dmas[q].append(op)
        else:
            me = ("c", eng, op.pos)
        for b in reads:
            self._need(op, b.w, True)
            if b.excl:
                for t in b.r:
                    self._need(op, t, False)
        for b in writes:
            self._need(op, b.w, False)
            for t in b.r:
                self._need(op, t, False)
        for b in reads:
            b.r.append(me)
        for b in writes:
            b.w = me
            b.r = []
        self.ops[eng].append(op)
        return op

    def barrier(self):
        tg = []
        for e in ENGS:
            for op in reversed(self.ops[e]):
                if op.dma_n < 0:
                    tg.append(("c", e, op.pos))
                    break
        for q in DMAQ:
            n = len(self.dmas[q])
            for k in range(max(0, n - DMA_R), n):
                tg.append(("d", q, k))
        for e in ENGS:
            self.pending[e] = self.pending[e] + tg

    def finish(self):
        self.barrier()
        for e in ENGS:
            self.add(e, lambda eng: eng.nop())

    def emit(self):
        nc = self.nc
        with ExitStack() as st:
            csem = {}
            cidx = {}
            for e in ENGS:
                k = 0
                for op in self.ops[e]:
                    if op.inc:
                        cidx[(e, op.pos)] = k
                        k += 1
                nch = (k + SEM_CH - 1) // SEM_CH
                csem[e] = [st.enter_context(nc.semaphore(f"c_{e}_{i}")) for i in range(max(1, nch))]
            dsem = {q: [st.enter_context(nc.semaphore(f"d_{q}_{i}")) for i in range(DMA_R)] for q in DMAQ}
            block = st.enter_context(nc.Block())

            def replay(e, eng):
                for op in self.ops[e]:
                    for t in op.waits:
                        if t[0] == "c":
                            k = cidx[(t[1], t[2])]
                            eng.wait_ge(csem[t[1]][k // SEM_CH], k % SEM_CH + 1)
                        else:
                            _, q, n = t
                            eng.wait_ge(dsem[q][n % DMA_R], QINC[q] * (n // DMA_R + 1))
                    ins = op.fn(eng)
                    if op.dma_n >= 0:
                        ins.then_inc(dsem[op.q][op.dma_n % DMA_R], QINC[op.q])
                    elif op.inc:
                        k = cidx[(e, op.pos)]
                        ins.then_inc(csem[e][k // SEM_CH], 1)

            @block.tensor
            def _(eng):
                replay("pe", eng)

            @block.scalar
            def _(eng):
                replay("act", eng)

            @block.vector
            def _(eng):
                replay("dve", eng)

            @block.gpsimd
            def _(eng):
                replay("pool", eng)

            @block.sync
            def _(eng):
                replay("sp", eng)


def bcast(ap, pos, n):
    dims = [list(d) for d in ap.ap]
    dims.insert(pos, [0, n])
    return bass.AP(ap.tensor, ap.offset, dims)


class Arena:
    def __init__(self, nc, base=16512, limit=229376):
        self.nc = nc
        self.off = base
        self.limit = limit
        self.n = 0

    def alloc(self, shape, dtype, name="t"):
        esz = 4 if dtype == F32 else 2
        per = esz
        for s in shape[1:]:
            per *= s
        off = (self.off + 31) // 32 * 32
        assert off + per <= self.limit, f"SBUF overflow {name} {off}+{per}"
        self.n += 1
        h = self.nc.alloc_sbuf_tensor_at(f"{name}{self.n}", list(shape), dtype, offset=off)
        self.off = off + per
        return h.ap()

    def mark(self):
        return self.off

    def release(self, m):
        self.off = m


class K:
    def __init__(self, stage=99, ncores=NCORES):
        self.stage = stage
        self.ncores = ncores
        nc = bass.Bass("TRN2", target_bir_lowering=False)
        self.nc = nc
        self.P = Prog(nc)
        self.A = Arena(nc)
        self.ins = {}
        self.outs = {}
        self.build()

    def din(self, name, shape):
        t = self.nc.dram_tensor(name, list(shape), F32, kind="ExternalInput").ap()
        self.ins[name] = t
        return t

    def I(self, name):
        if name not in self.ins:
            self.din(name, self.in_shapes[name])
        return self.ins[name]

    def dout(self, name, shape):
        t = self.nc.dram_tensor(name, list(shape), F32, kind="ExternalOutput").ap()
        self.outs[name] = t
        return t

    def pe(self, fn, r=(), w=()):
        return self.P.add("pe", fn, r, w)

    def act(self, fn, r=(), w=()):
        return self.P.add("act", fn, r, w)

    def dve(self, fn, r=(), w=()):
        return self.P.add("dve", fn, r, w)

    def pool(self, fn, r=(), w=()):
        return self.P.add("pool", fn, r, w)

    def dma(self, q, out, in_, r=(), w=(), nc_ok=False):
        if nc_ok:
            def fn(e):
                with self.nc.allow_non_contiguous_dma(reason="small strided transfer"):
                    return e.dma_start(out=out, in_=in_)
        else:
            def fn(e):
                return e.dma_start(out=out, in_=in_)
        return self.P.add(q, fn, r, w, dma=True)

    def psum_bank(self):
        i = self.ps_i % 6
        self.ps_i += 1
        return self.ps[i], self.psb[i]

    def acc_bank(self):
        i = 6 + self.pa_i % 2
        self.pa_i += 1
        return self.ps[i], self.psb[i]

    def declare_io(self):
        self.in_shapes = {
            "xp": [TP, DM],
            "xs": [TS, DM],
            "c_sb_k": [2, 2, PAST, 256],
            "c_sb_v": [2, 2, PAST, 256],
            "c_fox_k": [2, 2, PAST, 256],
            "c_fox_v": [2, 2, PAST, 256],
            "c_fox_logf": [2, 2, PAST, 4],
            "st_ssm": [2, 2, 512, 128],
            "st_conv": [2, 2, 3, 1024],
            "c_mem_k": [2, 2, MEM, 1024],
            "c_mem_v": [2, 2, MEM, 1024],
            "memp": [MEM, DM],
            "flags": [128, 8],
            "norm_mix_g": [2, DM],
            "w_in": [2, DM, NIN],
            "fox_b_f": [2, 4],
            "conv_w": [2, 4, 1024],
            "conv_b": [2, 1024],
            "dt_bias": [2, 8],
            "a_log": [2, 8],
            "d_skip": [2, 8],
            "ssd_norm_g": [2, 512],
            "w_out": [2, DM, DM],
            "norm_x_g": [2, DM],
            "mem_norm_g": [2, DM],
            "wq_x": [2, DM, DM],
            "wk_x": [2, DM, DM],
            "wv_x": [2, DM, DM],
            "wo_x": [2, DM, DM],
            "norm_ffn_g": [2, DM],
            "w_gate": [2, DM, DFF],
            "w_up": [2, DM, DFF],
            "w_down": [2, DFF, DM],
            "final_norm_g": [1, DM],
        }
        o = self.dout
        self.yp = o("yp", [TP, DM])
        self.ys = o("ys", [TS, DM])
        self.o_kv = {}
        for nm in ("sb_k", "sb_v", "fox_k", "fox_v"):
            self.o_kv["p_" + nm] = o("p_" + nm, [2, TP, 256])
            self.o_kv["s_" + nm] = o("s_" + nm, [2, TS, 256])
        self.p_logf = o("p_fox_logf", [2, TP, 4])
        self.s_logf = o("s_fox_logf", [2, TS, 4])
        self.p_ssm = o("p_ssm", [2, 512, 128])
        self.s_ssm = o("s_ssm", [2, 2, 512, 128])
        self.p_conv = o("p_conv", [2, 3, 1024])
        self.s_conv = o("s_conv", [2, 2, 3, 1024])
        self.p_mem_k = o("p_mem_k", [2, MEM, 1024])
        self.p_mem_v = o("p_mem_v", [2, MEM, 1024])

    def consts(self):
        A = self.A
        nc = self.nc
        self.ones_bf = A.alloc([128, 128], BF16, "ones_bf")
        self.ones_f = A.alloc([128, 128], F32, "ones_f")
        self.ident_bf = A.alloc([128, 128], BF16, "ident_bf")
        self.ident_f = A.alloc([128, 128], F32, "ident_f")
        self.B_const = Buf("const")
        cb = self.B_const
        self.pool(lambda e: e.memset(self.ones_bf, 1.0), w=[cb])
        self.pool(lambda e: e.memset(self.ones_f, 1.0), w=[cb])
        self.pool(lambda e: e.affine_select(self.ident_f, self.ones_f, [[-1, 128]], ALU.is_equal, 0.0,
                                            base=0, channel_multiplier=1), r=[cb], w=[cb])
        self.pool(lambda e: e.affine_select(self.ident_bf, self.ones_bf, [[-1, 128]], ALU.is_equal, 0.0,
                                            base=0, channel_multiplier=1), r=[cb], w=[cb])
        self.U_f = A.alloc([128, 128], F32, "U_f")
        self.U_bf = A.alloc([128, 128], BF16, "U_bf")
        self.SL_f = A.alloc([128, 128], F32, "SL_f")
        self.pool(lambda e: e.affine_select(self.U_f, self.ones_f, [[1, 128]], ALU.is_ge, 0.0,
                                            base=0, channel_multiplier=-1), r=[cb], w=[cb])
        self.pool(lambda e: e.affine_select(self.U_bf, self.ones_bf, [[1, 128]], ALU.is_ge, 0.0,
                                            base=0, channel_multiplier=-1), r=[cb], w=[cb])
        self.pool(lambda e: e.affine_select(self.SL_f, self.ones_f, [[-1, 128]], ALU.is_gt, 0.0,
                                            base=0, channel_multiplier=1), r=[cb], w=[cb])
        self.negs_bf = A.alloc([128, 128], BF16, "negs")
        self.MSK_s = A.alloc([128, 128], BF16, "MSK_s")
        self.MSK_i = A.alloc([128, 128], BF16, "MSK_i")
        self.NLI_bf = A.alloc([128, 128], BF16, "NLI")
        self.mones_bf = A.alloc([128, 128], BF16, "mones")
        self.pool(lambda e: e.memset(self.negs_bf, NEG), w=[cb])
        self.pool(lambda e: e.memset(self.mones_bf, -1.0), w=[cb])
        self.pool(lambda e: e.affine_select(self.MSK_s, self.negs_bf, [[-1, 128]], ALU.is_ge, 0.0, base=0, channel_multiplier=1), r=[cb], w=[cb])
        self.pool(lambda e: e.affine_select(self.MSK_i, self.negs_bf, [[-1, 128]], ALU.is_gt, 0.0, base=0, channel_multiplier=1), r=[cb], w=[cb])
        self.pool(lambda e: e.affine_select(self.NLI_bf, self.mones_bf, [[-1, 128]], ALU.is_ge, 0.0, base=0, channel_multiplier=1), r=[cb], w=[cb])
        self.flags_t = A.alloc([128, 8], F32, "flags")
        self.dma("sp", self.flags_t, self.I("flags"), w=[cb])
        def colvec(src_row, n, name):
            t = A.alloc([128, n], F32, name)
            self.dma("sp", t, src_row.rearrange("(c p) -> p c", p=128), w=[cb], nc_ok=True)
            return t
        self.g_mix = [colvec(self.I("norm_mix_g")[l], KC, "g_mix") for l in range(2)]
        self.g_x = [colvec(self.I("norm_x_g")[l], KC, "g_x") for l in range(2)]
        self.g_ffn = [colvec(self.I("norm_ffn_g")[l], KC, "g_ffn") for l in range(2)]
        self.g_mem = [colvec(self.I("mem_norm_g")[l], KC, "g_mem") for l in range(2)]
        self.g_fin = colvec(self.I("final_norm_g")[0], KC, "g_fin")

    def load_transposed(self, src, ntok, dst, dst_buf, col0):
        nblk = (ntok + 127) // 128
        for tb in range(nblk):
            n = min(128, ntok - tb * 128)
            st, sb = self.stage_f[self.stg_i % 2], self.stage_fb[self.stg_i % 2]
            self.stg_i += 1
            self.dma("sp", st[:n, :], src[tb * 128: tb * 128 + n, :], w=[sb])
            for half in range(2):
                ps, pb = self.psum_bank()
                for j in range(4):
                    kc = half * 4 + j
                    self.pe(lambda e, ps=ps, j=j, st=st, kc=kc, n=n: e.transpose(
                        ps[:, j * 128: j * 128 + n], st[:n, kc * 128:(kc + 1) * 128], self.ident_f[:n, :n]),
                        r=[sb, self.B_const], w=[pb])
                c0 = col0 + tb * 128
                o = dst[:, half * 4: half * 4 + 4, c0: c0 + n]
                i = ps.rearrange("p (j t) -> p j t", j=4)[:, :, :n]
                if half == 0:
                    self.act(lambda e, o=o, i=i: e.copy(out=o, in_=i), r=[pb], w=[dst_buf])
                else:
                    self.dve(lambda e, o=o, i=i: e.tensor_copy(o, i), r=[pb], w=[dst_buf])

    def rmsnorm_fm(self, src, src_buf, dst, dst_buf, g, c0, n, dcol=None):
        d0 = c0 if dcol is None else dcol
        sq, sqb = self.sqt, self.sqt_b
        ps, pb = self.psum_bank()
        for kc in range(KC):
            s2, s2b = sq[kc % 2], sqb[kc % 2]
            self.act(lambda e, kc=kc, s2=s2: e.activation(out=s2[:, :n], in_=src[:, kc, c0:c0 + n], func=AF.Square),
                     r=[src_buf], w=[s2b])
            self.pe(lambda e, kc=kc, s2=s2: e.matmul(ps[:, :n], self.ones_bf, s2[:, :n], start=(kc == 0), stop=(kc == KC - 1)),
                    r=[s2b, self.B_const], w=[pb])
        rs, rsb = self.rstd, self.rstd_b
        self.act(lambda e: e.activation(out=rs[:, :n], in_=ps[:, :n], func=AF.Sqrt, bias=self.eps_t[:, 0:1], scale=1.0 / DM),
                 r=[pb, self.B_const], w=[rsb])
        self.dve(lambda e: e.reciprocal(rs[:, :n], rs[:, :n]), r=[rsb], w=[rsb])
        for kc in range(KC):
            self.dve(lambda e, kc=kc: e.scalar_tensor_tensor(dst[:, kc, d0:d0 + n], src[:, kc, c0:c0 + n], g[:, kc:kc + 1],
                                                              rs[:, :n], ALU.mult, ALU.mult),
                     r=[src_buf, rsb, self.B_const], w=[dst_buf])

    def load_w_panel(self, wsrc, c0, n, kc_n=KC):
        i = self.wp_i % len(self.wpan)
        self.wp_i += 1
        wt, wb = self.wpan[i], self.wpan_b[i]
        for kc in range(kc_n):
            self.dma("pool", wt[:, kc, :n], wsrc[kc * 128:(kc + 1) * 128, c0:c0 + n], w=[wb])
        return wt, wb

    def ttiles(self):
        return [(i * 512, 512) for i in range(4)] + [(TP, TS)]

    def tblocks(self):
        return [(i * 128, 128) for i in range(16)] + [(TP, 16), (TP + 16, 16)]

    def dram(self, name, shape, dtype=F32):
        t = self.nc.dram_tensor(name, list(shape), dtype).ap()
        return t, Buf(name)

    def allgather(self, xin, xin_b, xout, xout_b):
        groups = [[2 * i, 2 * i + 1] for i in range(self.ncores // 2)]
        self.P.add("pool", lambda e: e.collective_compute("AllGather", ALU.bypass, replica_groups=groups,
                                                          ins=[xin], outs=[xout]),
                   [xin_b], [xout_b], dma=True, q="cc")

    def dump_h(self, name):
        t = self.dout(name, [128, KC * T])
        self.dma("sp", t, self.hT.rearrange("p c t -> p (c t)"), r=[self.hT_b])

    def alloc_stage(self):
        self.stage_f = [self.A.alloc([128, DM], F32, "stg") for _ in range(2)]
        self.stage_fb = [Buf("stg") for _ in range(2)]

    def alloc_wpan(self, n=2):
        self.wpan = [self.A.alloc([128, KC, 512], BF16, "wpan") for _ in range(n)]
        self.wpan_b = [Buf("wpan") for _ in range(n)]
        self.wp_i = 0

    def bank_bf(self, ps):
        return ps.bitcast(BF16)

    def build(self):
        nc = self.nc
        A = self.A
        self.declare_io()
        self.ps_i = 0
        self.pa_i = 0
        self.stg_i = 0
        self.wp_i = 0
        self.ps = [nc.alloc_psum_tensor(f"ps{i}", [128, 512], F32).ap() for i in range(8)]
        self.psb = [Buf(f"ps{i}", excl=True) for i in range(8)]
        self.consts()
        self.eps_t = A.alloc([128, 1], F32, "eps")
        self.pool(lambda e: e.memset(self.eps_t, EPS), w=[self.B_const])
        self.hT = A.alloc([128, KC, T], F32, "hT")
        self.hT_b = Buf("hT")
        self.sqt = [A.alloc([128, 512], BF16, "sq") for _ in range(2)]
        self.sqt_b = [Buf("sq") for _ in range(2)]
        self.rstd = A.alloc([128, 512], F32, "rstd")
        self.rstd_b = Buf("rstd")
        mload = A.mark()
        self.alloc_stage()
        self.load_transposed(self.I("xp"), TP, self.hT, self.hT_b, 0)
        self.load_transposed(self.I("xs"), TS, self.hT, self.hT_b, TP)
        self.P.barrier()
        A.release(mload)

        for l in range(2):
            if self.stage <= 0:
                break
            self.layer(l)
            if self.stage < 10:
                self.dump_h("dbg_h")
                break
        if self.stage >= 10:
            self.final_phase()
        self.P.finish()
        self.P.emit()

    def layer(self, l):
        A = self.A
        m0 = A.mark()
        uT = self.uT = A.alloc([128, KC, T], BF16, "uT")
        uT_b = Buf("uT")
        self.uT_b = uT_b
        for (c0, n) in self.ttiles():
            self.rmsnorm_fm(self.hT, self.hT_b, uT, uT_b, self.g_mix[l], c0, n)
        if self.stage <= 0.5:
            return
        if self.stage >= 2:
            self.ssd_phase(l)
        if self.stage >= 3 or self.stage == 1:
            self.attn_phase(l, "sb")
        if self.stage >= 4 or self.stage == 1:
            self.attn_phase(l, "fox")
        self.P.barrier()
        A.release(m0)
        if self.stage >= 5:
            self.xattn_phase(l)
        if self.stage >= 6:
            self.ffn_phase(l)

    def kv_tm(self, l, kind):
        A = self.A
        uT, uT_b = self.uT, self.uT_b
        col = 1024 if kind == "fox" else 256
        vt = A.alloc([128, 18, 256], BF16, "v_" + kind)
        vb = Buf("v_" + kind)
        m1 = A.mark()
        self.alloc_wpan(1)
        self.alloc_stage()
        wt, wb = self.load_w_panel(self.I("w_in")[l], col, 512)
        for bi, (c0, n) in enumerate(self.tblocks()):
            ps, pb = self.psum_bank()
            for kc in range(KC):
                self.pe(lambda e, ps=ps, kc=kc, c0=c0, n=n: e.matmul(
                    ps[:n, :], uT[:, kc, c0:c0 + n], wt[:, kc, :], start=(kc == 0), stop=(kc == KC - 1)),
                    r=[uT_b, wb], w=[pb])
            st, sb = self.stage_f[self.stg_i % 2], self.stage_fb[self.stg_i % 2]
            self.stg_i += 1
            self.act(lambda e, st=st, ps=ps, n=n: e.copy(out=st[:n, 0:512], in_=ps[:n, :]), r=[pb], w=[sb])
            self.dve(lambda e, bi=bi, ps=ps, n=n: e.tensor_copy(vt[:n, bi, :], ps[:n, 256:512]), r=[pb], w=[vb])
            pre = "p_" if c0 < TP else "s_"
            r0 = c0 if c0 < TP else c0 - TP
            self.dma("sp", self.o_kv[pre + kind + "_k"][l, r0:r0 + n, :], st[:n, 0:256], r=[sb])
            self.dma("sp", self.o_kv[pre + kind + "_v"][l, r0:r0 + n, :], st[:n, 256:512], r=[sb])
        self.P.barrier()
        A.release(m1)
        return vt, vb

    def ssd_phase(self, l):
        A = self.A
        uT, uT_b = self.uT, self.uT_b
        win = self.I("w_in")[l]
        m0 = A.mark()
        cbuf = self.B_const
        XW = 3 + TP + 2 * 19
        seqs = [(0, TP, 3, 128), (TP, 16, 3 + TP + 3, 16), (TP + 16, 16, 3 + TP + 19 + 3, 16)]
        xcT = A.alloc([128, 8, T], BF16, "xcT")
        xcT_b = Buf("xcT")
        tailp = A.alloc([128, 8, 3], F32, "tailp")
        tails = A.alloc([128, 8, 2, 3], F32, "tails")
        tail_b = Buf("tail")
        prevt = A.alloc([128, 8, 3], F32, "prevt")
        prevt_b = Buf("prevt")
        cw = A.alloc([128, 8, 4], F32, "cw")
        cbias = A.alloc([128, 8], F32, "cbias")
        par_b = Buf("ssdpar")
        for kc in range(8):
            self.dma("sp", cw[:, kc, :], self.I("conv_w")[l][:, kc * 128:(kc + 1) * 128].rearrange("k p -> p k"), w=[par_b], nc_ok=True)
        self.dma("sp", cbias, self.I("conv_b")[l].rearrange("(c p) -> p c", p=128), w=[par_b], nc_ok=True)
        dtb = A.alloc([128, 8], F32, "dtb")
        alog = A.alloc([128, 8], F32, "alog")
        Abc = A.alloc([128, 8], F32, "Abc")
        dsk = A.alloc([128, 8], F32, "dsk")
        gssd = A.alloc([128, 512], F32, "gssd")
        self.dma("sp", dtb, bcast(self.I("dt_bias")[l], 0, 128), w=[par_b])
        self.dma("sp", alog, bcast(self.I("a_log")[l], 0, 128), w=[par_b])
        self.dma("sp", dsk, bcast(self.I("d_skip")[l], 0, 128), w=[par_b])
        self.dma("sp", gssd, bcast(self.I("ssd_norm_g")[l], 0, 128), w=[par_b])
        self.act(lambda e: e.activation(out=Abc, in_=alog, func=AF.Exp), r=[par_b], w=[par_b])
        self.dve(lambda e: e.tensor_scalar(Abc, Abc, -1.0, None, ALU.mult), r=[par_b], w=[par_b])
        wz = A.alloc([128, 8, 512], BF16, "wz")
        wz_b = Buf("wz")
        for kc in range(KC):
            self.dma("pool", wz[:, kc, :], win[kc * 128:(kc + 1) * 128, 1540:2052], w=[wz_b])
        wos = A.alloc([128, 4, 1024], BF16, "wos")
        wos_b = Buf("wos")
        for kc in range(4):
            self.dma("pool", wos[:, kc, :], self.I("w_out")[l][512 + kc * 128: 512 + (kc + 1) * 128, :], w=[wos_b])
        wdt = A.alloc([128, 8, 8], BF16, "wdt")
        wdt_b = Buf("wdt")
        self.dma("pool", wdt, win[:, 3076:3084].rearrange("(c p) n -> p c n", p=128), w=[wdt_b], nc_ok=True)

        m1 = A.mark()
        self.alloc_wpan()
        xtmp = [A.alloc([128, XW], BF16, "xtmp") for _ in range(2)]
        xtmp_b = [Buf("xtmp") for _ in range(2)]
        acc = [A.alloc([128, 512], F32, "acc") for _ in range(2)]
        acc_b = [Buf("acc") for _ in range(2)]
        acc3 = A.alloc([128, 8, 3], F32, "acc3")
        acc3_b = Buf("acc3")
        pf = A.alloc([128, 8, 3], F32, "pf")
        ctmp = A.alloc([128, 8], F32, "ctmp")
        pads = A.alloc([128, 8, 3, 3], BF16, "pads")
        pads_b = Buf("pads")
        pans = [self.load_w_panel(win, 2052 + pn * 512, 512) for pn in range(2)]
        for oc in range(8):
            wt, wb = pans[oc // 4]
            j = oc % 4
            ps, pb = self.psum_bank()
            for si, c0 in enumerate((TP - 32, TP)):
                for kc in range(KC):
                    self.pe(lambda e, ps=ps, kc=kc, c0=c0, si=si, wt=wt, j=j: e.matmul(
                        ps[:, si * 32:(si + 1) * 32], wt[:, kc, j * 128:(j + 1) * 128], uT[:, kc, c0:c0 + 32],
                        start=(kc == 0), stop=(kc == KC - 1)), r=[uT_b, wb], w=[pb])
            self.dve(lambda e, ps=ps, oc=oc: e.tensor_copy(tailp[:, oc, :], ps[:, 29:32]), r=[pb], w=[tail_b])
            for jj in range(2):
                self.dve(lambda e, ps=ps, oc=oc, jj=jj: e.tensor_copy(tails[:, oc, jj, :], ps[:, 32 + jj * 16 + 13:32 + jj * 16 + 16]),
                         r=[pb], w=[tail_b])
        x1in, x1in_b = self.dram(f"x1in{l}", [128, 24])
        x1out, x1out_b = self.dram(f"x1out{l}", [256, 24])
        self.dve(lambda e: e.memset(pads[:, :, 0, :], 0.0), w=[pads_b])
        stc = A.alloc([3, 2, DM], F32, "stc")
        stc_b = Buf("stc")
        for jj in range(2):
            self.dma("sp", stc[:, jj, :], self.I("st_conv")[l, jj], w=[stc_b])
        ps_c, pb_c = self.psum_bank()
        for jj in range(2):
            for kc in range(8):
                self.pe(lambda e, jj=jj, kc=kc: e.transpose(ps_c[:, (jj * 8 + kc) * 3:(jj * 8 + kc) * 3 + 3], stc[:, jj, kc * 128:(kc + 1) * 128], self.ident_f[:3, :3]),
                        r=[stc_b, cbuf], w=[pb_c])
        for jj in range(2):
            self.act(lambda e, jj=jj: e.copy(out=pads[:, :, 1 + jj, :], in_=ps_c[:, jj * 24:(jj + 1) * 24].rearrange("p (c t) -> p c t", t=3)),
                     r=[pb_c], w=[pads_b])
        for kc in range(8):
            for jj in range(2):
                self.dma("sp", self.s_conv[l, jj][:, kc * 128:(kc + 1) * 128].rearrange("t p -> p t"), tails[:, kc, jj, :], r=[tail_b], nc_ok=True)
            self.dma("sp", self.p_conv[l][:, kc * 128:(kc + 1) * 128].rearrange("t p -> p t"), tailp[:, kc, :], r=[tail_b], nc_ok=True)
        self.dma("sp", x1in, tailp.rearrange("p a b -> p (a b)"), r=[tail_b], w=[x1in_b])
        self.allgather(x1in, x1in_b, x1out, x1out_b)
        self.dma("sp", prevt.rearrange("p a b -> p (a b)"), x1out[0:128, :], r=[x1out_b], w=[prevt_b])
        ai = 0
        for oc in range(8):
            wt, wb = pans[oc // 4]
            j = oc % 4
            xt, xtb = xtmp[oc % 2], xtmp_b[oc % 2]
            for si, (tok0, n, xc, L) in enumerate(seqs):
                self.act(lambda e, xt=xt, oc=oc, si=si, xc=xc: e.copy(out=xt[:, xc - 3:xc], in_=pads[:, oc, si, :]), r=[pads_b], w=[xtb])
            for (c0, n) in self.ttiles():
                ps, pb = self.psum_bank()
                for kc in range(KC):
                    self.pe(lambda e, ps=ps, kc=kc, c0=c0, n=n, wt=wt, j=j: e.matmul(
                        ps[:, :n], wt[:, kc, j * 128:(j + 1) * 128], uT[:, kc, c0:c0 + n],
                        start=(kc == 0), stop=(kc == KC - 1)), r=[uT_b, wb], w=[pb])
                if c0 < TP:
                    self.act(lambda e, ps=ps, xt=xt, c0=c0, n=n: e.copy(out=xt[:, 3 + c0:3 + c0 + n], in_=ps[:, :n]), r=[pb], w=[xtb])
                else:
                    for jj in range(2):
                        xc = seqs[1 + jj][2]
                        self.act(lambda e, ps=ps, xt=xt, xc=xc, jj=jj: e.copy(out=xt[:, xc:xc + 16], in_=ps[:, jj * 16:(jj + 1) * 16]), r=[pb], w=[xtb])
            segs = [(c0, 512, 3 + c0) for c0 in range(0, TP, 512)] + [(TP, 16, seqs[1][2]), (TP + 16, 16, seqs[2][2])]
            for (tok0, n, xc) in segs:
                ac, acb = acc[ai % 2], acc_b[ai % 2]
                ai += 1
                self.dve(lambda e, ac=ac, xt=xt, oc=oc, xc=xc, n=n: e.tensor_scalar(ac[:, :n], xt[:, xc - 3:xc - 3 + n], cw[:, oc, 0:1], None, ALU.mult),
                         r=[xtb, par_b], w=[acb])
                for k in range(1, 4):
                    self.dve(lambda e, ac=ac, xt=xt, oc=oc, xc=xc, n=n, k=k: e.scalar_tensor_tensor(
                        ac[:, :n], xt[:, xc - 3 + k:xc - 3 + k + n], cw[:, oc, k:k + 1], ac[:, :n], ALU.mult, ALU.add),
                        r=[xtb, par_b, acb], w=[acb])
                self.act(lambda e, ac=ac, oc=oc, tok0=tok0, n=n: e.activation(out=xcT[:, oc, tok0:tok0 + n], in_=ac[:, :n], func=AF.Silu,
                                                                              bias=cbias[:, oc:oc + 1]),
                         r=[acb, par_b], w=[xcT_b])
                if tok0 == 0:
                    self.dve(lambda e, ac=ac, oc=oc: e.tensor_copy(acc3[:, oc, :], ac[:, 0:3]), r=[acb], w=[acc3_b])
        self.dve(lambda e: e.tensor_scalar(pf, prevt, self.flags_t[:, 0:1], None, ALU.mult), r=[prevt_b, cbuf], w=[acc3_b])
        for t in range(3):
            for k in range(3 - t):
                self.dve(lambda e, t=t, k=k: e.tensor_tensor(ctmp, cw[:, :, k], pf[:, :, t + k], ALU.mult), r=[par_b, acc3_b], w=[acc3_b])
                self.dve(lambda e, t=t: e.tensor_tensor(acc3[:, :, t], acc3[:, :, t], ctmp, ALU.add), r=[acc3_b], w=[acc3_b])
        self.dve(lambda e: e.tensor_tensor(acc3, acc3, bcast(cbias, 2, 3), ALU.add), r=[acc3_b, par_b], w=[acc3_b])
        self.act(lambda e: e.activation(out=xcT[:, :, 0:3], in_=acc3, func=AF.Silu), r=[acc3_b], w=[xcT_b])
        self.P.barrier()
        A.release(m1)
        NB = 18
        blocks = self.tblocks()
        ps_dt, pb_dt = self.psum_bank()
        for bi, (c0, n) in enumerate(blocks):
            for kc in range(KC):
                self.pe(lambda e, bi=bi, kc=kc, c0=c0, n=n: e.matmul(ps_dt[:n, bi * 8:(bi + 1) * 8], uT[:, kc, c0:c0 + n], wdt[:, kc, :],
                                                                     start=(kc == 0), stop=(kc == KC - 1)),
                        r=[uT_b, wdt_b], w=[pb_dt])
        dt_all = A.alloc([128, NB, 8], F32, "dt_all")
        a_all = A.alloc([128, NB, 8], F32, "a_all")
        dta_b = Buf("dta")
        self.dve(lambda e: e.tensor_tensor(dt_all, ps_dt[:, :NB * 8].rearrange("p (b h) -> p b h", h=8), bcast(dtb, 1, NB), ALU.add),
                 r=[pb_dt, par_b], w=[dta_b])
        self.act(lambda e: e.activation(out=dt_all, in_=dt_all, func=AF.Exp), r=[dta_b], w=[dta_b])
        self.act(lambda e: e.activation(out=dt_all, in_=dt_all, func=AF.Ln, bias=1.0), r=[dta_b], w=[dta_b])
        self.dve(lambda e: e.tensor_tensor(a_all, dt_all, bcast(Abc, 1, NB), ALU.mult), r=[dta_b, par_b], w=[dta_b])
        ps_ac, pb_ac = self.psum_bank()
        ps_at, pb_at = self.psum_bank()
        a2 = a_all.rearrange("p b h -> p (b h)")
        self.pe(lambda e: e.matmul(ps_ac[:, 0:128], self.U_f, a2[:, 0:128], start=True, stop=True), r=[dta_b, cbuf], w=[pb_ac])
        self.pe(lambda e: e.matmul(ps_ac[:16, 128:144], self.U_f[:16, :16], a2[:16, 128:144], start=True, stop=True), r=[dta_b, cbuf], w=[pb_ac])
        self.pe(lambda e: e.matmul(ps_at[:, 0:128], self.ones_f, a2[:, 0:128], start=True, stop=True), r=[dta_b, cbuf], w=[pb_at])
        self.pe(lambda e: e.matmul(ps_at[:, 128:144], self.ones_f[:16, :], a2[:16, 128:144], start=True, stop=True), r=[dta_b, cbuf], w=[pb_at])
        acum = A.alloc([128, 144], F32, "acum")
        ea = A.alloc([128, 144], F32, "ea")
        dec = A.alloc([128, 144], F32, "dec")
        wend = A.alloc([128, 144], F32, "wend")
        cs_b = Buf("cs")
        self.act(lambda e: e.copy(out=acum, in_=ps_ac[:, :144]), r=[pb_ac], w=[cs_b])
        self.act(lambda e: e.activation(out=ea, in_=ps_ac[:, :144], func=AF.Exp), r=[pb_ac], w=[cs_b])
        self.act(lambda e: e.activation(out=dec, in_=ps_at[:, :144], func=AF.Exp), r=[pb_at], w=[cs_b])
        self.dve(lambda e: e.tensor_tensor(wend, ps_at[:, :144], acum, ALU.subtract), r=[pb_at, cs_b], w=[cs_b])
        self.act(lambda e: e.activation(out=wend, in_=wend, func=AF.Exp), r=[cs_b], w=[cs_b])

        NR = 2
        xs_tm = [A.alloc([128, 512], BF16, "xs_tm") for _ in range(NR)]
        btm = [A.alloc([128, 256], BF16, "btm") for _ in range(NR)]
        xdt = [A.alloc([128, 512], BF16, "xdt") for _ in range(NR)]
        xdtw = [A.alloc([128, 512], BF16, "xdtw") for _ in range(NR)]
        prep_b = [Buf("prep") for _ in range(NR)]
        self.prep_i = 0

        def prep(bi, c0, L):
            i = self.prep_i % NR
            self.prep_i += 1
            pb_ = prep_b[i]
            ps, pb = self.psum_bank()
            pv = self.bank_bf(ps)
            for f in range(6):
                self.pe(lambda e, f=f: e.transpose(pv[:L, f * 128:(f + 1) * 128], xcT[:, f, c0:c0 + L], self.ident_bf),
                        r=[xcT_b, cbuf], w=[pb])
            self.act(lambda e: e.copy(out=xs_tm[i][:L, :], in_=pv[:L, 0:512]), r=[pb], w=[pb_])
            self.act(lambda e: e.copy(out=btm[i][:L, :], in_=pv[:L, 512:768]), r=[pb], w=[pb_])
            self.dve(lambda e: e.tensor_tensor(xdt[i][:L, :].rearrange("p (h d) -> p h d", h=8),
                                               xs_tm[i][:L, :].rearrange("p (h d) -> p h d", h=8),
                                               bcast(dt_all[:L, bi, :], 2, 64), ALU.mult), r=[pb_, dta_b], w=[pb_])
            self.dve(lambda e: e.tensor_tensor(xdtw[i][:L, :].rearrange("p (h d) -> p h d", h=8),
                                               xdt[i][:L, :].rearrange("p (h d) -> p h d", h=8),
                                               bcast(wend[:L, bi * 8:(bi + 1) * 8], 2, 64), ALU.mult), r=[pb_, cs_b], w=[pb_])
            return i, pb_

        hs = A.alloc([128, 512], F32, "hs")
        hs_bf = A.alloc([128, 512], BF16, "hs_bf")
        hs_b = Buf("hs")
        hsb_b = Buf("hs_bf")

        def state_update(bi, L, i, pb_):
            ps, pb = self.psum_bank()
            for g in range(2):
                self.pe(lambda e, g=g: e.matmul(ps[:, g * 256:(g + 1) * 256], btm[i][:L, g * 128:(g + 1) * 128], xdtw[i][:L, g * 256:(g + 1) * 256],
                                                start=True, stop=True), r=[pb_], w=[pb])
            h3 = hs.rearrange("p (h d) -> p h d", h=8)
            self.dve(lambda e: e.tensor_tensor(h3, h3, bcast(dec[:, bi * 8:(bi + 1) * 8], 2, 64), ALU.mult), r=[hs_b, cs_b], w=[hs_b])
            self.dve(lambda e: e.tensor_tensor(hs, hs, ps, ALU.add), r=[hs_b, pb], w=[hs_b])

        self.dve(lambda e: e.memset(hs, 0.0), w=[hs_b])
        for ci in range(16):
            i, pb_ = prep(ci, ci * 128, 128)
            state_update(ci, 128, i, pb_)
        x2in, x2in_b = self.dram(f"x2in{l}", [128, 512])
        x2out, x2out_b = self.dram(f"x2out{l}", [256, 512])
        self.dma("sp", x2in, hs, r=[hs_b], w=[x2in_b])
        self.allgather(x2in, x2in_b, x2out, x2out_b)

        t1_l = [A.alloc([128, 512], F32, "t1") for _ in range(2)]
        t4_l = [A.alloc([128, 512], F32, "t4") for _ in range(2)]
        yb_l = [Buf("y") for _ in range(2)]
        gm_l = [A.alloc([128, 2, 128], F32, "gm") for _ in range(2)]
        RA_l = [A.alloc([128, 8, 128], F32, "RA") for _ in range(2)]
        Lm_l = [A.alloc([128, 8, 128], BF16, "Lm") for _ in range(2)]
        Mm_l = [A.alloc([128, 8, 128], BF16, "Mm") for _ in range(2)]
        mm_bl = [Buf("mm") for _ in range(2)]
        zs_l = [A.alloc([128, 512], F32, "zs") for _ in range(2)]
        zs_bl = [Buf("zs") for _ in range(2)]
        ss_l = [A.alloc([128, 1], F32, "ss") for _ in range(2)]
        ob_l = [A.alloc([128, 512], BF16, "ob") for _ in range(2)]
        oT_l = [A.alloc([128, 4, 128], BF16, "oT") for _ in range(2)]
        o_bl = [Buf("o") for _ in range(2)]
        sttm = A.alloc([128, 4, 128], F32, "sttm")
        sttm_b = Buf("sttm")

        def load_state_tm(src):
            self.dma("sp", sttm, src.rearrange("(c p) n -> p c n", p=128), w=[sttm_b])
            ps, pb = self.psum_bank()
            for c in range(4):
                self.pe(lambda e, c=c: e.transpose(ps[:, c * 128:(c + 1) * 128], sttm[:, c, :], self.ident_f), r=[sttm_b, cbuf], w=[pb])
            self.dve(lambda e: e.tensor_copy(hs, ps), r=[pb], w=[hs_b])

        def store_state_tm(dst):
            ps, pb = self.psum_bank()
            for c in range(4):
                self.pe(lambda e, c=c: e.transpose(ps[:, c * 128:(c + 1) * 128], hs[:, c * 128:(c + 1) * 128], self.ident_f), r=[hs_b, cbuf], w=[pb])
            self.act(lambda e: e.copy(out=sttm, in_=ps.rearrange("p (c n) -> p c n", c=4)), r=[pb], w=[sttm_b])
            self.dma("sp", dst.rearrange("(c p) n -> p c n", p=128), sttm, r=[sttm_b])

        def chunk_full(bi, c0, L):
            par = bi % 2
            t1, t4, yb = t1_l[par], t4_l[par], yb_l[par]
            gm, RA, Lm, Mm, mm_b = gm_l[par], RA_l[par], Lm_l[par], Mm_l[par], mm_bl[par]
            zs, zs_b, ss, ob, oT, o_b = zs_l[par], zs_bl[par], ss_l[par], ob_l[par], oT_l[par], o_bl[par]
            i, pb_ = prep(bi, c0, L)
            ps_z, pb_z = self.psum_bank()
            for kc in range(KC):
                self.pe(lambda e, kc=kc: e.matmul(ps_z[:L, :], uT[:, kc, c0:c0 + L], wz[:, kc, :], start=(kc == 0), stop=(kc == KC - 1)),
                        r=[uT_b, wz_b], w=[pb_z])
            self.act(lambda e: e.activation(out=zs[:L, :], in_=ps_z[:L, :], func=AF.Silu), r=[pb_z], w=[zs_b])
            ps_g, pb_g = self.psum_bank()
            for g in range(2):
                self.pe(lambda e, g=g: e.matmul(ps_g[:L, g * 128:g * 128 + L], xcT[:, 4 + g, c0:c0 + L], xcT[:, 6 + g, c0:c0 + L], start=True, stop=True),
                        r=[xcT_b], w=[pb_g])
            self.dve(lambda e: e.tensor_tensor(gm[:L, :, :L], ps_g[:L, 0:256].rearrange("p (g t) -> p g t", g=2)[:, :, :L],
                                               bcast(self.U_f[:L, :L], 1, 2), ALU.mult), r=[pb_g, cbuf], w=[mm_b])
            self.dve(lambda e: e.tensor_tensor(RA[:L, :, :L], bcast(self.U_f[:L, :L], 1, 8), bcast(a_all[:L, bi, :], 2, L), ALU.mult),
                     r=[dta_b, cbuf], w=[mm_b])
            for hh in range(2):
                ps_s, pb_s = self.psum_bank()
                self.pe(lambda e, hh=hh, ps_s=ps_s: e.matmul(ps_s[:L, :4 * L].rearrange("p (h t) -> p h t", h=4), self.SL_f[:L, :L], RA[:L, hh * 4:hh * 4 + 4, :L],
                                                             start=True, stop=True), r=[mm_b, cbuf], w=[pb_s])
                self.act(lambda e, hh=hh, ps_s=ps_s: e.activation(out=Lm[:L, hh * 4:hh * 4 + 4, :L], in_=ps_s[:L, :4 * L].rearrange("p (h t) -> p h t", h=4), func=AF.Exp),
                         r=[pb_s], w=[mm_b])
                self.pool(lambda e, hh=hh: e.tensor_tensor(Mm[:L, hh * 4:hh * 4 + 4, :L], Lm[:L, hh * 4:hh * 4 + 4, :L], bcast(gm[:L, hh, :L], 1, 4), ALU.mult),
                          r=[mm_b], w=[mm_b])
            ps_yd, pb_yd = self.psum_bank()
            for h in range(8):
                self.pe(lambda e, h=h: e.matmul(ps_yd[:L, h * 64:(h + 1) * 64], Mm[:L, h, :L], xdt[i][:L, h * 64:(h + 1) * 64], start=True, stop=True),
                        r=[mm_b, pb_], w=[pb_yd])
            ps_yo, pb_yo = self.psum_bank()
            for g in range(2):
                self.pe(lambda e, g=g: e.matmul(ps_yo[:L, g * 256:(g + 1) * 256], xcT[:, 6 + g, c0:c0 + L], hs_bf[:, g * 256:(g + 1) * 256], start=True, stop=True),
                        r=[xcT_b, hsb_b], w=[pb_yo])
            self.dve(lambda e: e.tensor_tensor(t1[:L, :].rearrange("p (h d) -> p h d", h=8), ps_yo[:L, :].rearrange("p (h d) -> p h d", h=8),
                                               bcast(ea[:L, bi * 8:(bi + 1) * 8], 2, 64), ALU.mult), r=[pb_yo, cs_b], w=[yb])
            self.dve(lambda e: e.tensor_tensor(t1[:L, :], t1[:L, :], ps_yd[:L, :], ALU.add), r=[pb_yd, yb], w=[yb])
            self.pool(lambda e: e.tensor_tensor(t4[:L, :].rearrange("p (h d) -> p h d", h=8), xs_tm[i][:L, :].rearrange("p (h d) -> p h d", h=8),
                                                bcast(dsk[:L, :], 2, 64), ALU.mult), r=[pb_, par_b], w=[zs_b])
            self.dve(lambda e: e.tensor_tensor(t1[:L, :], t1[:L, :], t4[:L, :], ALU.add), r=[yb, zs_b], w=[yb])
            state_update(bi, L, i, pb_)
            self.act(lambda e: e.copy(out=hs_bf, in_=hs), r=[hs_b], w=[hsb_b])
            self.dve(lambda e: e.tensor_tensor(t1[:L, :], t1[:L, :], zs[:L, :], ALU.mult), r=[yb, zs_b], w=[yb])
            self.act(lambda e: e.activation(out=zs[:L, :], in_=t1[:L, :], func=AF.Square, accum_out=ss[:L, 0:1]), r=[yb], w=[zs_b])
            self.act(lambda e: e.activation(out=ss[:L, :], in_=ss[:L, :], func=AF.Sqrt, bias=self.eps_t[:L, 0:1], scale=1.0 / 512), r=[zs_b, cbuf], w=[zs_b])
            self.dve(lambda e: e.reciprocal(ss[:L, :], ss[:L, :]), r=[zs_b], w=[zs_b])
            self.dve(lambda e: e.scalar_tensor_tensor(ob[:L, :], t1[:L, :], ss[:L, 0:1], gssd[:L, :], ALU.mult, ALU.mult), r=[yb, zs_b, par_b], w=[o_b])
            ps_t, pb_t = self.psum_bank()
            pv = self.bank_bf(ps_t)
            for f in range(4):
                self.pe(lambda e, f=f: e.transpose(pv[:, f * 128:f * 128 + L], ob[:L, f * 128:(f + 1) * 128], self.ident_bf[:L, :L]), r=[o_b, cbuf], w=[pb_t])
            self.act(lambda e: e.copy(out=oT[:, :, :L], in_=pv[:, 0:512].rearrange("p (f t) -> p f t", f=4)[:, :, :L]), r=[pb_t], w=[o_b])
            for half in range(2):
                ps_o, pb_o = self.psum_bank()
                for oo in range(4):
                    o_c = half * 4 + oo
                    for kc in range(4):
                        self.pe(lambda e, oo=oo, o_c=o_c, kc=kc, ps_o=ps_o: e.matmul(ps_o[:, oo * 128:oo * 128 + L], wos[:, kc, o_c * 128:(o_c + 1) * 128], oT[:, kc, :L],
                                                                                     start=(kc == 0), stop=(kc == 3)), r=[wos_b, o_b], w=[pb_o])
                hsl = self.hT[:, half * 4:half * 4 + 4, c0:c0 + L]
                self.dve(lambda e, hsl=hsl, ps_o=ps_o: e.tensor_tensor(hsl, hsl, ps_o.rearrange("p (o t) -> p o t", o=4)[:, :, :L], ALU.add),
                         r=[pb_o, self.hT_b], w=[self.hT_b])

        self.dma("sp", hs, x2out[0:128, :], r=[x2out_b], w=[hs_b])
        self.dve(lambda e: e.tensor_scalar(hs, hs, self.flags_t[:, 0:1], None, ALU.mult), r=[hs_b, cbuf], w=[hs_b])
        self.act(lambda e: e.copy(out=hs_bf, in_=hs), r=[hs_b], w=[hsb_b])
        for ci in range(16):
            chunk_full(ci, ci * 128, 128)
        store_state_tm(self.p_ssm[l])
        for jj in range(2):
            load_state_tm(self.I("st_ssm")[l, jj])
            self.act(lambda e: e.copy(out=hs_bf, in_=hs), r=[hs_b], w=[hsb_b])
            chunk_full(16 + jj, TP + 16 * jj, 16)
            store_state_tm(self.s_ssm[l, jj])
        self.P.barrier()
        A.release(m0)


    def attn_phase(self, l, kind):
        A = self.A
        uT, uT_b = self.uT, self.uT_b
        cbuf = self.B_const
        win = self.I("w_in")[l]
        m0 = A.mark()
        fox = kind == "fox"
        KA = 71 if fox else 65
        VW = 128
        colq = 768 if fox else 0
        wo_base = 256 if fox else 0
        vt, vb = self.kv_tm(l, kind)
        if self.stage == 1:
            self.P.barrier()
            A.release(m0)
            return
        Fk = Fq = None
        if fox:
            Fk, Fq, Fk_b, Fp, Fp_b = self.fox_gates(l)
        ck = self.I("c_fox_k" if fox else "c_sb_k")
        cv = self.I("c_fox_v" if fox else "c_sb_v")
        blocks = self.tblocks()
        wt = A.alloc([128, KC, 256], BF16, "wqk")
        wb = Buf("wqk")
        wqh = A.alloc([128, KC, 128], BF16, "wqh")
        wqh_b = Buf("wqh")
        for kc in range(KC):
            self.dma("pool", wt[:, kc, :], win[kc * 128:(kc + 1) * 128, colq + 256:colq + 512], w=[wb])
        wo = A.alloc([128, 1, DM], BF16, "wo_att")
        wo_b = Buf("wo_att")
        kT = [A.alloc([128, T], BF16, "kT") for _ in range(4)]
        kT_b = [Buf("kT") for _ in range(4)]
        NB2 = 1
        qT = [A.alloc([128, T], BF16, "qT") for _ in range(2)]
        qT_b = [Buf("qT") for _ in range(2)]
        oT = [A.alloc([128, T], BF16, "oT") for _ in range(NB2)]
        oT_b = [Buf("oT") for _ in range(NB2)]
        kTp = [A.alloc([128, TP], BF16, "kTp") for _ in range(NB2)]
        kTp_b = [Buf("kTp") for _ in range(NB2)]
        vtp = [A.alloc([128, 16, VW], BF16, "vtp") for _ in range(NB2)]
        vtp_b = [Buf("vtp") for _ in range(NB2)]
        vown = [A.alloc([128, 18, VW], BF16, "vown") for _ in range(NB2)]
        vown_b = [Buf("vown") for _ in range(NB2)]
        kTc = [A.alloc([128, PAST], BF16, "kTc") for _ in range(NB2)]
        kTc_b = [Buf("kTc") for _ in range(NB2)]
        vtc = [A.alloc([128, 32, VW], BF16, "vtc") for _ in range(NB2)]
        vtc_b = [Buf("vtc") for _ in range(NB2)]
        kst = A.alloc([128, 32, 64], BF16, "kst")
        kst_b = Buf("kst")
        e32 = [A.alloc([128, 512], BF16, "e32") for _ in range(4)]
        e32_b = [Buf("e32") for _ in range(4)]
        tbf = [A.alloc([128, 512], BF16, "tbf") for _ in range(2)]
        tbf_b = [Buf("tbf") for _ in range(2)]
        spb = [A.alloc([128, 512], BF16, "spb") for _ in range(3)]
        spb_b = [Buf("spb") for _ in range(3)]
        arg = [A.alloc([128, 512], F32, "arg") for _ in range(2)]
        arg_b = [Buf("arg") for _ in range(2)]
        wbf = [A.alloc([128, 512], BF16, "wbf") for _ in range(3)]
        wbf_b = [Buf("wbf") for _ in range(3)]
        self.ag_i = 0
        self.wb_i = 0
        carry = A.alloc([128, 512], F32, "carry")
        carry_b = Buf("carry")
        rec = A.alloc([128, 512], F32, "rec")
        rec_b = Buf("rec")
        self.aw_i = 0

        for kt_, ktb_ in [(kT[i], kT_b[i]) for i in range(4)] + [(qT[i], qT_b[i]) for i in range(2)] + [(kTp[0], kTp_b[0]), (kTc[0], kTc_b[0])]:
            self.dve(lambda e, kt_=kt_: e.memset(kt_[64:128, :], 0.0), w=[ktb_])
        for vt_, vtb_ in ((vtp[0], vtp_b[0]), (vown[0], vown_b[0]), (vtc[0], vtc_b[0])):
            self.pool(lambda e, vt_=vt_: e.memset(vt_[:, :, 64:128], 1.0), w=[vtb_])
        def proj_pair(dst0, dst0_b, dst1, dst1_b, wsl, wsl_b, scale):
            for (c0, n) in self.ttiles():
                ps, pb = self.psum_bank()
                for kc in range(KC):
                    self.pe(lambda e, ps=ps, kc=kc, c0=c0, n=n: e.matmul(ps[:, :n], wsl[:, kc, :], uT[:, kc, c0:c0 + n],
                                                                         start=(kc == 0), stop=(kc == KC - 1)), r=[uT_b, wsl_b], w=[pb])
                self.act(lambda e, ps=ps, c0=c0, n=n: e.activation(out=dst0[0:64, c0:c0 + n], in_=ps[0:64, :n], func=AF.Copy, scale=scale), r=[pb], w=[dst0_b])
                self.act(lambda e, ps=ps, c0=c0, n=n: e.activation(out=dst1[0:64, c0:c0 + n], in_=ps[64:128, :n], func=AF.Copy, scale=scale), r=[pb], w=[dst1_b])

        for hp in range(2):
            proj_pair(kT[2 * hp], kT_b[2 * hp], kT[2 * hp + 1], kT_b[2 * hp + 1], wt[:, :, hp * 128:(hp + 1) * 128], wb, 1.0)
        for h in range(4):
            if fox:
                self.dve(lambda e, h=h: e.memset(kT[h][64:71, :], 1.0), w=[kT_b[h]])
                self.dve(lambda e, h=h: e.memset(kT[h][64:65, :], 0.0), w=[kT_b[h]])
                self.dma("sp", kT[h][65:68, :], Fk[h], r=[Fk_b], w=[kT_b[h]])
            else:
                self.dve(lambda e, h=h: e.memset(kT[h][64:65, :], 0.0), w=[kT_b[h]])
        for kc in range(KC):
            self.dma("pool", wqh[:, kc, :], win[kc * 128:(kc + 1) * 128, colq:colq + 128], w=[wqh_b], nc_ok=True)
        self.dma("pool", wo[:, 0, :], self.I("w_out")[l][wo_base: wo_base + 128, :], w=[wo_b])
        proj_pair(qT[0], qT_b[0], qT[1], qT_b[1], wqh, wqh_b, 0.125)
        xk_in, xk_in_b = self.dram(f"xk_in_{kind}{l}", [4 * KA, TP], BF16)
        xk_out, xk_out_b = self.dram(f"xk_out_{kind}{l}", [8 * KA, TP], BF16)
        xv_in, xv_in_b = self.dram(f"xv_in_{kind}{l}", [128, 16 * 256], BF16)
        xv_out, xv_out_b = self.dram(f"xv_out_{kind}{l}", [256, 16 * 256], BF16)
        onesrow = A.alloc([1, TP], BF16, "onesrow")
        onesrow_b = Buf("onesrow")
        self.dve(lambda e: e.memset(onesrow, 1.0), w=[onesrow_b])
        for h in range(4):
            self.dma("sp", xk_in[h * KA:h * KA + 64, :], kT[h][0:64, 0:TP], r=[kT_b[h]], w=[xk_in_b])
            self.dma("sp", xk_in[h * KA + 64:h * KA + 65, :], onesrow, r=[onesrow_b], w=[xk_in_b])
            if fox:
                self.dma("sp", xk_in[h * KA + 65:(h + 1) * KA, :], kT[h][65:KA, 0:TP], r=[kT_b[h]], w=[xk_in_b])
        self.dma("sp", xv_in, vt[:, 0:16, :].rearrange("p b c -> p (b c)"), r=[vb], w=[xv_in_b])
        self.allgather(xk_in, xk_in_b, xk_out, xk_out_b)
        self.allgather(xv_in, xv_in_b, xv_out, xv_out_b)

        def run_tile(blist, q_ap, q_b, nq, ps_o, pb_o):
            nb = len(blist)
            ctx = [dict() for _ in range(nb)]

            def sb_A(b):
                kT_ap, v_ap, kbufs, L, lo, diag = blist[b]
                n = nq - lo
                c = ctx[b]
                ps_z, pb_z = self.psum_bank()
                self.pe(lambda e: e.matmul(ps_z[:L, lo:nq], kT_ap, q_ap[:, lo:nq], start=True, stop=not diag), r=kbufs + [q_b], w=[pb_z])
                if diag:
                    self.pe(lambda e: e.matmul(ps_z[:L, lo:lo + L], self.ident_bf[:L, :L], self.MSK_s[:L, :L], start=False, stop=True), r=[cbuf], w=[pb_z])
                i = b % 4
                c["sp"], c["sp_b"] = spb[i % 3] if False else spb[b % 3], spb_b[b % 3]
                c["e"], c["e_b"] = e32[i], e32_b[i]
                self.act(lambda e: e.activation(out=e32[i][:L, :n], in_=ps_z[:L, lo:nq], func=AF.Exp), r=[pb_z], w=[e32_b[i]])
                self.act(lambda e: e.activation(out=spb[b % 3][:L, :n], in_=e32[i][:L, :n], func=AF.Ln, bias=1.0), r=[e32_b[i]], w=[spb_b[b % 3]])

            def sb_B1(b):
                kT_ap, v_ap, kbufs, L, lo, diag = blist[b]
                n = nq - lo
                c = ctx[b]
                sp_, sp_b = c["sp"], c["sp_b"]
                ps_r, pb_r = self.psum_bank()
                self.pe(lambda e: e.matmul(ps_r[:L, lo:nq], self.NLI_bf[:L, :L], sp_[:L, :n], start=True, stop=True), r=[sp_b, cbuf], w=[pb_r])
                ps_t, pb_t = self.psum_bank()
                self.pe(lambda e: e.matmul(ps_t[:, lo:nq], self.ones_bf[:L, :], sp_[:L, :n], start=True, stop=True), r=[sp_b, cbuf], w=[pb_t])
                j = b % 2
                c["arg"], c["arg_b"] = arg[j], arg_b[j]
                self.dve(lambda e: e.tensor_tensor(arg[j][:L, :n], ps_r[:L, lo:nq], carry[:L, lo:nq], ALU.subtract), r=[pb_r, carry_b], w=[arg_b[j]])
                self.dve(lambda e: e.tensor_tensor(carry[:, lo:nq], carry[:, lo:nq], ps_t[:, lo:nq], ALU.add), r=[pb_t, carry_b], w=[carry_b])

            def sb_B2(b):
                kT_ap, v_ap, kbufs, L, lo, diag = blist[b]
                n = nq - lo
                c = ctx[b]
                j = b % 2
                ar, ar_b = c["arg"], c["arg_b"]
                ee, ee_b = c["e"], c["e_b"]
                self.act(lambda e: e.activation(out=tbf[j][:L, :n], in_=ar[:L, :n], func=AF.Exp), r=[ar_b], w=[tbf_b[j]])
                jw = b % 3
                self.dve(lambda e: e.tensor_tensor(wbf[jw][:L, :n], ee[:L, :n], tbf[j][:L, :n], ALU.mult), r=[ee_b, tbf_b[j]], w=[wbf_b[jw]])

            def sb_B3(b):
                kT_ap, v_ap, kbufs, L, lo, diag = blist[b]
                n = nq - lo
                j = b % 3
                self.pe(lambda e: e.matmul(ps_o[:, lo:nq], v_ap, wbf[j][:L, :n], start=(b == 0), stop=(b == nb - 1), skip_group_check=True),
                        r=kbufs + [wbf_b[j]], w=[pb_o])

            def fox_A(b):
                kT_ap, v_ap, kbufs, L, lo, diag = blist[b]
                n = nq - lo
                c = ctx[b]
                ps_z, pb_z = self.psum_bank()
                self.pe(lambda e: e.matmul(ps_z[:L, lo:nq], kT_ap, q_ap[:, lo:nq], start=True, stop=not diag), r=kbufs + [q_b], w=[pb_z])
                if diag:
                    self.pe(lambda e: e.matmul(ps_z[:L, lo:lo + L], self.ident_bf[:L, :L], self.MSK_i[:L, :L], start=False, stop=True), r=[cbuf], w=[pb_z])
                i = b % 3
                c["w"], c["w_b"] = spb[i], spb_b[i]
                self.act(lambda e: e.activation(out=spb[i][:L, :n], in_=ps_z[:L, lo:nq], func=AF.Exp), r=[pb_z], w=[spb_b[i]])

            def fox_B(b):
                kT_ap, v_ap, kbufs, L, lo, diag = blist[b]
                n = nq - lo
                c = ctx[b]
                w_, w_b = c["w"], c["w_b"]
                self.pe(lambda e: e.matmul(ps_o[:, lo:nq], v_ap, w_[:L, :n], start=(b == 0), stop=(b == nb - 1), skip_group_check=True),
                        r=kbufs + [w_b], w=[pb_o])

            if fox:
                stages = [(fox_A, 0), (fox_B, 1)]
            else:
                stages = [(sb_B2, 2), (sb_A, 0), (sb_B1, 1), (sb_B3, 4)]
            S = max(k for _, k in stages)
            for s_ in range(nb + S):
                for st, k in stages:
                    bb = s_ - k
                    if 0 <= bb < nb:
                        st(bb)

        def finish_tile(ps_o, pb_o, dst, dst_b, c0, nq, po=0):
            if not fox:
                self.act(lambda e: e.copy(out=dst[po:po + 64, c0:c0 + nq], in_=ps_o[:64, :nq]), r=[pb_o], w=[dst_b])
                return
            self.dve(lambda e: e.reciprocal(rec[64:128, :nq], ps_o[64:128, :nq]), r=[pb_o], w=[rec_b])
            self.act(lambda e: e.copy(out=rec[0:64, :nq], in_=rec[64:128, :nq]), r=[rec_b], w=[rec_b])
            self.dve(lambda e: e.tensor_tensor(dst[po:po + 64, c0:c0 + nq], ps_o[:64, :nq], rec[0:64, :nq], ALU.mult), r=[pb_o, rec_b], w=[dst_b])

        for h in range(4):
            hi = 0
            q, q_b = qT[h % 2], qT_b[h % 2]
            o, o_b = oT[hi], oT_b[hi]
            if h == 2:
                for kc in range(KC):
                    self.dma("pool", wqh[:, kc, :], win[kc * 128:(kc + 1) * 128, colq + h * 64:colq + (h + 2) * 64], w=[wqh_b], nc_ok=True)
                proj_pair(qT[0], qT_b[0], qT[1], qT_b[1], wqh, wqh_b, 0.125)
            if h == 2:
                self.dma("pool", wo[:, 0, :], self.I("w_out")[l][wo_base + 128: wo_base + 256, :], w=[wo_b])
            if fox:
                self.dve(lambda e, q=q: e.memset(q[64:71, :], 1.0), w=[q_b])
                self.dma("sp", q[68:71, :], Fq[h], r=[Fk_b], w=[q_b])
            else:
                self.dve(lambda e, q=q: e.memset(q[64:65, :], 1.0), w=[q_b])
            self.dve(lambda e, q=q: e.tensor_scalar(q[64:65, :], q[64:65, :], self.flags_t[64:65, 1:2], None, ALU.mult), r=[q_b, cbuf], w=[q_b])
            vo, vo_b = vown[hi], vown_b[hi]
            self.dve(lambda e, vo=vo, h=h: e.tensor_copy(vo[:, 0:16, 0:64], vt[:, 0:16, h * 64:(h + 1) * 64]), r=[vb], w=[vo_b])
            self.dve(lambda e, vo=vo, h=h: e.tensor_copy(vo[:16, 16:18, 0:64], vt[:16, 16:18, h * 64:(h + 1) * 64]), r=[vb], w=[vo_b])
            kp, kp_b = kTp[hi], kTp_b[hi]
            self.dma("sp", kp[0:KA, :], xk_out[h * KA:(h + 1) * KA, :], r=[xk_out_b], w=[kp_b])
            if fox:
                self.dma("sp", kp[65:68, :], Fp[h], r=[Fp_b], w=[kp_b])
            vp, vp_b = vtp[hi], vtp_b[hi]
            self.dma("sp", vp[:, :, 0:64], xv_out[0:128, :].rearrange("p (b c) -> p b c", c=256)[:, :, h * 64:(h + 1) * 64], r=[xv_out_b], w=[vp_b], nc_ok=True)
            kc_, kc_b = kTc[0], kTc_b[0]
            vc, vc_b = vtc[0], vtc_b[0]

            def load_cache_k(jj):
                self.dma("pool", kst, ck[l, jj].rearrange("(b p) c -> p b c", p=128)[:, :, h * 64:(h + 1) * 64], w=[kst_b], nc_ok=True)

            def load_cache_v(jj):
                self.dma("pool", vc[:, :, 0:64], cv[l, jj].rearrange("(b p) c -> p b c", p=128)[:, :, h * 64:(h + 1) * 64], w=[vc_b], nc_ok=True)

            def build_kTc(jj):
                for g8 in range(4):
                    ps, pb = self.psum_bank()
                    pv = self.bank_bf(ps)
                    for b8 in range(8):
                        self.pe(lambda e, pv=pv, b8=b8, g8=g8: e.transpose(pv[:64, b8 * 128:(b8 + 1) * 128], kst[:, g8 * 8 + b8, :], self.ident_bf), r=[kst_b, cbuf], w=[pb])
                    self.act(lambda e, pv=pv, g8=g8: e.copy(out=kc_[0:64, g8 * 1024:(g8 + 1) * 1024], in_=pv[:64, 0:1024]), r=[pb], w=[kc_b])
                if fox:
                    self.dve(lambda e: e.memset(kc_[64:71, :], 1.0), w=[kc_b])
                    self.dve(lambda e: e.memset(kc_[64:65, :], 0.0), w=[kc_b])
                    self.dma("sp", kc_[65:68, :], self.gck[jj][0][h], r=[self.gck[jj][1]], w=[kc_b])
                else:
                    self.dve(lambda e: e.memset(kc_[64:65, :], 0.0), w=[kc_b])

            def sample_attend(jj):
                c0 = TP + 16 * jj
                ps_o, pb_o = self.acc_bank()
                if not fox:
                    self.dve(lambda e: e.memset(carry, 0.0), w=[carry_b])
                blist = [(kT[h][:, c0:c0 + 16], vo[:16, 16 + jj, :], [kT_b[h], vo_b], 16, 0, True)]
                for kb in range(31, -1, -1):
                    blist.append((kc_[:, kb * 128:(kb + 1) * 128], vc[:, kb, :], [kc_b, vc_b], 128, 0, False))
                run_tile(blist, q[:, c0:c0 + 16], q_b, 16, ps_o, pb_o)
                finish_tile(ps_o, pb_o, o, o_b, c0, 16, po=(h % 2) * 64)

            load_cache_k(0)
            load_cache_v(0)
            build_kTc(0)
            load_cache_k(1)
            for ti in range(4):
                c0 = ti * 512
                ps_o, pb_o = self.acc_bank()
                if not fox:
                    self.dve(lambda e: e.memset(carry, 0.0), w=[carry_b])
                blist = []
                for j in range(3, -1, -1):
                    kb = ti * 4 + j
                    blist.append((kT[h][:, kb * 128:(kb + 1) * 128], vo[:, kb, :], [kT_b[h], vo_b], 128, j * 128, True))
                for kb in range(ti * 4 - 1, -1, -1):
                    blist.append((kT[h][:, kb * 128:(kb + 1) * 128], vo[:, kb, :], [kT_b[h], vo_b], 128, 0, False))
                for kb in range(15, -1, -1):
                    blist.append((kp[:, kb * 128:(kb + 1) * 128], vp[:, kb, :], [kp_b, vp_b], 128, 0, False))
                run_tile(blist, q[:, c0:c0 + 512], q_b, 512, ps_o, pb_o)
                finish_tile(ps_o, pb_o, o, o_b, c0, 512, po=(h % 2) * 64)
            sample_attend(0)
            build_kTc(1)
            load_cache_v(1)
            sample_attend(1)
            for oc in range(8):
                if h % 2 == 0:
                    break
                for (c0, n) in self.ttiles():
                    ps, pb = self.psum_bank()
                    self.pe(lambda e, ps=ps, oc=oc, c0=c0, n=n, o=o, h=h: e.matmul(ps[:, :n], wo[:, 0, oc * 128:(oc + 1) * 128], o[:, c0:c0 + n], start=True, stop=True),
                            r=[wo_b, o_b], w=[pb])
                    hsl = self.hT[:, oc, c0:c0 + n]
                    self.dve(lambda e, ps=ps, hsl=hsl, n=n: e.tensor_tensor(hsl, hsl, ps[:, :n], ALU.add), r=[pb, self.hT_b], w=[self.hT_b])
        self.P.barrier()
        A.release(m0)

    def fox_gates(self, l):
        A = self.A
        uT, uT_b = self.uT, self.uT_b
        cbuf = self.B_const
        win = self.I("w_in")[l]
        m0 = A.mark()
        wg = A.alloc([128, KC, 4], BF16, "wg")
        wg_b = Buf("wg")
        self.dma("pool", wg, win[:, 1536:1540].rearrange("(c p) n -> p c n", p=128), w=[wg_b], nc_ok=True)
        bcol = A.alloc([4, 1], F32, "bcol")
        brow = A.alloc([128, 4], F32, "brow")
        b_b = Buf("foxb")
        self.dma("sp", bcol, self.I("fox_b_f")[l].rearrange("(h o) -> h o", o=1), w=[b_b], nc_ok=True)
        self.dma("sp", brow, bcast(self.I("fox_b_f")[l], 0, 128), w=[b_b])
        self.dve(lambda e: e.tensor_scalar(bcol, bcol, -1.0, None, ALU.mult), r=[b_b], w=[b_b])
        blocks = self.tblocks()
        ps_l, pb_l = self.psum_bank()
        for bi, (c0, n) in enumerate(blocks):
            for kc in range(KC):
                self.pe(lambda e, bi=bi, kc=kc, c0=c0, n=n: e.matmul(ps_l[:n, bi * 4:(bi + 1) * 4], uT[:, kc, c0:c0 + n], wg[:, kc, :],
                                                                     start=(kc == 0), stop=(kc == KC - 1)), r=[uT_b, wg_b], w=[pb_l])
        lf = A.alloc([128, 18, 4], F32, "lf")
        lf_b = Buf("lf")
        self.dve(lambda e: e.tensor_tensor(lf, ps_l[:, :72].rearrange("p (b h) -> p b h", h=4), bcast(brow, 1, 18), ALU.add), r=[pb_l, b_b], w=[lf_b])
        self.act(lambda e: e.activation(out=lf, in_=lf, func=AF.Exp, scale=-1.0), r=[lf_b], w=[lf_b])
        self.act(lambda e: e.activation(out=lf, in_=lf, func=AF.Ln, bias=1.0), r=[lf_b], w=[lf_b])
        self.dve(lambda e: e.tensor_scalar(lf, lf, -1.0, None, ALU.mult), r=[lf_b], w=[lf_b])
        for bi, (c0, n) in enumerate(blocks):
            dst = self.p_logf[l, c0:c0 + n, :] if c0 < TP else self.s_logf[l, c0 - TP:c0 - TP + n, :]
            self.dma("sp", dst, lf[:n, bi, :], r=[lf_b], nc_ok=True)
        S = A.alloc([4, T], F32, "S")
        G = A.alloc([4, T], F32, "G")
        S_b = Buf("S")
        G_b = Buf("G")
        for (c0, n) in self.ttiles():
            ps, pb = self.psum_bank()
            for kc in range(KC):
                self.pe(lambda e, ps=ps, kc=kc, c0=c0, n=n: e.matmul(ps[:4, :n], wg[:, kc, :], uT[:, kc, c0:c0 + n], start=(kc == 0), stop=(kc == KC - 1)),
                        r=[uT_b, wg_b], w=[pb])
            self.act(lambda e, ps=ps, c0=c0, n=n: e.activation(out=S[:, c0:c0 + n], in_=ps[:4, :n], func=AF.Exp, scale=-1.0, bias=bcol[:, 0:1]), r=[pb, b_b], w=[S_b])
        self.act(lambda e: e.activation(out=S, in_=S, func=AF.Ln, bias=1.0), r=[S_b], w=[S_b])
        ones4 = A.alloc([4, TP], F32, "ones4")
        self.dve(lambda e: e.memset(ones4, 1.0), w=[S_b])
        NH = 6
        his = [A.alloc([4, TP], BF16, "hi") for _ in range(NH)]
        his_b = [Buf("hi") for _ in range(NH)]
        r1s = [A.alloc([4, TP], F32, "r1") for _ in range(2)]
        r1s_b = [Buf("r1") for _ in range(2)]
        self.hi_i = 0

        def pieces(x, x_b, n, dk, dq, dk_b, dq_b):
            cur, cur_b = x, x_b
            for j in range(3):
                hi, hi_b = his[self.hi_i % NH], his_b[self.hi_i % NH]
                self.hi_i += 1
                self.dve(lambda e, cur=cur, hi=hi: e.tensor_copy(hi[:, :n], cur[:, :n]), r=[cur_b], w=[hi_b])
                if dk is not None:
                    self.dma("sp", dk[:, j, :], hi[:, :n], r=[hi_b], w=[dk_b])
                if j < 2:
                    r1, r1_b = r1s[j], r1s_b[j]
                    self.dve(lambda e, cur=cur, hi=hi, r1=r1: e.tensor_tensor(r1[:, :n], cur[:, :n], hi[:, :n], ALU.subtract), r=[cur_b, hi_b], w=[r1_b])
                if dq is not None:
                    hq, hq_b = his[self.hi_i % NH], his_b[self.hi_i % NH]
                    self.hi_i += 1
                    self.dve(lambda e, hi=hi, hq=hq: e.tensor_scalar(hq[:, :n], hi[:, :n], -1.0, None, ALU.mult), r=[hi_b], w=[hq_b])
                    self.dma("sp", dq[:, j, :], hq[:, :n], r=[hq_b], w=[dq_b])
                if j < 2:
                    cur, cur_b = r1, r1_b

        self.gck = []
        gtot = A.alloc([4, 2], F32, "gtot")
        gtot_b = Buf("gtot")
        lsts = [A.alloc([128, 8, 4], F32, "lst") for _ in range(2)]
        lsts_b = [Buf("lst") for _ in range(2)]
        Scs = [A.alloc([4, 1024], F32, "Sc") for _ in range(2)]
        Gcs = [A.alloc([4, 1024], F32, "Gc") for _ in range(2)]
        Scs_b = [Buf("Sc") for _ in range(2)]
        Gcs_b = [Buf("Gc") for _ in range(2)]
        cc_i = 0
        for jj in range(2):
            dk, dk_b = self.dram(f"gck{l}_{jj}", [4, 3, PAST], BF16)
            self.gck.append((dk, dk_b))
            for c in range(4):
                lst, lst_b = lsts[cc_i % 2], lsts_b[cc_i % 2]
                Sc, Sc_b = Scs[cc_i % 2], Scs_b[cc_i % 2]
                Gprev = Gcs[(cc_i + 1) % 2]
                Gprev_b = Gcs_b[(cc_i + 1) % 2]
                Gc, Gc_b = Gcs[cc_i % 2], Gcs_b[cc_i % 2]
                cc_i += 1
                self.dma("sp", lst, self.I("c_fox_logf")[l, jj, c * 1024:(c + 1) * 1024, :].rearrange("(b p) h -> p b h", p=128), w=[lst_b], nc_ok=True)
                for half in range(2):
                    ps, pb = self.psum_bank()
                    for b4 in range(4):
                        self.pe(lambda e, ps=ps, b4=b4, half=half, lst=lst: e.transpose(ps[:4, b4 * 128:(b4 + 1) * 128], lst[:, half * 4 + b4, :], self.ident_f),
                                r=[lst_b, cbuf], w=[pb])
                    self.act(lambda e, ps=ps, half=half, Sc=Sc: e.activation(out=Sc[:, half * 512:(half + 1) * 512], in_=ps[:4, :512], func=AF.Copy, scale=-1.0), r=[pb], w=[Sc_b])
                init = 0.0
                if c > 0:
                    self.dve(lambda e, jj=jj, Gprev=Gprev: e.tensor_copy(gtot[:, jj:jj + 1], Gprev[:, 1023:1024]), r=[Gprev_b], w=[gtot_b])
                    init = gtot[:, jj:jj + 1]
                self.dve(lambda e, init=init, Gc=Gc, Sc=Sc: e.tensor_tensor_scan(Gc, ones4[:, :1024], Sc, init, ALU.mult, ALU.add), r=[Sc_b, S_b, gtot_b], w=[Gc_b])
                pieces(Gc, Gc_b, 1024, dk[:, :, c * 1024:(c + 1) * 1024], None, dk_b, None)
            self.dve(lambda e, jj=jj, Gc=Gc: e.tensor_copy(gtot[:, jj:jj + 1], Gc[:, 1023:1024]), r=[Gc_b], w=[gtot_b])
        self.dve(lambda e: e.tensor_tensor_scan(G[:, 0:TP], ones4, S[:, 0:TP], 0.0, ALU.mult, ALU.add), r=[S_b], w=[G_b])
        for jj in range(2):
            c0 = TP + 16 * jj
            self.dve(lambda e, jj=jj, c0=c0: e.tensor_tensor_scan(G[:, c0:c0 + 16], ones4[:, :16], S[:, c0:c0 + 16], gtot[:, jj:jj + 1], ALU.mult, ALU.add),
                     r=[S_b, gtot_b], w=[G_b])
        x5in, x5in_b = self.dram(f"x5in{l}", [4, TP])
        x5out, x5out_b = self.dram(f"x5out{l}", [8, TP])
        self.dma("sp", x5in, G[:, 0:TP], r=[G_b], w=[x5in_b])
        self.allgather(x5in, x5in_b, x5out, x5out_b)
        Fk, Fk_b = self.dram(f"fk{l}", [4, 3, T], BF16)
        Fq, Fq_b = self.dram(f"fq{l}", [4, 3, T], BF16)
        pieces(G, G_b, TP, Fk[:, :, 0:TP], Fq[:, :, 0:TP], Fk_b, Fk_b)
        Gs = A.alloc([4, TS], F32, "Gs")
        Gs_b = Buf("Gs")
        self.dve(lambda e: e.tensor_copy(Gs, G[:, TP:T]), r=[G_b], w=[Gs_b])
        pieces(Gs, Gs_b, TS, Fk[:, :, TP:T], Fq[:, :, TP:T], Fk_b, Fk_b)
        Gp = A.alloc([4, TP], F32, "Gp")
        Gp_b = Buf("Gp")
        self.dma("sp", Gp, x5out[0:4, :], r=[x5out_b], w=[Gp_b])
        gpt = A.alloc([4, 1], F32, "gpt")
        self.dve(lambda e: e.tensor_copy(gpt, Gp[:, TP - 1:TP]), r=[Gp_b], w=[gtot_b])
        self.dve(lambda e: e.tensor_scalar(Gp, Gp, gpt[:, 0:1], None, ALU.subtract), r=[Gp_b, gtot_b], w=[Gp_b])
        Fp, Fp_b = self.dram(f"fp{l}", [4, 3, TP], BF16)
        pieces(Gp, Gp_b, TP, Fp, None, Fp_b, None)
        self.P.barrier()
        A.release(m0)
        return Fk, Fq, Fk_b, Fp, Fp_b

    def linear_fm(self, xT, x_b, wsrc, ncols, handler, col0=0):
        npan = (ncols + 511) // 512
        for pn in range(npan):
            w = min(512, ncols - pn * 512)
            wt, wb = self.load_w_panel(wsrc, col0 + pn * 512, w)
            for j in range(w // 128):
                oc = pn * 4 + j
                for (c0, n) in self.ttiles():
                    ps, pb = self.psum_bank()
                    for kc in range(KC):
                        self.pe(lambda e, ps=ps, kc=kc, c0=c0, n=n, wt=wt, j=j: e.matmul(
                            ps[:, :n], wt[:, kc, j * 128:(j + 1) * 128], xT[:, kc, c0:c0 + n],
                            start=(kc == 0), stop=(kc == KC - 1)), r=[x_b, wb], w=[pb])
                    handler(oc, ps, pb, c0, n)

    def add_to_h(self, oc, ps, pb, c0, n):
        hsl = self.hT[:, oc, c0:c0 + n]
        self.dve(lambda e: e.tensor_tensor(hsl, hsl, ps[:, :n], ALU.add), r=[pb, self.hT_b], w=[self.hT_b])

    def xattn_phase(self, l):
        A = self.A
        cbuf = self.B_const
        m0 = A.mark()
        xn = A.alloc([128, KC, T], BF16, "xn")
        xn_b = Buf("xn")
        for (c0, n) in self.ttiles():
            self.rmsnorm_fm(self.hT, self.hT_b, xn, xn_b, self.g_x[l], c0, n)
        qx = A.alloc([128, KC, T], BF16, "qx")
        qx_b = Buf("qx")
        self.alloc_wpan(2)
        self.alloc_stage()
        kmT = [A.alloc([128, KC, MEM], BF16, "kmT") for _ in range(3)]
        vm = [A.alloc([128, 2, DM], BF16, "vm") for _ in range(3)]
        km_b = [Buf("kmT") for _ in range(3)]
        vm_b = [Buf("vm") for _ in range(3)]
        m1 = A.mark()
        kst = A.alloc([128, 2, DM], BF16, "kmst")
        kst_b = Buf("kmst")
        memT = A.alloc([128, KC, MEM], F32, "memT")
        memT_b = Buf("memT")
        memn = A.alloc([128, KC, MEM], BF16, "memn")
        memn_b = Buf("memn")
        self.load_transposed(self.I("memp"), MEM, memT, memT_b, 0)
        self.rmsnorm_fm(memT, memT_b, memn, memn_b, self.g_mem[l], 0, MEM)
        for pn in range(2):
            wt, wb = self.load_w_panel(self.I("wk_x")[l], pn * 512, 512)
            for j in range(4):
                oc = pn * 4 + j
                ps, pb = self.psum_bank()
                for kc in range(KC):
                    self.pe(lambda e, ps=ps, kc=kc, wt=wt, j=j: e.matmul(ps[:, :MEM], wt[:, kc, j * 128:(j + 1) * 128], memn[:, kc, :],
                                                                         start=(kc == 0), stop=(kc == KC - 1)), r=[memn_b, wb], w=[pb])
                self.act(lambda e, ps=ps, oc=oc: e.copy(out=kmT[0][:, oc, :], in_=ps[:, :MEM]), r=[pb], w=[km_b[0]])
            for mb in range(2):
                ps, pb = self.psum_bank()
                for kc in range(KC):
                    self.pe(lambda e, ps=ps, kc=kc, wt=wt, mb=mb: e.matmul(ps[:, :], memn[:, kc, mb * 128:(mb + 1) * 128], wt[:, kc, :],
                                                                           start=(kc == 0), stop=(kc == KC - 1)), r=[memn_b, wb], w=[pb])
                st, sb = self.stage_f[self.stg_i % 2], self.stage_fb[self.stg_i % 2]
                self.stg_i += 1
                self.act(lambda e, st=st, ps=ps: e.copy(out=st[:, 0:512], in_=ps[:, :]), r=[pb], w=[sb])
                self.dma("sp", self.p_mem_k[l, mb * 128:(mb + 1) * 128, pn * 512:(pn + 1) * 512], st[:, 0:512], r=[sb])
        for pn in range(2):
            wt, wb = self.load_w_panel(self.I("wv_x")[l], pn * 512, 512)
            for mb in range(2):
                ps, pb = self.psum_bank()
                for kc in range(KC):
                    self.pe(lambda e, ps=ps, kc=kc, wt=wt, mb=mb: e.matmul(ps[:, :], memn[:, kc, mb * 128:(mb + 1) * 128], wt[:, kc, :],
                                                                           start=(kc == 0), stop=(kc == KC - 1)), r=[memn_b, wb], w=[pb])
                st, sb = self.stage_f[self.stg_i % 2], self.stage_fb[self.stg_i % 2]
                self.stg_i += 1
                self.act(lambda e, st=st, ps=ps: e.copy(out=st[:, 0:512], in_=ps[:, :]), r=[pb], w=[sb])
                self.dve(lambda e, ps=ps, mb=mb, pn=pn: e.tensor_copy(vm[0][:, mb, pn * 512:(pn + 1) * 512], ps[:, :]), r=[pb], w=[vm_b[0]])
                self.dma("sp", self.p_mem_v[l, mb * 128:(mb + 1) * 128, pn * 512:(pn + 1) * 512], st[:, 0:512], r=[sb])
        for jj in range(2):
            self.dma("pool", kst, self.I("c_mem_k")[l, jj].rearrange("(b p) c -> p b c", p=128), w=[kst_b])
            self.dma("pool", vm[1 + jj], self.I("c_mem_v")[l, jj].rearrange("(b p) c -> p b c", p=128), w=[vm_b[1 + jj]])
            for mb in range(2):
                ps, pb = self.psum_bank()
                pv = self.bank_bf(ps)
                for oc in range(8):
                    self.pe(lambda e, pv=pv, oc=oc, mb=mb: e.transpose(pv[:, oc * 128:(oc + 1) * 128], kst[:, mb, oc * 128:(oc + 1) * 128], self.ident_bf),
                            r=[kst_b, cbuf], w=[pb])
                self.act(lambda e, pv=pv, mb=mb, jj=jj: e.copy(out=kmT[1 + jj][:, :, mb * 128:(mb + 1) * 128], in_=pv[:, 0:1024].rearrange("p (c m) -> p c m", c=8)),
                         r=[pb], w=[km_b[1 + jj]])
        self.P.barrier()
        A.release(m1)
        def q_evac(oc, ps, pb, c0, n):
            self.act(lambda e: e.activation(out=qx[:, oc, c0:c0 + n], in_=ps[:, :n], func=AF.Copy, scale=1.0 / 16.0), r=[pb], w=[qx_b])
        self.linear_fm(xn, xn_b, self.I("wq_x")[l], DM, q_evac)
        pbf = [A.alloc([128, 2, 512], BF16, "pbf") for _ in range(2)]
        p_b = [Buf("pbf") for _ in range(2)]
        rec = A.alloc([128, 512], F32, "xrec")
        rec_b = Buf("xrec")
        groups = [(ti * 512, 512, 0) for ti in range(4)] + [(TP, 16, 1), (TP + 16, 16, 2)]
        gi = 0
        for (c0, nq, src) in groups:
            for h in range(4):
                pt, ptb = pbf[gi % 2], p_b[gi % 2]
                gi += 1
                for mb in range(2):
                    ps, pb = self.psum_bank()
                    for dc in range(2):
                        self.pe(lambda e, ps=ps, dc=dc, mb=mb, h=h, c0=c0, nq=nq, src=src: e.matmul(
                            ps[:, :nq], kmT[src][:, h * 2 + dc, mb * 128:(mb + 1) * 128], qx[:, h * 2 + dc, c0:c0 + nq],
                            start=(dc == 0), stop=(dc == 1)), r=[km_b[src], qx_b], w=[pb])
                    self.act(lambda e, ps=ps, pt=pt, mb=mb, nq=nq: e.activation(out=pt[:, mb, :nq], in_=ps[:, :nq], func=AF.Exp), r=[pb], w=[ptb])
                ps_d, pb_d = self.psum_bank()
                for mb in range(2):
                    self.pe(lambda e, ps_d=ps_d, pt=pt, mb=mb, nq=nq: e.matmul(ps_d[:, :nq], self.ones_bf, pt[:, mb, :nq], start=(mb == 0), stop=(mb == 1)),
                            r=[ptb, cbuf], w=[pb_d])
                self.dve(lambda e, ps_d=ps_d, nq=nq: e.reciprocal(rec[:, :nq], ps_d[:, :nq]), r=[pb_d], w=[rec_b])
                for dc in range(2):
                    ps_o, pb_o = self.psum_bank()
                    for mb in range(2):
                        self.pe(lambda e, ps_o=ps_o, pt=pt, mb=mb, dc=dc, h=h, nq=nq, src=src: e.matmul(
                            ps_o[:, :nq], vm[src][:, mb, h * 256 + dc * 128:h * 256 + (dc + 1) * 128], pt[:, mb, :nq],
                            start=(mb == 0), stop=(mb == 1)), r=[ptb, vm_b[src]], w=[pb_o])
                    self.dve(lambda e, ps_o=ps_o, dc=dc, h=h, c0=c0, nq=nq: e.tensor_tensor(xn[:, h * 2 + dc, c0:c0 + nq], ps_o[:, :nq], rec[:, :nq], ALU.mult),
                             r=[pb_o, rec_b], w=[xn_b])
        self.linear_fm(xn, xn_b, self.I("wo_x")[l], DM, self.add_to_h)
        self.P.barrier()
        A.release(m0)

    def ffn_phase(self, l):
        A = self.A
        m0 = A.mark()
        xn = A.alloc([128, KC, T], BF16, "fx")
        xn_b = Buf("fx")
        for (c0, n) in self.ttiles():
            self.rmsnorm_fm(self.hT, self.hT_b, xn, xn_b, self.g_ffn[l], c0, n)
        NFH = 12
        hid = A.alloc([128, NFH, T], BF16, "hid")
        hid_b = Buf("hid")
        self.alloc_wpan(4)
        sg = [A.alloc([128, 512], F32, "sg") for _ in range(2)]
        sg_b = [Buf("sg") for _ in range(2)]
        wd = [A.alloc([128, NFH, 128], BF16, "wd") for _ in range(2)]
        wd_b = [Buf("wd") for _ in range(2)]
        si = 0
        wi = 0
        for (f0, nf) in ((0, 12), (12, 10)):
            for pn in range((nf + 3) // 4):
                fw = min(4, nf - pn * 4)
                wg_t, wg_b = self.load_w_panel(self.I("w_gate")[l], (f0 + pn * 4) * 128, fw * 128)
                wu_t, wu_b = self.load_w_panel(self.I("w_up")[l], (f0 + pn * 4) * 128, fw * 128)
                for j in range(fw):
                    fi = pn * 4 + j
                    for (c0, n) in self.ttiles():
                        ps_g, pb_g = self.psum_bank()
                        ps_u, pb_u = self.psum_bank()
                        for kc in range(KC):
                            self.pe(lambda e, ps_g=ps_g, kc=kc, c0=c0, n=n, wg_t=wg_t, j=j: e.matmul(
                                ps_g[:, :n], wg_t[:, kc, j * 128:(j + 1) * 128], xn[:, kc, c0:c0 + n], start=(kc == 0), stop=(kc == KC - 1)),
                                r=[xn_b, wg_b], w=[pb_g])
                        for kc in range(KC):
                            self.pe(lambda e, ps_u=ps_u, kc=kc, c0=c0, n=n, wu_t=wu_t, j=j: e.matmul(
                                ps_u[:, :n], wu_t[:, kc, j * 128:(j + 1) * 128], xn[:, kc, c0:c0 + n], start=(kc == 0), stop=(kc == KC - 1)),
                                r=[xn_b, wu_b], w=[pb_u])
                        s_, s_b = sg[si % 2], sg_b[si % 2]
                        si += 1
                        self.act(lambda e, s_=s_, ps_g=ps_g, n=n: e.activation(out=s_[:, :n], in_=ps_g[:, :n], func=AF.Silu), r=[pb_g], w=[s_b])
                        self.dve(lambda e, s_=s_, ps_u=ps_u, fi=fi, c0=c0, n=n: e.tensor_tensor(hid[:, fi, c0:c0 + n], s_[:, :n], ps_u[:, :n], ALU.mult),
                                 r=[s_b, pb_u], w=[hid_b])
            for oc in range(8):
                w_, w_b = wd[wi % 2], wd_b[wi % 2]
                wi += 1
                self.dma("pool", w_[:, :nf, :], self.I("w_down")[l][f0 * 128:(f0 + nf) * 128, oc * 128:(oc + 1) * 128].rearrange("(f p) c -> p f c", p=128),
                         w=[w_b], nc_ok=True)
                for (c0, n) in self.ttiles():
                    ps, pb = self.psum_bank()
                    for fi in range(nf):
                        self.pe(lambda e, ps=ps, fi=fi, c0=c0, n=n, w_=w_, nf=nf: e.matmul(ps[:, :n], w_[:, fi, :], hid[:, fi, c0:c0 + n],
                                                                                      start=(fi == 0), stop=(fi == nf - 1)), r=[hid_b, w_b], w=[pb])
                    self.add_to_h(oc, ps, pb, c0, n)
        self.P.barrier()
        A.release(m0)

    def final_phase(self):
        A = self.A
        cbuf = self.B_const
        m0 = A.mark()
        self.alloc_stage()
        yT = [A.alloc([128, KC, 128], F32, "yT") for _ in range(2)]
        yT_b = [Buf("yT") for _ in range(2)]
        for bi, (c0, n) in enumerate(self.tblocks()):
            y, y_b = yT[bi % 2], yT_b[bi % 2]
            self.rmsnorm_fm(self.hT, self.hT_b, y, y_b, self.g_fin, c0, n, dcol=0)
            st, sb = self.stage_f[self.stg_i % 2], self.stage_fb[self.stg_i % 2]
            self.stg_i += 1
            for half in range(2):
                ps, pb = self.psum_bank()
                for j in range(4):
                    kc = half * 4 + j
                    self.pe(lambda e, ps=ps, j=j, kc=kc, n=n, y=y: e.transpose(ps[:n, j * 128:(j + 1) * 128], y[:, kc, :n], self.ident_f), r=[y_b, cbuf], w=[pb])
                if half == 0:
                    self.act(lambda e, ps=ps, st=st, n=n: e.copy(out=st[:n, 0:512], in_=ps[:n, :]), r=[pb], w=[sb])
                else:
                    self.dve(lambda e, ps=ps, st=st, n=n: e.tensor_copy(st[:n, 512:1024], ps[:n, :]), r=[pb], w=[sb])
            dst = self.yp[c0:c0 + n, :] if c0 < TP else self.ys[c0 - TP:c0 - TP + n, :]
            self.dma("sp", dst, st[:n, :], r=[sb])
        self.P.barrier()
        A.release(m0)


def prep_inputs(inp):
    f = lambda a: np.ascontiguousarray(a, dtype=np.float32)
    maps = []
    for c in range(NCORES):
        b, r = c // 2, c % 2
        sb = slice(2 * c, 2 * c + 2)
        m = {
            "xp": f(inp["x_prompt"][b, r * TP:(r + 1) * TP]),
            "xs": f(inp["x_sample"][sb].reshape(TS, DM)),
            "c_sb_k": f(inp["cache_sb_k"][:, sb].reshape(2, 2, PAST, 256)),
            "c_sb_v": f(inp["cache_sb_v"][:, sb].reshape(2, 2, PAST, 256)),
            "c_fox_k": f(inp["cache_fox_k"][:, sb].reshape(2, 2, PAST, 256)),
            "c_fox_v": f(inp["cache_fox_v"][:, sb].reshape(2, 2, PAST, 256)),
            "c_fox_logf": f(inp["cache_fox_logf"][:, sb]),
            "st_ssm": f(inp["state_ssm"][:, sb].reshape(2, 2, 512, 128)),
            "st_conv": f(inp["state_conv"][:, sb]),
            "c_mem_k": f(inp["cache_mem_k"][:, sb].reshape(2, 2, MEM, 1024)),
            "c_mem_v": f(inp["cache_mem_v"][:, sb].reshape(2, 2, MEM, 1024)),
            "memp": f(inp["mem_prompt"][b]),
        }
        fl = np.zeros((128, 8), np.float32)
        fl[:, 0] = float(r)
        fl[:, 1] = 0.0 if r == 1 else NEG
        m["flags"] = fl
        for k in ("norm_mix_g", "w_in", "fox_b_f", "conv_w", "conv_b", "dt_bias", "a_log", "d_skip", "ssd_norm_g",
                  "w_out", "norm_x_g", "mem_norm_g", "wq_x", "wk_x", "wv_x", "wo_x", "norm_ffn_g", "w_gate",
                  "w_up", "w_down"):
            m[k] = f(inp[k])
        m["final_norm_g"] = f(inp["final_norm_g"]).reshape(1, DM)
        maps.append(m)
    return maps


def assemble(res):
    R = res
    g = lambda c, k: np.asarray(R[c][k], dtype=np.float32)
    y_prompt = np.stack([np.concatenate([g(2 * b, "yp"), g(2 * b + 1, "yp")], 0) for b in range(4)], 0)
    y_sample = np.concatenate([g(c, "ys").reshape(2, 16, DM) for c in range(NCORES)], 0)
    outs = [y_prompt, y_sample]
    for nm in ("sb_k", "sb_v", "fox_k", "fox_v"):
        a = np.stack([np.concatenate([g(2 * b, "p_" + nm), g(2 * b + 1, "p_" + nm)], 1) for b in range(4)], 1)
        outs.append(a.reshape(2, 4, 4096, 4, 64))
    a = np.stack([np.concatenate([g(2 * b, "p_fox_logf"), g(2 * b + 1, "p_fox_logf")], 1) for b in range(4)], 1)
    outs.append(a.reshape(2, 4, 4096, 4))
    outs.append(np.stack([g(2 * b + 1, "p_ssm") for b in range(4)], 1).reshape(2, 4, 8, 64, 128))
    outs.append(np.stack([g(2 * b + 1, "p_conv") for b in range(4)], 1).reshape(2, 4, 3, 1024))
    outs.append(np.stack([g(2 * b, "p_mem_k") for b in range(4)], 1).reshape(2, 4, MEM, 4, 256))
    outs.append(np.stack([g(2 * b, "p_mem_v") for b in range(4)], 1).reshape(2, 4, MEM, 4, 256))
    for nm in ("sb_k", "sb_v", "fox_k", "fox_v"):
        a = np.concatenate([g(c, "s_" + nm).reshape(2, 2, 16, 256) for c in range(NCORES)], 1)
        outs.append(a.reshape(2, 16, 16, 4, 64))
    a = np.concatenate([g(c, "s_fox_logf").reshape(2, 2, 16, 4) for c in range(NCORES)], 1)
    outs.append(a)
    outs.append(np.concatenate([g(c, "s_ssm") for c in range(NCORES)], 1).reshape(2, 16, 8, 64, 128))
    outs.append(np.concatenate([g(c, "s_conv") for c in range(NCORES)], 1).reshape(2, 16, 3, 1024))
    return tuple(outs)


_CACHE = {}


def kernel(**inputs):
    stage = inputs.pop("_stage", 99)
    if stage not in _CACHE:
        _CACHE[stage] = K(stage)
    k = _CACHE[stage]
    maps = [{n: m[n] for n in k.ins} for m in prep_inputs(inputs)]
    res = run_bass_kernel_spmd(k.nc, maps, core_ids=list(range(NCORES)))
    return assemble(res.results)
```
